# Optimizing a Trainium2 kernel written in Bass

```python
import jax
import jax.numpy as jnp
from jax import lax
import numpy as np

D_MODEL = 2048
BATCH = 32
SEQ = 256
DEPTH = 4
DEC_BATCH = 2
DEC_SEQ = 4096
PAST_LEN = 512

GRID_W = 64
N_MIXERS = 3
N_WIN = (DEPTH + 2) // 3
N_MLA = (DEPTH + 1) // 3
N_GLA = DEPTH // 3
Q_BLOCK = 128
EPS = 1e-6
ROPE_BASE = 10000.0
NEG = -1e30
WIN_HEADS = 16
WIN_KV_HEADS = 4
WIN_GROUP = WIN_HEADS // WIN_KV_HEADS
WIN_HEAD_DIM = D_MODEL // WIN_HEADS
WINDOW = 128
MLA_HEADS = 16
MLA_Q_RANK = 512
MLA_KV_RANK = 256
MLA_NOPE = 128
MLA_ROPE = 64
MLA_V = 128
MLA_SCALE = (MLA_NOPE + MLA_ROPE) ** -0.5
GLA_HEADS = 4
GLA_DK = D_MODEL // 2 // GLA_HEADS
GLA_DV = D_MODEL // GLA_HEADS
GLA_GATE_RANK = 16
GLA_TAU = 16.0
GLA_CHUNK = 64
FFN_HIDDEN = 4 * D_MODEL

kernel_name = 'hybrid_diffusion_win_mla_gla_step'


def rmsnorm(x, g):
    xf = x.astype(jnp.float32)
    y = xf * lax.rsqrt(jnp.mean(xf * xf, axis=-1, keepdims=True) + EPS)
    return (y * g.astype(jnp.float32)).astype(x.dtype)


def modulation(cond, w, b):
    m = jax.nn.silu(cond) @ w + b
    return jnp.split(m[..., None, :], 6, axis=-1)


def modulate(x, g, shift, scale):
    return rmsnorm(x, g) * (1 + scale) + shift


def grid_positions(n_tokens):
    rows = n_tokens // GRID_W
    row = jnp.repeat(jnp.arange(rows, dtype=jnp.int32), GRID_W)
    col = jnp.tile(jnp.arange(GRID_W, dtype=jnp.int32), rows)
    return row, col


def rope_1d(x, pos):
    half = x.shape[-1] // 2
    inv_freq = ROPE_BASE ** (-jnp.arange(half, dtype=jnp.float32) / half)
    ang = pos.astype(jnp.float32)[:, None] * inv_freq[None, :]
    cos = jnp.cos(ang)[None, :, None, :]
    sin = jnp.sin(ang)[None, :, None, :]
    xf = x.astype(jnp.float32)
    x1, x2 = xf[..., :half], xf[..., half:]
    return jnp.concatenate([x1 * cos - x2 * sin, x2 * cos + x1 * sin], axis=-1).astype(x.dtype)


def axial_rope(x):
    row, col = grid_positions(x.shape[1])
    r = x.shape[-1] // 2
    return jnp.concatenate([rope_1d(x[..., :r], row), rope_1d(x[..., r:], col)], axis=-1)


def attn_core(q, k, v, valid, sink):
    s = jnp.einsum('bqhgd,bkhd->bhgqk', q, k, preferred_element_type=jnp.float32)
    if valid is not None:
        s = jnp.where(valid, s, NEG)
    if sink is not None:
        sk = jnp.broadcast_to(sink.astype(jnp.float32)[None, :, :, None, None], s.shape[:-1] + (1,))
        p = jax.nn.softmax(jnp.concatenate([s, sk], axis=-1), axis=-1)[..., :-1]
    else:
        p = jax.nn.softmax(s, axis=-1)
    return jnp.einsum('bhgqk,bkhe->bqhge', p.astype(v.dtype), v)


def dense_attention(q, k, v, sink):
    B, Lq = q.shape[:2]
    nb = Lq // Q_BLOCK
    qb = jnp.moveaxis(q.reshape((B, nb, Q_BLOCK) + q.shape[2:]), 1, 0)
    ob = lax.map(lambda qq: attn_core(qq, k, v, None, sink), qb)
    return jnp.moveaxis(ob, 0, 1).reshape((B, Lq) + ob.shape[3:])


def window_attention(q, k, v, k_ctx, v_ctx, sink):
    B, L = q.shape[:2]
    nb = L // Q_BLOCK
    Lc = k_ctx.shape[1]
    pad = ((0, 0), (Q_BLOCK, Q_BLOCK), (0, 0), (0, 0))
    kp, vp = jnp.pad(k, pad), jnp.pad(v, pad)
    qi = jnp.arange(Q_BLOCK)[:, None]
    kj = jnp.arange(3 * Q_BLOCK)[None, :] - Q_BLOCK
    rel_ok = jnp.abs(kj - qi) <= WINDOW
    ctx_ok = jnp.ones((Q_BLOCK, Lc), dtype=bool)

    def one_block(n):
        start = n * Q_BLOCK
        qq = lax.dynamic_slice_in_dim(q, start, Q_BLOCK, axis=1)
        kk = lax.dynamic_slice_in_dim(kp, start, 3 * Q_BLOCK, axis=1)
        vv = lax.dynamic_slice_in_dim(vp, start, 3 * Q_BLOCK, axis=1)
        key_pos = start + kj
        valid = rel_ok & (key_pos >= 0) & (key_pos < L)
        valid = jnp.concatenate([ctx_ok, valid], axis=1)
        return attn_core(qq, jnp.concatenate([k_ctx, kk], axis=1),
                         jnp.concatenate([v_ctx, vv], axis=1), valid, sink)

    ob = lax.map(one_block, jnp.arange(nb))
    return jnp.moveaxis(ob, 0, 1).reshape((B, L) + ob.shape[3:])


def win_qkv(h, w_qkv):
    B, L, _ = h.shape
    qkv = h @ w_qkv
    nq = WIN_HEADS * WIN_HEAD_DIM
    nkv = WIN_KV_HEADS * WIN_HEAD_DIM
    q = qkv[..., :nq].reshape(B, L, WIN_HEADS, WIN_HEAD_DIM)
    k = qkv[..., nq:nq + nkv].reshape(B, L, WIN_KV_HEADS, WIN_HEAD_DIM)
    v = qkv[..., nq + nkv:].reshape(B, L, WIN_KV_HEADS, WIN_HEAD_DIM)
    return q, k, v


def win_group(q):
    B, L = q.shape[:2]
    return (q * WIN_HEAD_DIM ** -0.5).reshape(B, L, WIN_KV_HEADS, WIN_GROUP, WIN_HEAD_DIM)


def win_context(h, w_qkv, sink, w_o):
    B, L, _ = h.shape
    q, k, v = win_qkv(h, w_qkv)
    o = dense_attention(win_group(q), k, v, sink.reshape(WIN_KV_HEADS, WIN_GROUP))
    return o.reshape(B, L, -1) @ w_o, k, v


def win_latent(h, k_ctx, v_ctx, w_qkv, sink, w_o):
    B, L, _ = h.shape
    q, k, v = win_qkv(h, w_qkv)
    q, k = axial_rope(q), axial_rope(k)
    o = window_attention(win_group(q), k, v, k_ctx, v_ctx, sink.reshape(WIN_KV_HEADS, WIN_GROUP))
    return o.reshape(B, L, -1) @ w_o


def mla_down(h, w_down, q_norm, kv_norm, w_uq):
    B, L, _ = h.shape
    d = h @ w_down
    cq = rmsnorm(d[..., :MLA_Q_RANK], q_norm)
    ckv = rmsnorm(d[..., MLA_Q_RANK:MLA_Q_RANK + MLA_KV_RANK], kv_norm)
    k_rope = d[..., MLA_Q_RANK + MLA_KV_RANK:]
    q = (cq @ w_uq).reshape(B, L, MLA_HEADS, MLA_NOPE + MLA_ROPE)
    return q, ckv, k_rope


def mla_expand(ckv, k_rope, w_ukv):
    B, L, _ = ckv.shape
    kv = (ckv @ w_ukv).reshape(B, L, MLA_HEADS, MLA_NOPE + MLA_V)
    k = jnp.concatenate([kv[..., :MLA_NOPE],
                         jnp.broadcast_to(k_rope[:, :, None, :], (B, L, MLA_HEADS, MLA_ROPE))], axis=-1)
    return k, kv[..., MLA_NOPE:]


def mla_attend(q, k, v, w_o):
    B, L = q.shape[:2]
    o = dense_attention((q * MLA_SCALE)[:, :, :, None, :], k, v, None)
    return o.reshape(B, L, -1) @ w_o


def mla_context(h, w_down, q_norm, w_uq, kv_norm, w_ukv, w_o):
    q, ckv, k_rope = mla_down(h, w_down, q_norm, kv_norm, w_uq)
    k, v = mla_expand(ckv, k_rope, w_ukv)
    return mla_attend(q, k, v, w_o), ckv, k_rope


def mla_latent(h, ckv_ctx, krope_ctx, w_down, q_norm, w_uq, kv_norm, w_ukv, w_o):
    q, ckv, k_rope = mla_down(h, w_down, q_norm, kv_norm, w_uq)
    q = jnp.concatenate([q[..., :MLA_NOPE], axial_rope(q[..., MLA_NOPE:])], axis=-1)
    k_rope = axial_rope(k_rope[:, :, None, :])[:, :, 0, :]
    k_c, v_c = mla_expand(ckv_ctx, krope_ctx, w_ukv)
    k_l, v_l = mla_expand(ckv, k_rope, w_ukv)
    return mla_attend(q, jnp.concatenate([k_c, k_l], axis=1), jnp.concatenate([v_c, v_l], axis=1), w_o)


def gla_log_gate(h, wa1, wa2, ba):
    z = (h @ wa1) @ wa2 + ba
    return jax.nn.log_sigmoid(z.astype(jnp.float32)) / GLA_TAU


def gla_chunk_scan(q, k, v, g, s0):
    B, H, L, _ = q.shape
    n = L // GLA_CHUNK

    def to_chunks(t):
        return jnp.moveaxis(t.reshape(B, H, n, GLA_CHUNK, t.shape[-1]), 2, 0)

    tri = jnp.tril(jnp.ones((GLA_CHUNK, GLA_CHUNK), dtype=bool))

    def step(s, inp):
        qc, kc, vc, gc = inp
        qc, kc, vc = qc.astype(jnp.float32), kc.astype(jnp.float32), vc.astype(jnp.float32)
        b = lax.cumsum(gc.astype(jnp.float32), axis=2)
        b_last = b[:, :, -1:, :]
        q_t = qc * jnp.exp(b)
        k_t = kc * jnp.exp(-b)
        a = jnp.where(tri, jnp.einsum('bhid,bhjd->bhij', q_t, k_t), 0.0)
        o = jnp.einsum('bhij,bhje->bhie', a, vc) + jnp.einsum('bhid,bhde->bhie', q_t, s)
        k_dec = kc * jnp.exp(b_last - b)
        s_new = jnp.exp(b_last)[:, :, 0, :, None] * s + jnp.einsum('bhjd,bhje->bhde', k_dec, vc)
        return s_new, o

    s_fin, o = lax.scan(step, s0.astype(jnp.float32), (to_chunks(q), to_chunks(k), to_chunks(v), to_chunks(g)))
    o = jnp.moveaxis(o, 0, 2).reshape(B, H, L, v.shape[-1])
    return o.astype(v.dtype), s_fin.astype(v.dtype)


def gla_mixer(h, s_f0, s_b0, w_in, wa1, wa2, ba, norm_g, w_o):
    B, L, _ = h.shape
    p = h @ w_in
    nk = GLA_HEADS * GLA_DK
    nv = GLA_HEADS * GLA_DV

    def heads(t, d):
        return t.reshape(B, L, GLA_HEADS, d).transpose(0, 2, 1, 3)

    q = heads(p[..., :nk], GLA_DK) * GLA_DK ** -0.5
    k = heads(p[..., nk:2 * nk], GLA_DK)
    v = heads(p[..., 2 * nk:2 * nk + nv], GLA_DV)
    r = p[..., 2 * nk + nv:]
    g_f = heads(gla_log_gate(h, wa1[0], wa2[0], ba[0]), GLA_DK)
    g_b = heads(gla_log_gate(h, wa1[1], wa2[1], ba[1]), GLA_DK)
    o_f, s_f = gla_chunk_scan(q, k, v, g_f, s_f0)
    flip = lambda t: jnp.flip(t, axis=2)
    o_b, s_b = gla_chunk_scan(flip(q), flip(k), flip(v), flip(g_b), s_b0)
    o = (o_f + flip(o_b)).transpose(0, 2, 1, 3)
    o = rmsnorm(o, norm_g).reshape(B, L, nv) * jax.nn.silu(r)
    return o @ w_o, s_f, s_b


def sqrelu_ffn(h, w1, w2):
    return jnp.square(jax.nn.relu(h @ w1)) @ w2


def setup_inputs(seed: int = 0) -> dict:
    key = jax.random.key(seed)
    ks = jax.random.split(key, 32)

    def nrm(k, shape, s):
        return jax.random.normal(k, shape, jnp.float32) * s

    def gain(k, shape):
        return 1.0 + nrm(k, shape, 0.02)

    D = D_MODEL
    return {
        'x_prompt': nrm(ks[0], (BATCH, SEQ, D), 1.0),
        'x_sample': nrm(ks[1], (DEC_BATCH, DEC_SEQ, D), 1.0),
        'c': nrm(ks[2], (DEC_BATCH, D), 1.0),
        'cache_win_k': nrm(ks[3], (DEC_BATCH, N_WIN, PAST_LEN, WIN_KV_HEADS, WIN_HEAD_DIM), 1.0),
        'cache_win_v': nrm(ks[4], (DEC_BATCH, N_WIN, PAST_LEN, WIN_KV_HEADS, WIN_HEAD_DIM), 1.0),
        'cache_mla_ckv': nrm(ks[5], (DEC_BATCH, N_MLA, PAST_LEN, MLA_KV_RANK), 1.0),
        'cache_mla_krope': nrm(ks[6], (DEC_BATCH, N_MLA, PAST_LEN, MLA_ROPE), 1.0),
        'state_gla_fwd': nrm(ks[7], (DEC_BATCH, N_GLA, GLA_HEADS, GLA_DK, GLA_DV), 1.0),
        'state_gla_bwd': nrm(ks[8], (DEC_BATCH, N_GLA, GLA_HEADS, GLA_DK, GLA_DV), 1.0),
        'c_ctx': nrm(ks[9], (D,), 1.0),
        'ada_w': nrm(ks[10], (DEPTH, D, 6 * D), 0.5 * D ** -0.5),
        'ada_b': nrm(ks[11], (DEPTH, 6 * D), 0.02),
        'norm_g': gain(ks[12], (DEPTH, 2, D)),
        'win_wqkv': nrm(ks[13], (N_WIN, D, (WIN_HEADS + 2 * WIN_KV_HEADS) * WIN_HEAD_DIM), D ** -0.5),
        'win_sink': nrm(ks[14], (N_WIN, WIN_HEADS), 0.5),
        'win_wo': nrm(ks[15], (N_WIN, WIN_HEADS * WIN_HEAD_DIM, D), (WIN_HEADS * WIN_HEAD_DIM) ** -0.5),
        'mla_wdown': nrm(ks[16], (N_MLA, D, MLA_Q_RANK + MLA_KV_RANK + MLA_ROPE), D ** -0.5),
        'mla_q_norm': gain(ks[17], (N_MLA, MLA_Q_RANK)),
        'mla_wuq': nrm(ks[18], (N_MLA, MLA_Q_RANK, MLA_HEADS * (MLA_NOPE + MLA_ROPE)), MLA_Q_RANK ** -0.5),
        'mla_kv_norm': gain(ks[19], (N_MLA, MLA_KV_RANK)),
        'mla_wukv': nrm(ks[20], (N_MLA, MLA_KV_RANK, MLA_HEADS * (MLA_NOPE + MLA_V)), MLA_KV_RANK ** -0.5),
        'mla_wo': nrm(ks[21], (N_MLA, MLA_HEADS * MLA_V, D), (MLA_HEADS * MLA_V) ** -0.5),
        'gla_win': nrm(ks[22], (N_GLA, D, 2 * GLA_HEADS * GLA_DK + 2 * GLA_HEADS * GLA_DV), D ** -0.5),
        'gla_wa1': nrm(ks[23], (N_GLA, 2, D, GLA_GATE_RANK), D ** -0.5),
        'gla_wa2': nrm(ks[24], (N_GLA, 2, GLA_GATE_RANK, GLA_HEADS * GLA_DK), GLA_GATE_RANK ** -0.5),
        'gla_ba': nrm(ks[25], (N_GLA, 2, GLA_HEADS * GLA_DK), 0.1),
        'gla_norm': gain(ks[26], (N_GLA, GLA_DV)),
        'gla_wo': nrm(ks[27], (N_GLA, GLA_HEADS * GLA_DV, D), (GLA_HEADS * GLA_DV) ** -0.5),
        'ffn_w1': nrm(ks[28], (DEPTH, D, FFN_HIDDEN), D ** -0.5),
        'ffn_w2': nrm(ks[29], (DEPTH, FFN_HIDDEN, D), FFN_HIDDEN ** -0.5),
        'final_norm': gain(ks[30], (D,)),
    }


def reference(x_prompt, x_sample, c, cache_win_k, cache_win_v, cache_mla_ckv, cache_mla_krope,
              state_gla_fwd, state_gla_bwd, c_ctx, ada_w, ada_b, norm_g, win_wqkv, win_sink, win_wo,
              mla_wdown, mla_q_norm, mla_wuq, mla_kv_norm, mla_wukv, mla_wo, gla_win, gla_wa1,
              gla_wa2, gla_ba, gla_norm, gla_wo, ffn_w1, ffn_w2, final_norm):
    xp, xs = x_prompt, x_sample
    wk, wv, mc, mr, gf, gb = [], [], [], [], [], []
    for i in range(DEPTH):
        kind, j = i % N_MIXERS, i // N_MIXERS
        p_sh1, p_sc1, p_g1, p_sh2, p_sc2, p_g2 = modulation(c_ctx, ada_w[i], ada_b[i])
        s_sh1, s_sc1, s_g1, s_sh2, s_sc2, s_g2 = modulation(c, ada_w[i], ada_b[i])
        hp = modulate(xp, norm_g[i, 0], p_sh1, p_sc1)
        hs = modulate(xs, norm_g[i, 0], s_sh1, s_sc1)
        if kind == 0:
            yp, kc, vc = win_context(hp, win_wqkv[j], win_sink[j], win_wo[j])
            ys = win_latent(hs, cache_win_k[:, j], cache_win_v[:, j], win_wqkv[j], win_sink[j], win_wo[j])
            wk.append(kc)
            wv.append(vc)
        elif kind == 1:
            yp, ckv, kr = mla_context(hp, mla_wdown[j], mla_q_norm[j], mla_wuq[j], mla_kv_norm[j],
                                      mla_wukv[j], mla_wo[j])
            ys = mla_latent(hs, cache_mla_ckv[:, j], cache_mla_krope[:, j], mla_wdown[j], mla_q_norm[j],
                            mla_wuq[j], mla_kv_norm[j], mla_wukv[j], mla_wo[j])
            mc.append(ckv)
            mr.append(kr)
        else:
            zeros = jnp.zeros((xp.shape[0], GLA_HEADS, GLA_DK, GLA_DV), jnp.float32)
            yp, sf, sb = gla_mixer(hp, zeros, zeros, gla_win[j], gla_wa1[j], gla_wa2[j], gla_ba[j],
                                   gla_norm[j], gla_wo[j])
            ys, _, _ = gla_mixer(hs, state_gla_fwd[:, j], state_gla_bwd[:, j], gla_win[j], gla_wa1[j],
                                 gla_wa2[j], gla_ba[j], gla_norm[j], gla_wo[j])
            gf.append(sf)
            gb.append(sb)
        xp = xp + p_g1 * yp
        xs = xs + s_g1 * ys
        xp = xp + p_g2 * sqrelu_ffn(modulate(xp, norm_g[i, 1], p_sh2, p_sc2), ffn_w1[i], ffn_w2[i])
        xs = xs + s_g2 * sqrelu_ffn(modulate(xs, norm_g[i, 1], s_sh2, s_sc2), ffn_w1[i], ffn_w2[i])
    y_prompt = rmsnorm(xp, final_norm)
    y_sample = rmsnorm(xs, final_norm)
    new_win_k = jnp.stack(wk, axis=1)
    new_win_v = jnp.stack(wv, axis=1)
    new_mla_ckv = jnp.stack(mc, axis=1)
    new_mla_krope = jnp.stack(mr, axis=1)
    new_gla_fwd = jnp.stack(gf, axis=1)
    new_gla_bwd = jnp.stack(gb, axis=1)
    return (y_prompt, y_sample, new_win_k, new_win_v, new_mla_ckv, new_mla_krope, new_gla_fwd, new_gla_bwd)
```

```python
import numpy as np
import concourse.bass as bass
import concourse.mybir as mybir
from concourse.bass_utils import run_bass_kernel_spmd

F32 = mybir.dt.float32
BF16 = mybir.dt.bfloat16
AF = mybir.ActivationFunctionType
ALU = mybir.AluOpType

P = 128
D = 2048
KC = 16
TP = 1024
TS = 4096
T = TP + TS
NBLK = T // 1024
EPS = 1e-6
FFN_H = 8192
PAST = 512


class _Op:
    __slots__ = ("eng", "fn", "deps", "idx", "signal", "semval", "dma", "dslot", "dval")


class Prog:
    ENGS = ("pe", "act", "dve", "pool", "sp")
    NDS = 8

    def __init__(self, nc):
        self.nc = nc
        self.ops = {e: [] for e in self.ENGS}
        self.res = {}
        self.ndma = {e: 0 for e in self.ENGS}
        self.dma_last = {}

    maxops = None
    count = 0
    log = None
    pre_hook = None
    hook_off = False

    def op(self, eng, fn, reads=(), writes=(), dma=False):
        if self.pre_hook is not None and not self.hook_off:
            self.pre_hook()
        self.count += 1
        if self.log is not None and self.count % 50 == 0:
            import traceback
            fr = traceback.extract_stack(limit=4)
            self.log.append((self.count, eng, [(f.name, f.lineno) for f in fr[:-1]]))
        if self.maxops is not None and self.count > self.maxops:
            d = _Op()
            d.eng = eng
            d.dma = False
            d.signal = False
            d.deps = []
            return d
        o = _Op()
        o.eng = eng
        o.fn = fn
        o.dma = dma
        o.signal = False
        o.semval = 0
        lst = self.ops[eng]
        o.idx = len(lst)
        deps = []
        for k in reads:
            st = self.res.get(k)
            if st is not None and st[0] is not None:
                deps.append(st[0])
        for k in writes:
            st = self.res.get(k)
            if st is not None:
                if st[0] is not None:
                    deps.append(st[0])
                lastrd = {}
                for r in st[1]:
                    if r.dma:
                        deps.append(r)
                    else:
                        lastrd[r.eng] = r
                deps.extend(lastrd.values())
        if dma:
            n = self.ndma[eng]
            slot = n % self.NDS
            self.ndma[eng] = n + 1
            prev = self.dma_last.get((eng, slot))
            if prev is not None:
                deps.append(prev)
            o.dslot = slot
            o.dval = 16 * (n // self.NDS + 1)
            self.dma_last[(eng, slot)] = o
        seen = set()
        fdeps = []
        for d in deps:
            if id(d) in seen:
                continue
            seen.add(id(d))
            if (not d.dma) and d.eng == "pe" and eng == "pe" and not dma:
                continue
            if not d.dma:
                d.signal = True
            fdeps.append(d)
        o.deps = fdeps
        lst.append(o)
        for k in reads:
            st = self.res.get(k)
            if st is None:
                self.res[k] = [None, [o]]
            else:
                st[1].append(o)
        for k in writes:
            self.res[k] = [o, []]
        return o

    def emit(self):
        nc = self.nc
        for e in self.ENGS:
            cnt = 0
            for o in self.ops[e]:
                if o.dma:
                    continue
                if o.signal:
                    cnt += 1
                    o.semval = cnt
        finals = []
        for e in self.ENGS:
            last = None
            for o in reversed(self.ops[e]):
                if not o.dma:
                    last = o
                    break
            if last is not None and e != "sp":
                if not last.signal:
                    last.signal = True
                    cnt = 0
                    for o in self.ops[e]:
                        if o.dma:
                            continue
                        if o.signal:
                            cnt += 1
                            o.semval = cnt
                finals.append(last)
        finals.extend(self.dma_last.values())
        import contextlib
        with contextlib.ExitStack() as st:
            sems = {}
            for e in self.ENGS:
                sems[("eng", e)] = st.enter_context(nc.semaphore("s_" + e))
                for s in range(self.NDS):
                    sems[("dma", e, s)] = st.enter_context(nc.semaphore("d_%s_%d" % (e, s)))
            block = st.enter_context(nc.Block())
            engobj = {"pe": block.tensor, "act": block.scalar, "dve": block.vector,
                      "pool": block.gpsimd, "sp": block.sync}

            def mk(e):
                lst = self.ops[e]

                def body(eng):
                    seen = {}

                    def wait(d):
                        if d.dma:
                            key, val = ("dma", d.eng, d.dslot), d.dval
                        else:
                            key, val = ("eng", d.eng), d.semval
                        if seen.get(key, 0) >= val:
                            return
                        seen[key] = val
                        eng.wait_ge(sems[key], val)

                    for o in lst:
                        for d in o.deps:
                            wait(d)
                        ins = o.fn(eng)
                        if o.dma:
                            ins.then_inc(sems[("dma", o.eng, o.dslot)], 16)
                        elif o.signal:
                            ins.then_inc(sems[("eng", o.eng)], 1)
                    if e == "sp":
                        for d in finals:
                            wait(d)
                return body

            for e in self.ENGS:
                engobj[e](mk(e))


def _rope_tables(dim):
    r = dim // 2
    half = r // 2
    inv = (10000.0 ** (-np.arange(half, dtype=np.float32) / half)).astype(np.float32)
    tok = np.arange(TS)
    row = (tok // 64).astype(np.float32)
    col = (tok % 64).astype(np.float32)
    cos = np.zeros((dim, TS), np.float32)
    sin = np.zeros((dim, TS), np.float32)
    for f in range(dim):
        pos = row if f < r else col
        i = (f % r) % half
        ang = pos * inv[i]
        cos[f] = np.cos(ang)
        sin[f] = np.sin(ang)
    R = np.zeros((dim, dim), np.float32)
    for f in range(dim):
        base = 0 if f < r else r
        j = f - base
        if j < half:
            R[f, base + j + half] = -1.0
        else:
            R[f, base + j - half] = 1.0
    return cos, sin, np.ascontiguousarray(R.T)


def _consts():
    c = {}
    cos, sin, rt = _rope_tables(128)
    c["cos128"], c["sin128"], c["rot128"] = cos, sin, rt
    cos, sin, rt = _rope_tables(64)
    c["cos64"], c["sin64"], c["rot64"] = cos, sin, rt
    c["ident"] = np.eye(128, dtype=np.float32)
    k = np.arange(128)[:, None]
    q = np.arange(128)[None, :]
    c["mprev"] = np.tile((k >= q).astype(np.float32), (1, 4))
    c["mnext"] = np.tile((k <= q).astype(np.float32), (1, 4))
    sel = np.zeros((2, 2, 128), np.float32)
    sel[0, 0, :] = 1.0
    sel[1, 1, :] = 1.0
    c["sel"] = sel
    j = np.arange(128)[:, None]
    i = np.arange(128)[None, :]
    same = (j // 64) == (i // 64)
    c["g_incl_f"] = (same & (j <= i)).astype(np.float32)
    c["g_suf_f"] = (same & (j > i)).astype(np.float32)
    c["g_incl_b"] = (same & (j >= i)).astype(np.float32)
    c["g_suf_b"] = (same & (j < i)).astype(np.float32)
    c["g_tri_f"] = (j[:64, :] <= i[:, :64]).astype(np.float32)
    c["g_tri_b"] = (j[:64, :] >= i[:, :64]).astype(np.float32)
    return c


CONST_SHAPES = {
    "cos128": (128, TS), "sin128": (128, TS), "rot128": (128, 128),
    "cos64": (64, TS), "sin64": (64, TS), "rot64": (64, 64),
    "ident": (128, 128), "mprev": (128, 512), "mnext": (128, 512), "sel": (2, 2, 128),
    "g_incl_f": (128, 128), "g_suf_f": (128, 128), "g_incl_b": (128, 128), "g_suf_b": (128, 128),
    "g_tri_f": (64, 64), "g_tri_b": (64, 64),
}

LAYER_KIND = [0, 1, 2, 0]
LAYER_J = [0, 0, 0, 1]


class Builder:
    def __init__(self, layers, final_norm=True, stop=None):
        self.stop = stop
        self.layers = layers
        self.final_norm = final_norm
        self.nc = bass.Bass("TRN2", target_bir_lowering=False)
        self.pg = Prog(self.nc)
        self.pg.pre_hook = self.flush
        self.uid = 0

    def din(self, name, shape, dt=F32):
        return self.nc.dram_tensor(name, list(shape), dt, kind="ExternalInput").ap()

    def dout(self, name, shape, dt=F32):
        return self.nc.dram_tensor(name, list(shape), dt, kind="ExternalOutput").ap()

    def dscr(self, name, shape, dt):
        return self.nc.dram_tensor(name, list(shape), dt).ap()

    def declare(self):
        I = {}
        I["xp"] = self.din("xp", (TP, D))
        I["xs"] = self.din("xs", (TS, D))
        I["cvT"] = self.din("cvT", (P, KC, 2))
        I["cwk"] = self.din("cwk", (2, PAST, 512))
        I["cwv"] = self.din("cwv", (2, PAST, 512))
        I["cmc"] = self.din("cmc", (PAST, 256))
        I["cmr"] = self.din("cmr", (PAST, 64))
        I["gsf"] = self.din("gsf", (4, 256, 512))
        I["gsb"] = self.din("gsb", (4, 256, 512))
        I["ada_w"] = [self.din("ada_w%d" % i, (D, 6 * D) if i in self.layers else (1, 1)) for i in range(4)]
        I["ada_bT"] = self.din("ada_bT", (4, P, 96))
        I["ngT"] = self.din("ngT", (4, 2, P, KC))
        need = lambda k: any(LAYER_KIND[l] == k for l in self.layers)
        needj = lambda j: any(LAYER_KIND[l] == 0 and LAYER_J[l] == j for l in self.layers)
        I["win_wqkv"] = [self.din("win_wqkv%d" % j, (D, 3072) if needj(j) else (1, 1)) for j in range(2)]
        I["win_sink"] = self.din("win_sink", (2, 1, 16))
        I["win_wo"] = [self.din("win_wo%d" % j, (D, D) if needj(j) else (1, 1)) for j in range(2)]
        I["mla_wdown"] = self.din("mla_wdown", (D, 832) if need(1) else (1, 1))
        I["mla_qnT"] = self.din("mla_qnT", (P, 4))
        I["mla_wuq"] = self.din("mla_wuq", (512, 3072) if need(1) else (1, 1))
        I["mla_kvn"] = self.din("mla_kvn", (1, 256))
        I["mla_wukv"] = self.din("mla_wukv", (256, 4096) if need(1) else (1, 1))
        I["mla_wo"] = self.din("mla_wo", (D, D) if need(1) else (1, 1))
        I["gla_win"] = self.din("gla_win", (D, 6144) if need(2) else (1, 1))
        I["gla_wa1"] = self.din("gla_wa1", (2, D, 16))
        I["gla_wa2"] = self.din("gla_wa2", (2, 16, 1024))
        I["gla_ba"] = self.din("gla_ba", (2, 1, 1024))
        I["gla_norm"] = self.din("gla_norm", (1, 512))
        I["gla_wo"] = self.din("gla_wo", (D, D) if need(2) else (1, 1))
        I["ffn_w1"] = [self.din("ffn_w1_%d" % i, (D, FFN_H) if i in self.layers else (1, 1)) for i in range(4)]
        I["ffn_w2"] = [self.din("ffn_w2_%d" % i, (FFN_H, D) if i in self.layers else (1, 1)) for i in range(4)]
        I["fnorm"] = self.din("fnorm", (1, D))
        for k, s in CONST_SHAPES.items():
            I["c_" + k] = self.din("c_" + k, s)
        self.I = I
        O = {}
        O["yp"] = self.dout("yp", (TP, D))
        O["ys"] = self.dout("ys", (TS, D))
        O["nwk"] = self.dout("nwk", (2, TP, 512))
        O["nwv"] = self.dout("nwv", (2, TP, 512))
        O["nmc"] = self.dout("nmc", (TP, 256))
        O["nmr"] = self.dout("nmr", (TP, 64))
        O["ngf"] = self.dout("ngf", (4, 4, 256, 512))
        O["ngb"] = self.dout("ngb", (4, 4, 256, 512))
        if self.stop is not None:
            O["dbg"] = self.dout("dbg", (D, T), BF16)
        self.O = O
        S = {}
        S["xres"] = self.dscr("xres", (T, D), F32)
        S["attT"] = self.dscr("attT", (D, T), BF16)
        S["qT"] = self.dscr("qT", (4, T // P, P, 512), BF16)
        S["kT"] = self.dscr("kT", (4, P, T), BF16)
        S["vv"] = self.dscr("vv", (T, 512), BF16)
        NK = T + PAST
        S["ckvT"] = self.dscr("ckvT", (2, P, NK), BF16)
        S["krT"] = self.dscr("krT", (64, NK), BF16)
        S["gqT"] = self.dscr("gqT", (1024, T), BF16)
        S["gkT"] = self.dscr("gkT", (1024, T), BF16)
        S["gk"] = self.dscr("gk", (T, 1024), BF16)
        S["gv"] = self.dscr("gv", (T, 2048), BF16)
        S["gr"] = self.dscr("gr", (T, 2048), BF16)
        S["gsc"] = self.dscr("gsc", (2, T, 1024), F32)
        S["gof"] = self.dscr("gof", (T, 2048), F32)
        S["qnT"] = self.dscr("qnT", (16, P, T), BF16)
        S["qrT"] = self.dscr("qrT", (16, 64, T), BF16)
        self.S = S
        self.dram_names = set()
        for dct in (self.I, self.O, self.S):
            for v in dct.values():
                for a in (v if isinstance(v, list) else [v]):
                    self.dram_names.add(a.name)

    def sb(self, name, shape, dt):
        return self._stack.enter_context(self.nc.sbuf_tensor(name, list(shape), dt))

    def psum(self, name, shape, dt):
        return self._stack.enter_context(self.nc.psum_tensor(name, list(shape), dt))

    def dma(self, q, out, in_, reads=(), writes=()):
        if self.STORE_Q_POOL and q == "sp" and in_.name not in self.dram_names:
            q = "pool"
        return self.pg.op(q, lambda e, o=out, i=in_: e.dma_start(out=o, in_=i), reads, writes, dma=True)

    def nps(self):
        i = self._psi
        self._psi = (i + 1) % len(self.PS)
        return i

    def mm(self, out, lhsT, rhs, start, stop, reads, writes=()):
        return self.pg.op("pe", lambda e: e.matmul(out, lhsT=lhsT, rhs=rhs, start=start, stop=stop), reads, writes)

    def tr(self, out, in_, reads, writes):
        return self.pg.op("pe", lambda e: e.transpose(out=out, in_=in_, identity=self.ident[0:in_.shape[0], 0:in_.shape[0]]),
                          list(reads) + ["ident"], writes)

    def act(self, out, in_, func, reads, writes, **kw):
        return self.pg.op("act", lambda e: e.activation(out=out, in_=in_, func=func, **kw), reads, writes)

    def cp(self, eng, out, in_, reads, writes):
        if eng == "act":
            return self.pg.op("act", lambda e: e.copy(out=out, in_=in_), reads, writes)
        return self.pg.op(eng, lambda e: e.tensor_copy(out=out, in_=in_), reads, writes)

    def tt(self, eng, out, a, b, op, reads, writes):
        return self.pg.op(eng, lambda e: e.tensor_tensor(out=out, in0=a, in1=b, op=op), reads, writes)

    def ts(self, eng, out, a, s1, s2, op0, op1, reads, writes):
        if s2 is None:
            return self.pg.op(eng, lambda e: e.tensor_scalar(out=out, in0=a, scalar1=s1, scalar2=None, op0=op0), reads, writes)
        return self.pg.op(eng, lambda e: e.tensor_scalar(out=out, in0=a, scalar1=s1, scalar2=s2, op0=op0, op1=op1), reads, writes)

    def stt(self, eng, out, a, sc, b, op0, op1, reads, writes):
        return self.pg.op(eng, lambda e: e.scalar_tensor_tensor(out=out, in0=a, scalar=sc, in1=b, op0=op0, op1=op1), reads, writes)

    def load_w(self, Wap, r0, nk, c0, ncols):
        pg = self.pg
        i = self._wti
        self._wti = (i + 1) % len(self.WT)
        wt = self.WT[i]
        gsz = max(1, min(nk, 2048 // ncols))
        ng = (nk + gsz - 1) // gsz
        self.wt_gsz[i] = gsz
        for g in range(ng):
            k0 = g * gsz
            kn = min(gsz, nk - k0)
            b = self._stgi
            self._stgi = (b + 1) % 2
            stg = self.STGF[b][:, 0:kn * ncols].rearrange("p (k n) -> p k n", n=ncols)
            src = Wap[r0 + k0 * P:r0 + (k0 + kn) * P, c0:c0 + ncols].rearrange("(kc p) n -> p kc n", p=P)
            self.dma("sp", stg, src, writes=[("stg", b)])
            ce = ("pool", "dve", "act")[self._casti % 3]
            self._casti += 1
            wkeys = [("wt", i, g)] + ([("wt", i, x) for x in range(1, 16)] if g == 0 else [])
            self.cp(ce, wt[:, k0:k0 + kn, 0:ncols], stg, [("stg", b)], wkeys)
        return i

    def build(self):
        import contextlib
        self.declare()
        nc, pg, I, O, S = self.nc, self.pg, self.I, self.O, self.S
        with contextlib.ExitStack() as stack:
            self._stack = stack
            self.hT = self.sb("hT", (P, KC, 1024), BF16)
            self.aT = self.sb("aT", (P, KC, 1024), BF16)
            self.big = self.sb("big", (P, KC, 1024), BF16)
            self.WT = [self.sb("wt%d" % i, (P, KC, 512), BF16) for i in range(2)]
            self._wti = 0
            self.XT = [self.sb("xt%d" % i, (P, D), F32) for i in range(2)]
            self.XN = [self.sb("xn%d" % i, (P, D), BF16) for i in range(2)]
            self.gbc = self.sb("gbc", (P, D), F32)
            bigf = self.big[:, :, :].rearrange("p a b -> p (a b)").bitcast(F32)
            self.STG = [bigf[:, 4096 + i * 2048:4096 + (i + 1) * 2048].rearrange("p (k n) -> p k n", n=P) for i in range(2)]
            self.STGF = [bigf[:, 4096 + i * 2048:4096 + (i + 1) * 2048] for i in range(2)]
            self.wt_gsz = [1, 1]
            self._stgi = 0
            self._casti = 0
            self.modT = self.sb("modT", (P, 96, 2), F32)
            self.adaB = self.sb("adaB", (P, 96), F32)
            self.AB = self.sb("AB", (P, 4, KC, 2), F32)
            self.ngt = self.sb("ngt", (P, 2, KC), F32)
            self.sc = self.sb("sc", (P, KC, 2), F32)
            self.grow = self.sb("grow", (2, 512), F32)
            self.sel = self.sb("sel", (2, 2, P), F32)
            self.ident_f = self.sb("ident_f", (P, P), F32)
            self.ident = self.sb("ident", (P, P), BF16)
            self.ones = self.sb("ones", (P, P), BF16)
            self.stat = self.sb("stat", (P, 16), F32)
            self.small = self.sb("small", (P, 64), F32)
            self.cosT = self.sb("cosT", (P, 1024), F32)
            self.sinT = self.sb("sinT", (P, 1024), F32)
            self.rot = self.sb("rot", (P, P), BF16)
            self.rotf = self.sb("rotf", (P, P), F32)
            self.mprev = self.sb("mprev", (P, 512), BF16)
            self.mnext = self.sb("mnext", (P, 512), BF16)
            self.mtmp = self.sb("mtmp", (P, 512), F32)
            self.kvn_bc = self.sb("kvn_bc", (P, 256), F32)
            self.rot64 = self.sb("rot64", (64, 64), BF16)
            self.w2b = self.sb("w2b", (16, 512), BF16)
            self.bab = self.sb("bab", (1, 512), BF16)
            self.ET = [self.sb("et%d" % i, (P, 512), BF16) for i in range(3)]
            self.tmpf = [self.sb("tmpf%d" % i, (P, 512), F32) for i in range(2)]
            self.tmpb = [self.sb("tmpb%d" % i, (P, 512), BF16) for i in range(2)]
            self.kvf = [self.sb("kvf%d" % i, (P, 512), F32) for i in range(2)]
            self.PS = [self.psum("ps%d" % i, (P, 512), F32) for i in range(6)]
            self.PT = [self.psum("pt%d" % i, (P, 1024), BF16) for i in range(2)]
            self.growm = [self.tmpf[0][0:2, :], self.tmpf[1][0:2, :]]
            self._psi = 0
            self._pti = 0
            self._eti = 0

            self.prologue()
            for li in self.layers:
                self.layer(li)
            self.epilogue()
            self.flush()
            pg.emit()
        return nc

    def prologue(self):
        pg, I, S = self.pg, self.I, self.S
        self.dma("sp", S["xres"][0:TP, :], I["xp"][:, :], writes=[("xres", b) for b in range(0, 8)])
        for b in range(4):
            self.dma("sp", S["xres"][TP + b * 1024:TP + (b + 1) * 1024, :], I["xs"][b * 1024:(b + 1) * 1024, :],
                     writes=[("xres", t) for t in range(8 + b * 8, 16 + b * 8)])
        self.dma("sp", self.ident_f[:, :], I["c_ident"][:, :], writes=["ident_f"])
        self.dma("sp", self.rotf[:, :], I["c_rot128"][:, :], writes=["rotf"])
        self.dma("sp", self.sel[:, :, :], I["c_sel"][:, :, :], writes=["sel"])
        self.dma("sp", self.sc[:, :, :], I["cvT"][:, :, :], writes=["sc_raw"])
        self.dma("sp", self.tmpf[0][:, :], I["c_mprev"][:, :], writes=[("tmpf", 0)])
        self.dma("sp", self.tmpf[1][:, :], I["c_mnext"][:, :], writes=[("tmpf", 1)])
        pg.op("dve", lambda e: e.tensor_copy(out=self.ident[:, :], in_=self.ident_f[:, :]), ["ident_f"], ["ident"])
        pg.op("dve", lambda e: e.tensor_copy(out=self.rot[:, :], in_=self.rotf[:, :]), ["rotf"], ["rot"])
        pg.op("dve", lambda e: e.memset(self.ones[:, :], 1.0), [], ["ones"])
        pg.op("dve", lambda e: e.tensor_copy(out=self.mprev[:, :], in_=self.tmpf[0][:, :]), [("tmpf", 0)], ["mprev"])
        pg.op("dve", lambda e: e.tensor_copy(out=self.mnext[:, :], in_=self.tmpf[1][:, :]), [("tmpf", 1)], ["mnext"])
        pg.op("act", lambda e: e.activation(out=self.sc[:, :, :], in_=self.sc[:, :, :], func=AF.Silu),
              ["sc_raw"], ["sc_raw", "sc"])

    def modulation(self, li):
        pg, I = self.pg, self.I
        self.dma("sp", self.adaB[:, :], I["ada_bT"][li], writes=["adaB"])
        self.dma("sp", self.ngt[:, :, :], I["ngT"][li].rearrange("n p k -> p n k"), writes=["ngt"])
        W = I["ada_w"][li]
        psi = self.nps()
        ps = self.PS[psi]
        first = True

        def transposes(cbk):
            nonlocal first
            for c in range(4):
                j = cbk * 4 + c
                self.mm(ps[:, 2 * j:2 * j + 2], self.growm[cbk % 2][0:2, c * P:(c + 1) * P], self.ident_f[0:2, 0:2], True, True,
                        [("tmpf", cbk % 2), "ident_f"], [("ps", psi)] if first else [])
                first = False

        for cbk in range(24):
            pri = self.nps()
            while pri == psi:
                pri = self.nps()
            pr = self.PS[pri]
            for g in range(4):
                b = self._stgi
                self._stgi = (b + 1) % 2
                stg = self.STGF[b][:, :].rearrange("p (k n) -> p k n", n=512)
                self.dma("sp", stg, W[g * 512:(g + 1) * 512, cbk * 512:(cbk + 1) * 512].rearrange("(kc p) n -> p kc n", p=P),
                         writes=[("stg", b)])
                for k4 in range(4):
                    kc = g * 4 + k4
                    self.mm(pr[0:2, :], self.sc[:, kc, :], stg[:, k4, :], kc == 0, kc == KC - 1, [("stg", b), "sc"],
                            [("ps", pri)] if kc == 0 else [])
            self._mark_ps(pri)
            self.cp("dve", self.growm[cbk % 2][:, :], pr[0:2, :], [("ps", pri)], [("tmpf", cbk % 2)])
            if cbk > 0:
                transposes(cbk - 1)
        transposes(23)
        self._mark_ps(psi)
        for c in range(2):
            pg.op("dve", lambda e, c=c: e.tensor_tensor(
                out=self.modT[:, :, c], in0=ps[:, 0:192].rearrange("p (j c) -> p j c", c=2)[:, :, c],
                in1=self.adaB[:, :], op=ALU.add), [("ps", psi), "adaB"], ["modT"])
        for n in range(2):
            sh, scl = (0, 1) if n == 0 else (3, 4)
            for c in range(2):
                pg.op("dve", lambda e, n=n, c=c, scl=scl: e.scalar_tensor_tensor(
                    out=self.AB[:, 2 * n, :, c], in0=self.modT[:, scl * 16:(scl + 1) * 16, c], scalar=1.0,
                    in1=self.ngt[:, n, :], op0=ALU.add, op1=ALU.mult), ["modT", "ngt"], ["AB"])
                pg.op("dve", lambda e, n=n, c=c, sh=sh: e.tensor_copy(
                    out=self.AB[:, 2 * n + 1, :, c], in_=self.modT[:, sh * 16:(sh + 1) * 16, c]), ["modT"], ["AB"])

    def load_gate(self, gi, cond):
        pg = self.pg
        seg = (2, 5)[gi]
        for q4 in range(4):
            psj = self.nps()
            pj = self.PS[psj]
            for cc in range(4):
                ch = q4 * 4 + cc
                pg.op("pe", lambda e, pj=pj, cc=cc, ch=ch: e.matmul(
                    pj[0:2, cc * P:(cc + 1) * P], lhsT=self.modT[:, seg * 16 + ch, :], rhs=self.ident_f[:, :],
                    start=True, stop=True), ["modT", "ident_f"], [("ps", psj)] if cc == 0 else [])
            self._mark_ps(psj)
            pg.op("dve", lambda e, pj=pj: e.tensor_copy(out=self.grow[:, :], in_=pj[0:2, :]), [("ps", psj)], ["grow"])
            psk = self.nps()
            pk = self.PS[psk]
            pg.op("pe", lambda e, pk=pk: e.matmul(pk[:, :], lhsT=self.sel[:, cond, :], rhs=self.grow[:, :],
                                                  start=True, stop=True), ["grow", "sel"], [("ps", psk)])
            pg.op("act", lambda e, pk=pk, q4=q4: e.copy(out=self.gbc[:, q4 * 512:(q4 + 1) * 512], in_=pk[:, :]),
                  [("ps", psk)], ["gbc"])

    def norm_block(self, blk, n, cond, src=None):
        pg, S = self.pg, self.S
        for t in range(8):
            tt = blk * 8 + t
            b = tt % 2
            xt, xn = self.XT[b], self.XN[b]
            if src is None:
                self.dma("sp", xt[:, :], S["xres"][tt * P:(tt + 1) * P, :], reads=[("xres", tt)], writes=[("xt", b)])
            else:
                src(t, xt, b)
            st = self.stat
            pg.op("act", lambda e, xt=xt, xn=xn, b=b: e.activation(
                out=xn[:, :], in_=xt[:, :], func=AF.Square, accum_out=self.stat[:, b:b + 1]),
                [("xt", b)], [("xn", b), ("stat", b)])
            pg.op("dve", lambda e, b=b: e.tensor_scalar(
                out=self.stat[:, 2 + b:3 + b], in0=self.stat[:, b:b + 1], scalar1=1.0 / D, scalar2=EPS,
                op0=ALU.mult, op1=ALU.add), [("stat", b)], [("stat2", b)])
            pg.op("act", lambda e, b=b: e.sqrt(out=self.stat[:, 6 + b:7 + b], in_=self.stat[:, 2 + b:3 + b]),
                  [("stat2", b)], [("stat2s", b)])
            pg.op("dve", lambda e, b=b: e.reciprocal(out=self.stat[:, 4 + b:5 + b], in_=self.stat[:, 6 + b:7 + b]),
                  [("stat2s", b)], [("stat3", b)])
            pg.op("dve", lambda e, xt=xt, xn=xn, b=b: e.tensor_scalar(
                out=xn[:, :], in0=xt[:, :], scalar1=self.stat[:, 4 + b:5 + b], scalar2=None, op0=ALU.mult),
                [("xt", b), ("stat3", b)], [("xn", b)])
            for c4 in range(4):
                pi = self._pti
                self._pti = (pi + 1) % 2
                pt = self.PT[pi]
                for cc in range(4):
                    ch = c4 * 4 + cc
                    pg.op("pe", lambda e, pt=pt, xn=xn, cc=cc, ch=ch: e.transpose(
                        out=pt[:, cc * P:(cc + 1) * P], in_=xn[:, ch * P:(ch + 1) * P], identity=self.ident[:, :]),
                        [("xn", b), "ident"], [("pt", pi)])
                for cc in range(4):
                    ch = c4 * 4 + cc
                    pg.op("act", lambda e, pt=pt, cc=cc, ch=ch, t=t: e.activation(
                        out=self.hT[:, ch, t * P:(t + 1) * P], in_=pt[:, cc * P:(cc + 1) * P], func=AF.Identity,
                        scale=self.AB[:, 2 * n, ch, cond:cond + 1], bias=self.AB[:, 2 * n + 1, ch, cond:cond + 1]),
                        [("pt", pi), "AB"], [("hT", ch, t)])

    _pending = None
    STORE_Q_POOL = False

    def flush(self):
        p = self._pending
        if p is not None:
            self._pending = None
            p()

    def _defer(self, W, r0, nk, c0, ncols, body):
        self.pg.hook_off = True
        try:
            wi = self.load_w(W, r0, nk, c0, ncols)
        finally:
            self.pg.hook_off = False
        self.flush()
        self._pending = lambda: body(wi)

    def lin_fm(self, src, srckey, W, r0, nk, c0, ncols, evac):
        self._defer(W, r0, nk, c0, ncols, lambda wi: self._lin_fm_body(src, srckey, nk, ncols, evac, wi))

    def _lin_fm_body(self, src, srckey, nk, ncols, evac, wi):
        pg = self.pg
        wt = self.WT[wi]
        gsz = max(1, min(nk, 2048 // ncols))
        for oc in range(ncols // P):
            for tb in range(2):
                psi = self.nps()
                ps = self.PS[psi]
                for kc in range(nk):
                    rd = [("wt", wi, kc // gsz)] + [(srckey, kc, t) for t in range(tb * 4, tb * 4 + 4)]
                    pg.op("pe", lambda e, ps=ps, wt=wt, kc=kc, oc=oc, tb=tb: e.matmul(
                        ps[:, :], lhsT=wt[:, kc, oc * P:(oc + 1) * P], rhs=src[:, kc, tb * 512:(tb + 1) * 512],
                        start=(kc == 0), stop=(kc == nk - 1)), rd, [("ps", psi)] if kc == 0 else [])
                self._mark_ps(psi)
                evac(ps, psi, oc, tb)

    def _mark_ps(self, psi):
        lst = self.pg.ops["pe"]
        self.pg.res[("ps", psi)] = [lst[-1], []]

    def lin_tm(self, src, srckey, W, r0, nk, c0, ncols, evac, tiles=range(8)):
        self._defer(W, r0, nk, c0, ncols, lambda wi: self._lin_tm_body(src, srckey, nk, ncols, evac, tiles, wi))

    def _lin_tm_body(self, src, srckey, nk, ncols, evac, tiles, wi):
        pg = self.pg
        wt = self.WT[wi]
        gsz = max(1, min(nk, 2048 // ncols))
        for t in tiles:
            psi = self.nps()
            ps = self.PS[psi]
            for kc in range(nk):
                rd = [("wt", wi, kc // gsz), (srckey, kc, t)]
                pg.op("pe", lambda e, ps=ps, wt=wt, kc=kc, t=t: e.matmul(
                    ps[:, 0:ncols], lhsT=src[:, kc, t * P:(t + 1) * P], rhs=wt[:, kc, 0:ncols],
                    start=(kc == 0), stop=(kc == nk - 1)), rd, [("ps", psi)] if kc == 0 else [])
            self._mark_ps(psi)
            evac(ps, psi, t)

    def resid_cols(self, xt, b, ps, psi, cb):
        pg = self.pg
        i = cb % 2
        tf = self.tmpf[i]
        pg.op("dve", lambda e, tf=tf, ps=ps, cb=cb: e.tensor_tensor(
            out=tf[:, :], in0=ps[:, :], in1=self.gbc[:, cb * 512:(cb + 1) * 512], op=ALU.mult),
            [("ps", psi), "gbc"], [("tmpf", i)])
        pg.op("pool", lambda e, tf=tf, xt=xt, cb=cb: e.tensor_tensor(
            out=xt[:, cb * 512:(cb + 1) * 512], in0=xt[:, cb * 512:(cb + 1) * 512], in1=tf[:, :], op=ALU.add),
            [("tmpf", i), ("xt", b)], [("xt", b)])

    def out_proj_and_ffn(self, li, Wo, blk, cond):
        pg, I, S = self.pg, self.I, self.S
        for ch in range(KC):
            self.dma("sp", self.aT[:, ch, :], S["attT"][ch * P:(ch + 1) * P, blk * 1024:(blk + 1) * 1024],
                     reads=[("attT", ch, blk)], writes=[("aT", ch, t) for t in range(8)] + ["knT", "kr_sb", "Sf", "Sb", "AT", "kd", "obf", "ofc", "on", ("qt", 0), ("qt", 1), ("kt", 0), ("kt", 1)])
        self.load_gate(0, cond)
        for cb in range(4):
            def evac(ps, psi, t, cb=cb):
                tt = blk * 8 + t
                b = t % 2
                kf = self.kvf[b]
                self.dma("sp", kf[:, :], S["xres"][tt * P:(tt + 1) * P, cb * 512:(cb + 1) * 512],
                         reads=[("xres", tt)], writes=[("kvf", b)])
                tf = self.tmpf[b]
                pg.op("dve", lambda e: e.tensor_tensor(out=tf[:, :], in0=ps[:, :], in1=self.gbc[:, cb * 512:(cb + 1) * 512],
                                                       op=ALU.mult), [("ps", psi), "gbc"], [("tmpf", b)])
                pg.op("pool", lambda e: e.tensor_tensor(out=kf[:, :], in0=kf[:, :], in1=tf[:, :], op=ALU.add),
                      [("tmpf", b), ("kvf", b)], [("kvf", b)])
                self.dma("sp", S["xres"][tt * P:(tt + 1) * P, cb * 512:(cb + 1) * 512], kf[:, :],
                         reads=[("kvf", b)], writes=[("xres", tt)])
            self.lin_tm(self.aT, "aT", Wo, 0, KC, cb * 512, 512, evac)
        self.norm_block(blk, 1, cond)
        self.load_gate(1, cond)
        W1, W2 = I["ffn_w1"][li], I["ffn_w2"][li]
        for hq in range(4):
            for w4 in range(4):
                def evac1(ps, psi, oc, tb, w4=w4):
                    hc = w4 * 4 + oc
                    i = self._eti
                    self._eti = (i + 1) % 2
                    tb_ = self.tmpb[i]
                    pg.op("act", lambda e: e.activation(out=tb_[:, :], in_=ps[:, :], func=AF.Relu),
                          [("ps", psi)], [("tmpb", i)])
                    pg.op("pool", lambda e: e.tensor_tensor(out=self.aT[:, hc, tb * 512:(tb + 1) * 512], in0=tb_[:, :],
                                                            in1=tb_[:, :], op=ALU.mult),
                          [("tmpb", i)], [("aT", hc, t) for t in range(tb * 4, tb * 4 + 4)])
                self.lin_fm(self.hT, "hT", W1, 0, KC, hq * 2048 + w4 * 512, 512, evac1)
            for cb in range(4):
                def evac2(ps, psi, t, cb=cb):
                    tt = blk * 8 + t
                    b = t % 2
                    kf = self.kvf[b]
                    self.dma("sp", kf[:, :], S["xres"][tt * P:(tt + 1) * P, cb * 512:(cb + 1) * 512],
                             reads=[("xres", tt)], writes=[("kvf", b)])
                    tf = self.tmpf[b]
                    pg.op("dve", lambda e: e.tensor_tensor(out=tf[:, :], in0=ps[:, :],
                                                           in1=self.gbc[:, cb * 512:(cb + 1) * 512], op=ALU.mult),
                          [("ps", psi), "gbc"], [("tmpf", b)])
                    pg.op("pool", lambda e: e.tensor_tensor(out=kf[:, :], in0=kf[:, :], in1=tf[:, :], op=ALU.add),
                          [("tmpf", b), ("kvf", b)], [("kvf", b)])
                    self.dma("sp", S["xres"][tt * P:(tt + 1) * P, cb * 512:(cb + 1) * 512], kf[:, :],
                             reads=[("kvf", b)], writes=[("xres", tt)])
                self.lin_tm(self.aT, "aT", W2, hq * 2048, KC, cb * 512, 512, evac2)

    def layer(self, li):
        kind, j = LAYER_KIND[li], LAYER_J[li]
        if li != self.layers[-1]:
            saved, self.stop = self.stop, None
            try:
                self._layer(li, kind, j)
            finally:
                self.stop = saved
        else:
            self._layer(li, kind, j)

    def _layer(self, li, kind, j):
        self.modulation(li)
        if self.stop == "mod":
            return
        if kind == 0:
            self.win_layer(li, j)
        elif kind == 1:
            self.mla_layer(li)
        else:
            self.gla_layer(li)

    def win_layer(self, li, j):
        pg, I, O, S = self.pg, self.I, self.O, self.S
        Wq = I["win_wqkv"][j]
        scale = 128.0 ** -0.5
        self.dma("sp", self.small[0:1, 16:32], I["win_sink"][j], writes=["sinkrow"])
        psj = self.nps()
        pg.op("pe", lambda e: e.matmul(self.PS[psj][:, 0:16], lhsT=self.sel[0:1, 0, :],
                                      rhs=self.small[0:1, 16:32], start=True, stop=True), ["sinkrow", "sel"], [("ps", psj)])
        pg.op("act", lambda e: e.activation(out=self.small[:, 0:16], in_=self.PS[psj][:, 0:16], func=AF.Exp),
              [("ps", psj)], ["sinkexp"])
        for blk in range(NBLK):
            def _blk(blk=blk):
                cond = 0 if blk == 0 else 1
                sample = blk > 0
                self.norm_block(blk, 0, cond)
                if self.stop == "norm":
                    return True
                tok0 = blk * 1024
                if sample:
                    s0 = (blk - 1) * 1024
                    self.dma("sp", self.cosT[:, :], I["c_cos128"][:, s0:s0 + 1024], writes=["cosT"])
                    self.dma("sp", self.sinT[:, :], I["c_sin128"][:, s0:s0 + 1024], writes=["sinT"])
                for w in range(5):
                    def evac(ps, psi, oc, tb, w=w):
                        i = self._eti
                        self._eti = (i + 1) % 2
                        tb_ = self.tmpb[i]
                        t0 = tok0 + tb * 512
                        if not sample:
                            pg.op("act", lambda e: e.copy(out=tb_[:, :], in_=ps[:, :]), [("ps", psi)], [("tmpb", i)])
                            res = tb_
                            rkey = ("tmpb", i)
                        else:
                            pg.op("act", lambda e: e.copy(out=tb_[:, :], in_=ps[:, :]), [("ps", psi)], [("tmpb", i)])
                            ps2i = self.nps()
                            ps2 = self.PS[ps2i]
                            pg.op("pe", lambda e: e.matmul(ps2[:, :], lhsT=self.rot[:, :], rhs=tb_[:, :], start=True, stop=True),
                                  [("tmpb", i), "rot"], [("ps", ps2i)])
                            tf = self.tmpf[i]
                            pg.op("dve", lambda e: e.tensor_tensor(out=tf[:, :], in0=ps2[:, :], in1=self.sinT[:, tb * 512:(tb + 1) * 512], op=ALU.mult),
                                  [("ps", ps2i), "sinT"], [("tmpf", i)])
                            pg.op("pool", lambda e: e.tensor_tensor(out=self.mtmp[:, :], in0=tb_[:, :], in1=self.cosT[:, tb * 512:(tb + 1) * 512], op=ALU.mult),
                                  [("tmpb", i), "cosT"], ["mtmp"])
                            pg.op("dve", lambda e: e.tensor_tensor(out=tb_[:, :], in0=tf[:, :], in1=self.mtmp[:, :], op=ALU.add),
                                  [("tmpf", i), "mtmp"], [("tmpb", i)])
                            res = tb_
                            rkey = ("tmpb", i)
                        if w < 4:
                            h = w * 4 + oc
                            kvh, g = h // 4, h % 4
                            qb0 = t0 // P
                            dst = S["qT"][kvh, qb0:qb0 + 4, :, g * P:(g + 1) * P].rearrange("q d i -> d q i")
                            self.dma("sp", dst, res[:, :].rearrange("d (q i) -> d q i", i=P), reads=[rkey],
                                     writes=[("qT", kvh, qb0 + x) for x in range(4)])
                        else:
                            kvh = oc
                            self.dma("sp", S["kT"][kvh, :, t0:t0 + 512], res[:, :], reads=[rkey],
                                     writes=[("kT", kvh, t0 // 512)])
                    c0 = w * 512 if w < 4 else 2048
                    self.lin_fm(self.hT, "hT", Wq, 0, KC, c0, 512, evac)
                def evac_v(ps, psi, t):
                    tt = blk * 8 + t
                    i = self._eti
                    self._eti = (i + 1) % 2
                    tb_ = self.tmpb[i]
                    if sample:
                        pg.op("act", lambda e: e.copy(out=tb_[:, :], in_=ps[:, :]), [("ps", psi)], [("tmpb", i)])
                    else:
                        b = t % 2
                        kf = self.kvf[b]
                        pg.op("dve", lambda e: e.tensor_copy(out=kf[:, :], in_=ps[:, :]), [("ps", psi)], [("kvf", b)])
                        pg.op("act", lambda e: e.copy(out=tb_[:, :], in_=kf[:, :]), [("kvf", b)], [("tmpb", i)])
                        self.dma("sp", O["nwv"][j, tt * P:(tt + 1) * P, :], kf[:, :], reads=[("kvf", b)], writes=[("nwv", j, tt)])
                    self.dma("sp", S["vv"][tt * P:(tt + 1) * P, :], tb_[:, :], reads=[("tmpb", i)], writes=[("vv", tt)])
                self.lin_tm(self.hT, "hT", Wq, 0, KC, 2560, 512, evac_v)
                if not sample:
                    def evac_k(ps, psi, t):
                        tt = blk * 8 + t
                        b = t % 2
                        kf = self.kvf[b]
                        pg.op("dve", lambda e: e.tensor_copy(out=kf[:, :], in_=ps[:, :]), [("ps", psi)], [("kvf", b)])
                        self.dma("sp", O["nwk"][j, tt * P:(tt + 1) * P, :], kf[:, :], reads=[("kvf", b)], writes=[("nwk", j, tt)])
                    self.lin_tm(self.hT, "hT", Wq, 0, KC, 2048, 512, evac_k)
                if self.stop == "qkv":
                    return True
            if _blk():
                return
        if self.stop == "p1":
            return
        kT_sb = self.big
        kflat = self.big[:, :, :].rearrange("p a b -> p (a b)")
        vflat = self.hT[:, :, :].rearrange("p a b -> p (a b)")
        for kvh in range(4):
            self.dma("sp", kflat[:, 0:TP], S["kT"][kvh, :, 0:TP], reads=[("kT", kvh, 0), ("kT", kvh, 1)], writes=["kflat"])
            self.dma("sp", vflat[:, 0:8 * P].rearrange("p (t c) -> p t c", c=P),
                     S["vv"][0:TP, kvh * P:(kvh + 1) * P].rearrange("(t p) c -> p t c", p=P),
                     reads=[("vv", t) for t in range(8)], writes=["vflat"])
            for s in range(4):
                for qb in range(2):
                    qblk = s * 2 + qb
                    tiles = [(s * 2 + kt, None) for kt in range(2)]
                    self.attn_block(kvh, qblk, tiles, kflat, vflat, scale, True)
            if self.stop == "attp":
                return
            for ct in range(4):
                b = ct % 2
                kf = self.kvf[b]
                self.dma("sp", kf[:, 0:P], I["cwk"][j, ct * P:(ct + 1) * P, kvh * P:(kvh + 1) * P], writes=[("kvf", b)])
                self.dma("sp", kf[:, P:2 * P], I["cwv"][j, ct * P:(ct + 1) * P, kvh * P:(kvh + 1) * P], writes=[("kvf", b)])
                xn = self.XN[b]
                pg.op("dve", lambda e, xn=xn, kf=kf: e.tensor_copy(out=xn[:, 0:P], in_=kf[:, 0:P]), [("kvf", b)], [("xn", b)])
                pg.op("dve", lambda e, kf=kf, ct=ct: e.tensor_copy(out=vflat[:, ct * P:(ct + 1) * P], in_=kf[:, P:2 * P]),
                      [("kvf", b)], ["vflat"])
                pi = self._pti
                self._pti = (pi + 1) % 2
                pt = self.PT[pi]
                pg.op("pe", lambda e, pt=pt, xn=xn: e.transpose(out=pt[:, 0:P], in_=xn[:, 0:P], identity=self.ident[:, :]),
                      [("xn", b), "ident"], [("pt", pi)])
                pg.op("act", lambda e, pt=pt, ct=ct: e.copy(out=kflat[:, ct * P:(ct + 1) * P], in_=pt[:, 0:P]),
                      [("pt", pi)], ["kflat"])
            self.dma("sp", kflat[:, 512:512 + TS], S["kT"][kvh, :, TP:T], reads=[("kT", kvh, x) for x in range(2, 10)],
                     writes=["kflat"])
            for v4 in range(4):
                self.dma("sp", vflat[:, 512 + v4 * 1024:512 + (v4 + 1) * 1024].rearrange("p (t c) -> p t c", c=P),
                         S["vv"][TP + v4 * 1024:TP + (v4 + 1) * 1024, kvh * P:(kvh + 1) * P].rearrange("(t p) c -> p t c", p=P),
                         reads=[("vv", t) for t in range(8 + v4 * 8, 16 + v4 * 8)], writes=[("vflat", v4)])
            for n in range(32):
                tiles = [(ct, None) for ct in range(4)]
                if n > 0:
                    tiles.append((4 + n - 1, self.mprev))
                tiles.append((4 + n, None))
                if n < 31:
                    tiles.append((4 + n + 1, self.mnext))
                self.attn_block(kvh, 8 + n, tiles, kflat, vflat, scale, True)
        if self.stop == "att":
            return
        for blk in range(NBLK):
            self.out_proj_and_ffn(li, I["win_wo"][j], blk, 0 if blk == 0 else 1)

    def attn_block(self, kvh, qblk, tiles, kflat, vflat, scale, use_sink):
        pg, S = self.pg, self.S
        qi = qblk % 2
        qsb = self.XN[qi][:, 0:512]
        self.dma("sp", qsb, S["qT"][kvh, qblk], reads=[("qT", kvh, qblk)], writes=[("xn", qi)])
        par = self._attpar
        self._attpar = 1 - par
        oi, di = 2 * par, 2 * par + 1
        ops_, dps = self.PS[oi], self.PS[di]
        pend = None
        n = len(tiles)
        first = True

        def od(kt, ei, last):
            nonlocal first
            et = self.ET[ei]
            pg.op("pe", lambda e, f=first: e.matmul(ops_[:, :], lhsT=vflat[:, kt * P:(kt + 1) * P], rhs=et[:, :],
                                                   start=f, stop=last), [("et", ei), "vflat"] + [("vflat", q) for q in range(4)], [("ps", oi)] if first else [])
            pg.op("pe", lambda e, f=first: e.matmul(dps[:, :], lhsT=self.ones[:, :], rhs=et[:, :],
                                                   start=f, stop=last), [("et", ei), "ones"], [("ps", di)] if first else [])
            first = False

        for idx, (kt, mask) in enumerate(tiles):
            si = 4 + self._atts
            self._atts = 1 - self._atts
            sps = self.PS[si]
            pg.op("pe", lambda e, sps=sps, kt=kt: e.matmul(sps[:, :], lhsT=kflat[:, kt * P:(kt + 1) * P], rhs=qsb,
                                                          start=True, stop=True), ["kflat", ("xn", qi)], [("ps", si)])
            if pend is not None:
                od(pend[0], pend[1], False)
            ei = self._et3
            self._et3 = (ei + 1) % 3
            et = self.ET[ei]
            pg.op("act", lambda e, sps=sps, et=et: e.activation(out=et[:, :], in_=sps[:, :], func=AF.Exp, scale=scale),
                  [("ps", si)], [("et", ei)])
            if mask is not None:
                pg.op("pool", lambda e, et=et, mask=mask: e.tensor_tensor(out=et[:, :], in0=et[:, :], in1=mask[:, :], op=ALU.mult),
                      [("et", ei), "mprev", "mnext"], [("et", ei)])
            pend = (kt, ei)
        od(pend[0], pend[1], True)
        self.pg.res[("ps", oi)] = [self.pg.ops["pe"][-2], []]
        self.pg.res[("ps", di)] = [self.pg.ops["pe"][-1], []]
        b = qblk % 2
        tf = self.tmpf[b]
        for g in range(4):
            h = kvh * 4 + g
            if use_sink:
                pg.op("dve", lambda e, g=g, h=h: e.tensor_scalar(out=tf[:, g * P:(g + 1) * P], in0=dps[:, g * P:(g + 1) * P],
                                                                 scalar1=self.small[:, h:h + 1], scalar2=None, op0=ALU.add),
                      [("ps", di), "sinkexp"], [("tmpf", b)])
        pg.op("dve", lambda e: e.reciprocal(out=tf[:, :], in_=tf[:, :]), [("tmpf", b)], [("tmpf", b)])
        ob = self.tmpb[b]
        pg.op("dve", lambda e: e.tensor_tensor(out=ob[:, :], in0=ops_[:, :], in1=tf[:, :], op=ALU.mult),
              [("ps", oi), ("tmpf", b)], [("tmpb", b)])
        dst = S["attT"][kvh * 512:(kvh + 1) * 512, qblk * P:(qblk + 1) * P].rearrange("(g d) i -> d g i", d=P)
        self.dma("sp", dst, ob[:, :].rearrange("d (g i) -> d g i", i=P), reads=[("tmpb", b)],
                 writes=[("attT", kvh * 4 + g, qblk // 8) for g in range(4)])

    _et3 = 0
    _attpar = 0
    _atts = 0
    _cs_loaded = -1

    def rstd_from_ss(self, b, n):
        pg = self.pg
        pg.op("dve", lambda e: e.tensor_scalar(out=self.stat[:, 2 + b:3 + b], in0=self.stat[:, b:b + 1], scalar1=1.0 / n,
                                               scalar2=EPS, op0=ALU.mult, op1=ALU.add), [("stat", b)], [("stat2", b)])
        pg.op("act", lambda e: e.sqrt(out=self.stat[:, 6 + b:7 + b], in_=self.stat[:, 2 + b:3 + b]),
              [("stat2", b)], [("stat2s", b)])
        pg.op("dve", lambda e: e.reciprocal(out=self.stat[:, 4 + b:5 + b], in_=self.stat[:, 6 + b:7 + b]),
              [("stat2s", b)], [("stat3", b)])

    def rope_small(self, src_ps, psi, nrow, ncol, coff, dst_tb, dkey, i):
        pg = self.pg
        pg.op("act", lambda e: e.copy(out=dst_tb[0:nrow, 0:ncol], in_=src_ps[0:nrow, 0:ncol]), [("ps", psi)], [dkey])
        p2i = self.nps()
        p2 = self.PS[p2i]
        pg.op("pe", lambda e: e.matmul(p2[0:nrow, 0:ncol], lhsT=self.rot64[0:nrow, 0:nrow], rhs=dst_tb[0:nrow, 0:ncol],
                                      start=True, stop=True), [dkey, "rot64"], [("ps", p2i)])
        tf = self.tmpf[i]
        pg.op("dve", lambda e: e.tensor_tensor(out=tf[0:nrow, 0:ncol], in0=p2[0:nrow, 0:ncol],
                                               in1=self.sinT[0:nrow, coff:coff + ncol], op=ALU.mult),
              [("ps", p2i), "sinT"], [("tmpf", i)])
        pg.op("pool", lambda e: e.tensor_tensor(out=self.mtmp[0:nrow, 0:ncol], in0=dst_tb[0:nrow, 0:ncol],
                                                in1=self.cosT[0:nrow, coff:coff + ncol], op=ALU.mult),
              [dkey, "cosT"], ["mtmp"])
        pg.op("dve", lambda e: e.tensor_tensor(out=dst_tb[0:nrow, 0:ncol], in0=tf[0:nrow, 0:ncol],
                                               in1=self.mtmp[0:nrow, 0:ncol], op=ALU.add),
              [("tmpf", i), "mtmp"], [dkey])

    def mla_layer(self, li):
        pg, I, O, S = self.pg, self.I, self.O, self.S
        NK = T + PAST
        scale = 192.0 ** -0.5
        Wd, Wuq, Wukv = I["mla_wdown"], I["mla_wuq"], I["mla_wukv"]
        self.dma("sp", self.small[:, 32:36], I["mla_qnT"][:, :], writes=["qnT_c"])
        self.dma("sp", self.grow[0:1, 0:256], I["mla_kvn"][:, :], writes=["grow"])
        psj = self.nps()
        pg.op("pe", lambda e: e.matmul(self.PS[psj][:, 0:256], lhsT=self.sel[0:1, 0, :], rhs=self.grow[0:1, 0:256],
                                      start=True, stop=True), ["grow", "sel"], [("ps", psj)])
        pg.op("act", lambda e: e.copy(out=self.kvn_bc[:, :], in_=self.PS[psj][:, 0:256]), [("ps", psj)], ["kvn_bc"])
        self.dma("sp", self.tmpf[0][0:64, 0:64], I["c_rot64"][:, :], writes=[("tmpf", 0)])
        pg.op("dve", lambda e: e.tensor_copy(out=self.rot64[:, :], in_=self.tmpf[0][0:64, 0:64]), [("tmpf", 0)], ["rot64"])
        cqT = self.aT
        for ct in range(4):
            b = ct % 2
            kf = self.kvf[b]
            self.dma("sp", kf[:, 0:256], I["cmc"][ct * P:(ct + 1) * P, :], writes=[("kvf", b)])
            self.dma("sp", kf[:, 256:320], I["cmr"][ct * P:(ct + 1) * P, :], writes=[("kvf", b)])
            self.mla_kv_tile(kf, b, TP + ct * P, False)
        for blk in range(NBLK):
            def _blk(blk=blk):
                cond = 0 if blk == 0 else 1
                sample = blk > 0
                self.norm_block(blk, 0, cond)
                tok0 = blk * 1024
                if sample:
                    s0 = (blk - 1) * 1024
                    self.dma("sp", self.cosT[0:64, :], I["c_cos64"][:, s0:s0 + 1024], writes=["cosT"])
                    self.dma("sp", self.sinT[0:64, :], I["c_sin64"][:, s0:s0 + 1024], writes=["sinT"])
                def evac_cq(ps, psi, t):
                    b = t % 2
                    i = b
                    tb_ = self.tmpb[i]
                    pg.op("act", lambda e: e.activation(out=tb_[:, :], in_=ps[:, :], func=AF.Square,
                                                        accum_out=self.stat[:, b:b + 1]), [("ps", psi)], [("tmpb", i), ("stat", b)])
                    self.rstd_from_ss(b, 512)
                    pg.op("dve", lambda e: e.tensor_scalar(out=tb_[:, :], in0=ps[:, :], scalar1=self.stat[:, 4 + b:5 + b],
                                                           scalar2=None, op0=ALU.mult), [("ps", psi), ("stat3", b)], [("tmpb", i)])
                    pi = self._pti
                    self._pti = (pi + 1) % 2
                    pt = self.PT[pi]
                    for c in range(4):
                        pg.op("pe", lambda e, c=c: e.transpose(out=pt[:, c * P:(c + 1) * P], in_=tb_[:, c * P:(c + 1) * P],
                                                               identity=self.ident[:, :]), [("tmpb", i), "ident"], [("pt", pi)])
                    for c in range(4):
                        pg.op("act", lambda e, c=c: e.activation(out=cqT[:, c, t * P:(t + 1) * P], in_=pt[:, c * P:(c + 1) * P],
                                                                 func=AF.Identity, scale=self.small[:, 32 + c:33 + c]),
                              [("pt", pi), "qnT_c"], [("aT", c, t)])
                self.lin_tm(self.hT, "hT", Wd, 0, KC, 0, 512, evac_cq)
                def evac_kv(ps, psi, t):
                    tt = blk * 8 + t
                    b = t % 2
                    kf = self.kvf[b]
                    i = b
                    tb_ = self.tmpb[i]
                    pg.op("act", lambda e: e.activation(out=tb_[:, 0:256], in_=ps[:, 0:256], func=AF.Square,
                                                        accum_out=self.stat[:, 8 + b:9 + b]), [("ps", psi)], [("tmpb", i), ("stat", 8 + b)])
                    pg.op("dve", lambda e: e.tensor_scalar(out=self.stat[:, 10 + b:11 + b], in0=self.stat[:, 8 + b:9 + b],
                                                           scalar1=1.0 / 256, scalar2=EPS, op0=ALU.mult, op1=ALU.add),
                          [("stat", 8 + b)], [("stat2", 8 + b)])
                    pg.op("act", lambda e: e.sqrt(out=self.stat[:, 12 + b:13 + b], in_=self.stat[:, 10 + b:11 + b]),
                          [("stat2", 8 + b)], [("stat2s", 8 + b)])
                    pg.op("dve", lambda e: e.reciprocal(out=self.stat[:, 14 + b:15 + b], in_=self.stat[:, 12 + b:13 + b]),
                          [("stat2s", 8 + b)], [("stat3", 8 + b)])
                    pg.op("dve", lambda e: e.scalar_tensor_tensor(out=kf[:, 0:256], in0=ps[:, 0:256], scalar=self.stat[:, 14 + b:15 + b],
                                                                  in1=self.kvn_bc[:, :], op0=ALU.mult, op1=ALU.mult),
                          [("ps", psi), ("stat3", 8 + b), "kvn_bc"], [("kvf", b)])
                    pg.op("dve", lambda e: e.tensor_copy(out=kf[:, 256:320], in_=ps[:, 256:320]), [("ps", psi), ("stat3", 8 + b)], [("kvf", b)])
                    if not sample:
                        self.dma("sp", O["nmc"][tt * P:(tt + 1) * P, :], kf[:, 0:256], reads=[("kvf", b)], writes=[("nmc", tt)])
                        self.dma("sp", O["nmr"][tt * P:(tt + 1) * P, :], kf[:, 256:320], reads=[("kvf", b)], writes=[("nmr", tt)])
                    col0 = tt * P if not sample else TP + PAST + (tt - 8) * P
                    self.mla_kv_tile(kf, b, col0, sample, (t * P) if sample else 0)
                self.lin_tm(self.hT, "hT", Wd, 0, KC, 512, 320, evac_kv)
                for h in range(16):
                    wi = self.load_w(Wuq, 0, 4, h * 192, 192)
                    wt = self.WT[wi]
                    for tb in range(2):
                        t0 = tok0 + tb * 512
                        psi = self.nps()
                        ps = self.PS[psi]
                        for kc in range(4):
                            pg.op("pe", lambda e, ps=ps, kc=kc, tb=tb, wt=wt: e.matmul(ps[:, :], lhsT=wt[:, kc, 0:P], rhs=cqT[:, kc, tb * 512:(tb + 1) * 512],
                                                                              start=(kc == 0), stop=(kc == 3)),
                                  [("wt", wi, 0)] + [("aT", kc, t) for t in range(tb * 4, tb * 4 + 4)], [("ps", psi)] if kc == 0 else [])
                        self._mark_ps(psi)
                        i = self._eti
                        self._eti = (i + 1) % 2
                        tb_ = self.tmpb[i]
                        pg.op("act", lambda e, tb_=tb_, ps=ps: e.copy(out=tb_[:, :], in_=ps[:, :]), [("ps", psi)], [("tmpb", i)])
                        self.dma("sp", S["qnT"][h, :, t0:t0 + 512], tb_[:, :], reads=[("tmpb", i)], writes=[("qnT", h, t0 // 512)])
                        psi2 = self.nps()
                        ps2 = self.PS[psi2]
                        for kc in range(4):
                            pg.op("pe", lambda e, ps2=ps2, kc=kc, tb=tb, wt=wt: e.matmul(ps2[0:64, :], lhsT=wt[:, kc, P:192], rhs=cqT[:, kc, tb * 512:(tb + 1) * 512],
                                                                                start=(kc == 0), stop=(kc == 3)),
                                  [("wt", wi, 1)] + [("aT", kc, t) for t in range(tb * 4, tb * 4 + 4)], [("ps", psi2)] if kc == 0 else [])
                        self._mark_ps(psi2)
                        i2 = self._eti
                        self._eti = (i2 + 1) % 2
                        tb2 = self.tmpb[i2]
                        if sample:
                            self.rope_small(ps2, psi2, 64, 512, tb * 512, tb2, ("tmpb", i2), i2)
                        else:
                            pg.op("act", lambda e, tb2=tb2, ps2=ps2: e.copy(out=tb2[0:64, :], in_=ps2[0:64, :]), [("ps", psi2)], [("tmpb", i2)])
                        self.dma("sp", S["qrT"][h, :, t0:t0 + 512], tb2[0:64, :], reads=[("tmpb", i2)], writes=[("qrT", h, t0 // 512)])
            if _blk():
                return
        if self.stop == "p1":
            return
        hflat = self.hT[:, :, :].rearrange("p a b -> p (a b)")
        aflat = self.aT[:, :, :].rearrange("p a b -> p (a b)")
        bflat = self.big[:, :, :].rearrange("p a b -> p (a b)")
        ckv_sb = [hflat[:, 0:NK], hflat[:, NK:2 * NK]]
        kr_sb = aflat[0:64, 0:NK]
        knT = aflat[:, NK:2 * NK]
        vsb = bflat[:, 0:NK]
        for c in range(2):
            for q4 in range(4):
                w = NK // 4
                self.dma("sp", ckv_sb[c][:, q4 * w:(q4 + 1) * w], S["ckvT"][c, :, q4 * w:(q4 + 1) * w],
                         reads=[("ckvT", x) for x in range(NK // P)], writes=[("hflat", c)] + [("hT", ch, t) for ch in range(KC) for t in range(8)])
        self.dma("sp", kr_sb, S["krT"][:, :], reads=[("krT", x) for x in range(NK // P)], writes=["kr_sb", "knT"] + [("aT", ch, t) for ch in range(KC) for t in range(8)])
        for h in range(16):
            wi = self.load_w(Wukv, 0, 2, h * 256, 256)
            wt = self.WT[wi]
            for kb in range(NK // 512):
                psi = self.nps()
                ps = self.PS[psi]
                for kc in range(2):
                    pg.op("pe", lambda e, ps=ps, kc=kc, kb=kb, wt=wt: e.matmul(ps[:, :], lhsT=wt[:, kc, 0:P], rhs=ckv_sb[kc][:, kb * 512:(kb + 1) * 512],
                                                                      start=(kc == 0), stop=(kc == 1)),
                          [("wt", wi, 0), ("hflat", kc)], [("ps", psi)] if kc == 0 else [])
                self._mark_ps(psi)
                pg.op("act", lambda e, ps=ps, kb=kb: e.copy(out=knT[:, kb * 512:(kb + 1) * 512], in_=ps[:, :]), [("ps", psi)], ["knT"])
                psi = self.nps()
                ps = self.PS[psi]
                for k4 in range(4):
                    kt = kb * 4 + k4
                    for kc in range(2):
                        pg.op("pe", lambda e, ps=ps, kc=kc, kt=kt, k4=k4, wt=wt: e.matmul(
                            ps[:, k4 * P:(k4 + 1) * P], lhsT=ckv_sb[kc][:, kt * P:(kt + 1) * P], rhs=wt[:, kc, P:256],
                            start=(kc == 0), stop=(kc == 1)), [("wt", wi, 1), ("hflat", kc)],
                            [("ps", psi)] if (kc == 0 and k4 == 0) else [])
                self._mark_ps(psi)
                pg.op("dve", lambda e, ps=ps, kb=kb: e.tensor_copy(out=vsb[:, kb * 512:(kb + 1) * 512], in_=ps[:, :]), [("ps", psi)], ["vsb"])
            for s_ in range(4):
                self.mla_attn(h, s_ * 256, 256, [s_ * 2, s_ * 2 + 1], knT, kr_sb, vsb, scale)
            for qb in range(8):
                self.mla_attn(h, TP + qb * 512, 512, list(range(8, 8 + 36)), knT, kr_sb, vsb, scale)
        if self.stop == "att":
            return
        for blk in range(NBLK):
            self.out_proj_and_ffn(li, I["mla_wo"], blk, 0 if blk == 0 else 1)

    def mla_kv_tile(self, kf, b, col0, rope, coff=0):
        pg, S = self.pg, self.S
        xn = self.XN[b]
        pg.op("act", lambda e: e.copy(out=xn[:, 0:320], in_=kf[:, 0:320]), [("kvf", b)], [("xn", b)])
        pi = self._pti
        self._pti = (pi + 1) % 2
        pt = self.PT[pi]
        for c in range(2):
            pg.op("pe", lambda e, c=c: e.transpose(out=pt[:, c * P:(c + 1) * P], in_=xn[:, c * P:(c + 1) * P],
                                                   identity=self.ident[:, :]), [("xn", b), "ident"], [("pt", pi)])
        pg.op("pe", lambda e: e.transpose(out=pt[0:64, 256:384], in_=xn[:, 256:320], identity=self.ident[:, :]),
              [("xn", b), "ident"], [("pt", pi)])
        i = b
        tb_ = self.tmpb[i]
        pg.op("act", lambda e: e.copy(out=tb_[:, 0:256], in_=pt[:, 0:256]), [("pt", pi)], [("tmpb", i)])
        for c in range(2):
            self.dma("sp", S["ckvT"][c, :, col0:col0 + P], tb_[:, c * P:(c + 1) * P], reads=[("tmpb", i)],
                     writes=[("ckvT", col0 // P)])
        e2 = self.ET[b]
        if rope:
            pg.op("act", lambda e: e.copy(out=e2[0:64, 0:P], in_=pt[0:64, 256:384]), [("pt", pi)], [("et", b)])
            p2i = self.nps()
            p2 = self.PS[p2i]
            pg.op("pe", lambda e: e.matmul(p2[0:64, 0:P], lhsT=self.rot64[:, :], rhs=e2[0:64, 0:P], start=True, stop=True),
                  [("et", b), "rot64"], [("ps", p2i)])
            tf = self.tmpf[b]
            pg.op("dve", lambda e: e.tensor_tensor(out=tf[0:64, 0:P], in0=p2[0:64, 0:P], in1=self.sinT[0:64, coff:coff + P],
                                                   op=ALU.mult), [("ps", p2i), "sinT"], [("tmpf", b)])
            pg.op("pool", lambda e: e.tensor_tensor(out=self.mtmp[0:64, 0:P], in0=e2[0:64, 0:P], in1=self.cosT[0:64, coff:coff + P],
                                                    op=ALU.mult), [("et", b), "cosT"], ["mtmp"])
            pg.op("dve", lambda e: e.tensor_tensor(out=e2[0:64, 0:P], in0=tf[0:64, 0:P], in1=self.mtmp[0:64, 0:P], op=ALU.add),
                  [("tmpf", b), "mtmp"], [("et", b)])
        else:
            pg.op("act", lambda e: e.copy(out=e2[0:64, 0:P], in_=pt[0:64, 256:384]), [("pt", pi)], [("et", b)])
        self.dma("sp", S["krT"][:, col0:col0 + P], e2[0:64, 0:P], reads=[("et", b)], writes=[("krT", col0 // P)])

    def mla_attn(self, h, tok0, N, tiles, knT, kr_sb, vsb, scale):
        pg, S = self.pg, self.S
        qi = self._attpar
        xq = self.XN[qi]
        self.dma("sp", xq[:, 0:N], S["qnT"][h, :, tok0:tok0 + N], reads=[("qnT", h, tok0 // 512)], writes=[("xn", qi)])
        self.dma("sp", xq[0:64, 512:512 + N], S["qrT"][h, :, tok0:tok0 + N], reads=[("qrT", h, tok0 // 512)], writes=[("xn", qi)])
        par = self._attpar
        self._attpar = 1 - par
        oi, di = 2 * par, 2 * par + 1
        ops_, dps = self.PS[oi], self.PS[di]
        pend = None
        first = True

        def od(kt, ei, last):
            nonlocal first
            et = self.ET[ei]
            pg.op("pe", lambda e, f=first: e.matmul(ops_[:, 0:N], lhsT=vsb[:, kt * P:(kt + 1) * P], rhs=et[:, 0:N],
                                                   start=f, stop=last), [("et", ei), "vsb"], [("ps", oi)] if first else [])
            pg.op("pe", lambda e, f=first: e.matmul(dps[:, 0:N], lhsT=self.ones[:, :], rhs=et[:, 0:N],
                                                   start=f, stop=last), [("et", ei), "ones"], [("ps", di)] if first else [])
            first = False

        for kt in tiles:
            si = 4 + self._atts
            self._atts = 1 - self._atts
            sps = self.PS[si]
            pg.op("pe", lambda e, sps=sps, kt=kt: e.matmul(sps[:, 0:N], lhsT=knT[:, kt * P:(kt + 1) * P], rhs=xq[:, 0:N],
                                                          start=True, stop=False), ["knT", ("xn", qi)], [("ps", si)])
            pg.op("pe", lambda e, sps=sps, kt=kt: e.matmul(sps[:, 0:N], lhsT=kr_sb[:, kt * P:(kt + 1) * P], rhs=xq[0:64, 512:512 + N],
                                                          start=False, stop=True), ["kr_sb", ("xn", qi)], [])
            self._mark_ps(si)
            if pend is not None:
                od(pend[0], pend[1], False)
            ei = self._et3
            self._et3 = (ei + 1) % 3
            et = self.ET[ei]
            pg.op("act", lambda e, sps=sps, et=et: e.activation(out=et[:, 0:N], in_=sps[:, 0:N], func=AF.Exp, scale=scale),
                  [("ps", si)], [("et", ei)])
            pend = (kt, ei)
        od(pend[0], pend[1], True)
        self.pg.res[("ps", oi)] = [self.pg.ops["pe"][-2], []]
        self.pg.res[("ps", di)] = [self.pg.ops["pe"][-1], []]
        b = par
        tf = self.tmpf[b]
        pg.op("dve", lambda e: e.reciprocal(out=tf[:, 0:N], in_=dps[:, 0:N]), [("ps", di)], [("tmpf", b)])
        ob = self.tmpb[b]
        pg.op("dve", lambda e: e.tensor_tensor(out=ob[:, 0:N], in0=ops_[:, 0:N], in1=tf[:, 0:N], op=ALU.mult),
              [("ps", oi), ("tmpf", b)], [("tmpb", b)])
        self.dma("sp", S["attT"][h * P:(h + 1) * P, tok0:tok0 + N], ob[:, 0:N], reads=[("tmpb", b)],
                 writes=[("attT", h, tok0 // 1024)])

    def gla_layer(self, li):
        pg, I, O, S = self.pg, self.I, self.O, self.S
        Wg = I["gla_win"]
        for blk in range(NBLK):
            def _blk(blk=blk):
                cond = 0 if blk == 0 else 1
                self.norm_block(blk, 0, cond)
                tok0 = blk * 1024
                for w in range(4):
                    dstT = S["gqT"] if w < 2 else S["gkT"]
                    nm = "gqT" if w < 2 else "gkT"

                    def evacT(ps, psi, oc, tb, w=w, dstT=dstT, nm=nm):
                        i = self._eti
                        self._eti = (i + 1) % 2
                        tb_ = self.tmpb[i]
                        self.cp("act", tb_[:, :], ps[:, :], [("ps", psi)], [("tmpb", i)])
                        row = ((w % 2) * 4 + oc) * P
                        t0 = tok0 + tb * 512
                        self.dma("sp", dstT[row:row + P, t0:t0 + 512], tb_[:, :], reads=[("tmpb", i)],
                                 writes=[(nm, row // P, t0 // 512)])
                    self.lin_fm(self.hT, "hT", Wg, 0, KC, w * 512, 512, evacT)
                for w in range(10):
                    if w < 2:
                        dst, nm, c0, cc = S["gk"], "gk", 1024 + w * 512, w * 512
                    elif w < 6:
                        dst, nm, c0, cc = S["gv"], "gv", 2048 + (w - 2) * 512, (w - 2) * 512
                    else:
                        dst, nm, c0, cc = S["gr"], "gr", 4096 + (w - 6) * 512, (w - 6) * 512

                    def evacM(ps, psi, t, w=w, dst=dst, nm=nm, cc=cc):
                        tt = blk * 8 + t
                        i = self._eti
                        self._eti = (i + 1) % 2
                        tb_ = self.tmpb[i]
                        if w >= 6:
                            self.act(tb_[:, :], ps[:, :], AF.Silu, [("ps", psi)], [("tmpb", i)])
                        else:
                            self.cp("act", tb_[:, :], ps[:, :], [("ps", psi)], [("tmpb", i)])
                        self.dma("sp", dst[tt * P:(tt + 1) * P, cc:cc + 512], tb_[:, :], reads=[("tmpb", i)],
                                 writes=[(nm, tt, cc // 512)])
                    self.lin_tm(self.hT, "hT", Wg, 0, KC, c0, 512, evacM)
                for dr in range(2):
                    wi = self.load_w(I["gla_wa1"][dr], 0, KC, 0, 16)
                    wt = self.WT[wi]
                    uT = self.XN[dr]
                    for tb in range(2):
                        psi = self.nps()
                        ps = self.PS[psi]
                        for kc in range(KC):
                            self.mm(ps[0:16, :], wt[:, kc, 0:16], self.hT[:, kc, tb * 512:(tb + 1) * 512], kc == 0, kc == KC - 1,
                                    [("wt", wi, 0)] + [("hT", kc, t) for t in range(tb * 4, tb * 4 + 4)],
                                    [("ps", psi)] if kc == 0 else [])
                        self._mark_ps(psi)
                        self.cp("act", uT[0:16, tb * 512:(tb + 1) * 512], ps[0:16, :], [("ps", psi)], [("xn", dr)])
                    for cb in range(2):
                        self.dma("sp", self.tmpf[0][0:16, :], I["gla_wa2"][dr, :, cb * 512:(cb + 1) * 512], writes=[("tmpf", 0)])
                        self.dma("sp", self.tmpf[1][0:1, :], I["gla_ba"][dr, :, cb * 512:(cb + 1) * 512], writes=[("tmpf", 1)])
                        self.cp("dve", self.w2b[0:16, :], self.tmpf[0][0:16, :], [("tmpf", 0)], ["w2b"])
                        self.cp("dve", self.bab[0:1, :], self.tmpf[1][0:1, :], [("tmpf", 1)], ["bab"])
                        for t in range(8):
                            tt = blk * 8 + t
                            psi = self.nps()
                            ps = self.PS[psi]
                            self.mm(ps[:, :], uT[0:16, t * P:(t + 1) * P], self.w2b[0:16, :], True, False,
                                    [("xn", dr), "w2b"], [("ps", psi)])
                            self.mm(ps[:, :], self.ones[0:1, :], self.bab[0:1, :], False, True, ["ones", "bab"], [])
                            self._mark_ps(psi)
                            b = t % 2
                            kf = self.kvf[b]
                            self.act(kf[:, :], ps[:, :], AF.Exp, [("ps", psi)], [("kvf", b)], scale=-1.0)
                            self.ts("dve", kf[:, :], kf[:, :], 1.0, None, ALU.add, None, [("kvf", b)], [("kvf", b)])
                            self.act(kf[:, :], kf[:, :], AF.Ln, [("kvf", b)], [("kvf", b)])
                            self.ts("pool", kf[:, :], kf[:, :], 1.0 / 16.0, None, ALU.mult, None, [("kvf", b)], [("kvf", b)])
                            self.dma("sp", S["gsc"][dr, tt * P:(tt + 1) * P, cb * 512:(cb + 1) * 512], kf[:, :],
                                     reads=[("kvf", b)], writes=[("gsc", dr, tt, cb)])
            if _blk():
                return
        if self.stop == "p1":
            return
        ab = self.aT[:, :, :].rearrange("p a b -> p (a b)")
        off = [0]

        def ab16(n, rows=P):
            o = off[0]
            off[0] += n
            return ab[0:rows, o:o + n]

        def af32(n, rows=P):
            o = off[0]
            off[0] += 2 * n
            return ab[:, o:o + 2 * n].bitcast(F32)[0:rows, :]

        Sf = af32(1024).rearrange("p (c n) -> p c n", n=512)
        Sb = ab16(1024).rearrange("p (c n) -> p c n", n=512)
        gch = [af32(256, 64) for _ in range(2)]
        vch = [ab16(512, 64) for _ in range(2)]
        kch = [ab16(256, 64) for _ in range(2)]
        E1 = af32(128).rearrange("p (c n) -> p c n", n=64)
        E2 = af32(128).rearrange("p (c n) -> p c n", n=64)
        Es = af32(256, 64)
        qt = ab16(128).rearrange("p (c n) -> p c n", n=64)
        kt_ = ab16(128).rearrange("p (c n) -> p c n", n=64)
        kd = ab16(256, 64)
        AT = ab16(64, 64)
        ofc = af32(512, 64)
        rch = ab16(512, 64)
        on_ = af32(512, 64)
        obf = ab16(512, 64)
        Minc = [af32(64, 64) for _ in range(2)]
        Msuf = [af32(64, 64) for _ in range(2)]
        tri = [ab16(64, 64) for _ in range(2)]
        gn_bc = af32(512, 64)
        tri32 = af32(64, 64)
        assert off[0] <= 16384
        allkeys = [("aT", ch, t) for ch in range(KC) for t in range(8)] + ["knT", "kr_sb"]
        for d_ in range(2):
            sfx = "f" if d_ == 0 else "b"
            self.dma("sp", Minc[d_][:, :], I["c_g_incl_" + sfx][0:64, 0:64], writes=[("Minc", d_)] + (allkeys if d_ == 0 else []))
            self.dma("sp", Msuf[d_][:, :], I["c_g_suf_" + sfx][0:64, 0:64], writes=[("Msuf", d_)])
            self.dma("sp", tri32[:, :], I["c_g_tri_" + sfx][:, :], writes=["tri32"])
            self.cp("dve", tri[d_][:, :], tri32[:, :], ["tri32"], [("tri", d_)])
        self.dma("sp", self.grow[0:1, :], I["gla_norm"][:, :], writes=["grow"])
        psj = self.nps()
        self.mm(self.PS[psj][0:64, :], self.sel[0:1, 0, 0:64], self.grow[0:1, :], True, True, ["grow", "sel"], [("ps", psj)])
        self.cp("act", gn_bc[:, :], self.PS[psj][0:64, :], [("ps", psj)], ["gn_bc"])
        hfl = self.hT[:, :, :].rearrange("p a b -> p (a b)")
        groups = [(s_ * 256, 256, s_) for s_ in range(4)] + [(TP, TS, None)]
        for (g0, L, pseq) in groups:
            nch = L // 64
            for h in range(4):
                qT = hfl[:, 0:2 * L].rearrange("p (c n) -> p c n", n=L)
                kT = hfl[:, 8192:8192 + 2 * L].rearrange("p (c n) -> p c n", n=L)
                for c in range(2):
                    for q4 in range(max(1, L // 1024)):
                        w_ = min(L, 1024)
                        self.dma("sp", qT[:, c, q4 * w_:(q4 + 1) * w_], S["gqT"][(h * 2 + c) * P:(h * 2 + c + 1) * P, g0 + q4 * w_:g0 + (q4 + 1) * w_],
                                 reads=[("gqT", h * 2 + c, x) for x in range(T // 512)],
                                 writes=["gq_sb"] + [("hT", ch, t) for ch in range(KC) for t in range(8)] + [("hflat", 0), ("hflat", 1)])
                        self.dma("sp", kT[:, c, q4 * w_:(q4 + 1) * w_], S["gkT"][(h * 2 + c) * P:(h * 2 + c + 1) * P, g0 + q4 * w_:g0 + (q4 + 1) * w_],
                                 reads=[("gkT", h * 2 + c, x) for x in range(T // 512)], writes=["gk_sb"])
                for d_ in range(2):
                    if pseq is None:
                        src = I["gsf"] if d_ == 0 else I["gsb"]
                        for c in range(2):
                            self.dma("sp", Sf[:, c, :], src[h, c * P:(c + 1) * P, :], writes=["Sf"])
                    else:
                        pg.op("pool", lambda e: e.memset(Sf[:, :, :], 0.0), [], ["Sf"])
                    self.cp("act", Sb[:, :, :], Sf[:, :, :], ["Sf"], ["Sb"])
                    order = range(nch) if d_ == 0 else range(nch - 1, -1, -1)
                    for ci in order:
                        tk = g0 + ci * 64
                        tile_, half = tk // P, (tk % P) // 64
                        lc = ci * 64
                        bq = ci % 2
                        g_, v_, k_ = gch[bq], vch[bq], kch[bq]
                        self.dma("sp", g_[:, :], S["gsc"][d_, tk:tk + 64, h * 256:(h + 1) * 256],
                                 reads=[("gsc", d_, tile_, (h * 256) // 512)], writes=[("gch", bq)])
                        self.dma("sp", v_[:, :], S["gv"][tk:tk + 64, h * 512:(h + 1) * 512], reads=[("gv", tile_, h)], writes=[("vch", bq)])
                        self.dma("sp", k_[:, :], S["gk"][tk:tk + 64, h * 256:(h + 1) * 256], reads=[("gk", tile_, (h * 256) // 512)],
                                 writes=[("kch", bq)])
                        for c in range(2):
                            psi = self.nps()
                            ps = self.PS[psi]
                            self.mm(ps[:, 0:64], g_[:, c * P:(c + 1) * P], Minc[d_][:, :], True, True, [("gch", bq), ("Minc", d_)], [("ps", psi)])
                            self.act(E1[:, c, :], ps[:, 0:64], AF.Exp, [("ps", psi)], [("E1", c)], scale=-1.0)
                            self.act(E2[:, c, :], ps[:, 0:64], AF.Exp, [("ps", psi)], [("E2", c)])
                            self.stt("dve", qt[:, c, :], qT[:, c, lc:lc + 64], 1.0 / 16.0, E1[:, c, :], ALU.mult, ALU.mult,
                                     ["gq_sb", ("E1", c)], [("qt", c)])
                            self.tt("pool", kt_[:, c, :], kT[:, c, lc:lc + 64], E2[:, c, :], ALU.mult, ["gk_sb", ("E2", c)], [("kt", c)])
                        psi = self.nps()
                        ps = self.PS[psi]
                        self.mm(ps[0:64, 0:256], Msuf[d_][:, :], g_[:, :], True, True, [("gch", bq), ("Msuf", d_)], [("ps", psi)])
                        self.act(Es[:, :], ps[0:64, 0:256], AF.Exp, [("ps", psi)], ["Es"], scale=-1.0)
                        self.tt("dve", kd[:, :], k_[:, :], Es[:, :], ALU.mult, [("kch", bq), "Es"], ["kd"])
                        psi = self.nps()
                        ps = self.PS[psi]
                        for c in range(2):
                            self.mm(ps[0:64, 0:64], kt_[:, c, :], qt[:, c, :], c == 0, c == 1, [("kt", c), ("qt", c)],
                                    [("ps", psi)] if c == 0 else [])
                        self._mark_ps(psi)
                        self.tt("dve", AT[:, :], ps[0:64, 0:64], tri[d_][:, :], ALU.mult, [("ps", psi), ("tri", d_)], ["AT"])
                        pso = self.nps()
                        po = self.PS[pso]
                        self.mm(po[0:64, :], AT[:, :], v_[:, :], True, False, ["AT", ("vch", bq)], [("ps", pso)])
                        for c in range(2):
                            self.mm(po[0:64, :], qt[:, c, :], Sb[:, c, :], False, c == 1, [("qt", c), "Sb"], [])
                        self._mark_ps(pso)
                        if d_ == 0:
                            self.cp("act", ofc[:, :], po[0:64, :], [("ps", pso)], ["ofc"])
                            self.dma("sp", S["gof"][tk:tk + 64, h * 512:(h + 1) * 512], ofc[:, :], reads=["ofc"],
                                     writes=[("gof", tk // 64, h)])
                        else:
                            self.dma("sp", ofc[:, :], S["gof"][tk:tk + 64, h * 512:(h + 1) * 512], reads=[("gof", tk // 64, h)], writes=["ofc"])
                            self.dma("sp", rch[:, :], S["gr"][tk:tk + 64, h * 512:(h + 1) * 512], reads=[("gr", tile_, h)], writes=["rch"])
                            self.tt("dve", ofc[:, :], ofc[:, :], po[0:64, :], ALU.add, ["ofc", ("ps", pso)], ["ofc"])
                            self.act(on_[:, :], ofc[:, :], AF.Square, ["ofc"], ["on", ("stat", 0)], accum_out=self.stat[0:64, 0:1])
                            self.rstd_from_ss(0, 512)
                            self.stt("dve", on_[:, :], ofc[:, :], self.stat[0:64, 4:5], gn_bc[:, :], ALU.mult, ALU.mult,
                                     ["ofc", ("stat3", 0), "gn_bc"], ["on"])
                            self.tt("pool", obf[:, :], on_[:, :], rch[:, :], ALU.mult, ["on", "rch"], ["obf"])
                            pi = self._pti
                            self._pti = (pi + 1) % 2
                            pt = self.PT[pi]
                            for c4 in range(4):
                                self.tr(pt[:, c4 * 64:(c4 + 1) * 64], obf[:, c4 * P:(c4 + 1) * P], ["obf"], [("pt", pi)])
                            i = self._eti
                            self._eti = (i + 1) % 2
                            tb_ = self.tmpb[i]
                            self.cp("act", tb_[:, 0:256], pt[:, 0:256], [("pt", pi)], [("tmpb", i)])
                            dst = S["attT"][h * 512:(h + 1) * 512, tk:tk + 64].rearrange("(c d) i -> d c i", d=P)
                            self.dma("sp", dst, tb_[:, 0:256].rearrange("d (c i) -> d c i", i=64), reads=[("tmpb", i)],
                                     writes=[("attT", h * 4 + c4, tk // 1024) for c4 in range(4)])
                        last = 63 if d_ == 0 else 0
                        for c in range(2):
                            psi = self.nps()
                            ps = self.PS[psi]
                            self.mm(ps[:, :], kd[:, c * P:(c + 1) * P], v_[:, :], True, True, ["kd", ("vch", bq)], [("ps", psi)])
                            self.stt("dve", Sf[:, c, :], Sf[:, c, :], E1[:, c, last:last + 1], ps[:, :], ALU.mult, ALU.add,
                                     ["Sf", ("E1", c), ("ps", psi)], ["Sf"])
                        self.cp("act", Sb[:, :, :], Sf[:, :, :], ["Sf"], ["Sb"])
                    if pseq is not None:
                        dsto = O["ngf"] if d_ == 0 else O["ngb"]
                        for c in range(2):
                            self.dma("sp", dsto[pseq, h, c * P:(c + 1) * P, :], Sf[:, c, :], reads=["Sf"], writes=[("ngo", d_, pseq, h, c)])
        if self.stop == "att":
            return
        for blk in range(NBLK):
            self.out_proj_and_ffn(li, I["gla_wo"], blk, 0 if blk == 0 else 1)

    def epilogue(self):
        pg, I, O, S = self.pg, self.I, self.O, self.S
        if self.stop is not None:
            for ch in range(KC):
                self.dma("sp", O["dbg"][ch * P:(ch + 1) * P, :], S["attT"][ch * P:(ch + 1) * P, :],
                         reads=[("attT", ch, b) for b in range(NBLK)], writes=[("dbg", ch)])
        if self.final_norm:
            for q4 in range(4):
                self.dma("sp", self.grow[0:1, :], I["fnorm"][:, q4 * 512:(q4 + 1) * 512], writes=["grow"])
                psj = self.nps()
                pj = self.PS[psj]
                pg.op("pe", lambda e, pj=pj: e.matmul(pj[:, :], lhsT=self.sel[0:1, 0, :], rhs=self.grow[0:1, :],
                                                      start=True, stop=True), ["grow", "sel"], [("ps", psj)])
                pg.op("act", lambda e, pj=pj, q4=q4: e.copy(out=self.gbc[:, q4 * 512:(q4 + 1) * 512], in_=pj[:, :]),
                      [("ps", psj)], ["gbc"])
        for tt in range(T // P):
            b = tt % 2
            xt = self.XT[b]
            self.dma("sp", xt[:, :], S["xres"][tt * P:(tt + 1) * P, :], reads=[("xres", tt)], writes=[("xt", b)])
            if self.final_norm:
                xn = self.XN[b]
                pg.op("act", lambda e, xt=xt, xn=xn, b=b: e.activation(out=xn[:, :], in_=xt[:, :], func=AF.Square,
                                                                        accum_out=self.stat[:, b:b + 1]),
                      [("xt", b)], [("xn", b), ("stat", b)])
                pg.op("dve", lambda e, b=b: e.tensor_scalar(out=self.stat[:, 2 + b:3 + b], in0=self.stat[:, b:b + 1],
                                                            scalar1=1.0 / D, scalar2=EPS, op0=ALU.mult, op1=ALU.add),
                      [("stat", b)], [("stat2", b)])
                pg.op("act", lambda e, b=b: e.sqrt(out=self.stat[:, 6 + b:7 + b], in_=self.stat[:, 2 + b:3 + b]),
                      [("stat2", b)], [("stat2s", b)])
                pg.op("dve", lambda e, b=b: e.reciprocal(out=self.stat[:, 4 + b:5 + b], in_=self.stat[:, 6 + b:7 + b]),
                      [("stat2s", b)], [("stat3", b)])
                pg.op("dve", lambda e, xt=xt, b=b: e.scalar_tensor_tensor(out=xt[:, :], in0=xt[:, :], scalar=self.stat[:, 4 + b:5 + b],
                                                                          in1=self.gbc[:, :], op0=ALU.mult, op1=ALU.mult),
                      [("xt", b), ("stat3", b), "gbc"], [("xt", b)])
            if tt < 8:
                dst = O["yp"][tt * P:(tt + 1) * P, :]
            else:
                dst = O["ys"][(tt - 8) * P:(tt - 7) * P, :]
            self.dma("sp", dst, xt[:, :], reads=[("xt", b)], writes=[("yout", tt)])


def _core_inputs(inp, core, consts):
    f = lambda a: np.ascontiguousarray(np.asarray(a, dtype=np.float32))
    b = core // 4
    m = {}
    m["xp"] = f(inp["x_prompt"][core * 4:(core + 1) * 4].reshape(TP, D))
    m["xs"] = f(inp["x_sample"][b])
    cv = np.stack([np.asarray(inp["c_ctx"]), np.asarray(inp["c"][b])], axis=0)
    m["cvT"] = f(cv.reshape(2, KC, P).transpose(2, 1, 0))
    m["cwk"] = f(np.asarray(inp["cache_win_k"][b]).reshape(2, PAST, 512))
    m["cwv"] = f(np.asarray(inp["cache_win_v"][b]).reshape(2, PAST, 512))
    m["cmc"] = f(inp["cache_mla_ckv"][b, 0])
    m["cmr"] = f(inp["cache_mla_krope"][b, 0])
    m["gsf"] = f(inp["state_gla_fwd"][b, 0])
    m["gsb"] = f(inp["state_gla_bwd"][b, 0])
    return m


def _shared_inputs(inp, consts, layers=(0, 1, 2, 3)):
    f = lambda a: np.ascontiguousarray(np.asarray(a, dtype=np.float32))
    z = np.zeros((1, 1), np.float32)
    need = lambda k: any(LAYER_KIND[l] == k for l in layers)
    needj = lambda j: any(LAYER_KIND[l] == 0 and LAYER_J[l] == j for l in layers)
    m = {}
    for i in range(4):
        m["ada_w%d" % i] = f(inp["ada_w"][i]) if i in layers else z
        m["ffn_w1_%d" % i] = f(inp["ffn_w1"][i]) if i in layers else z
        m["ffn_w2_%d" % i] = f(inp["ffn_w2"][i]) if i in layers else z
    for j in range(2):
        m["win_wqkv%d" % j] = f(inp["win_wqkv"][j]) if needj(j) else z
        m["win_wo%d" % j] = f(inp["win_wo"][j]) if needj(j) else z
    m["ada_bT"] = f(np.asarray(inp["ada_b"]).reshape(4, 96, P).transpose(0, 2, 1))
    m["ngT"] = f(np.asarray(inp["norm_g"]).reshape(4, 2, KC, P).transpose(0, 1, 3, 2))
    m["win_sink"] = f(np.asarray(inp["win_sink"]).reshape(2, 1, 16))
    m["mla_wdown"] = f(inp["mla_wdown"][0]) if need(1) else z
    m["mla_qnT"] = f(np.asarray(inp["mla_q_norm"][0]).reshape(4, P).T)
    m["mla_wuq"] = f(inp["mla_wuq"][0]) if need(1) else z
    m["mla_kvn"] = f(np.asarray(inp["mla_kv_norm"][0]).reshape(1, 256))
    m["mla_wukv"] = f(inp["mla_wukv"][0]) if need(1) else z
    m["mla_wo"] = f(inp["mla_wo"][0]) if need(1) else z
    m["gla_win"] = f(inp["gla_win"][0]) if need(2) else z
    m["gla_wa1"] = f(inp["gla_wa1"][0])
    m["gla_wa2"] = f(inp["gla_wa2"][0])
    m["gla_ba"] = f(np.asarray(inp["gla_ba"][0]).reshape(2, 1, 1024))
    m["gla_norm"] = f(np.asarray(inp["gla_norm"][0]).reshape(1, 512))
    m["gla_wo"] = f(inp["gla_wo"][0]) if need(2) else z
    m["fnorm"] = f(np.asarray(inp["final_norm"]).reshape(1, D))
    for k, v in consts.items():
        m["c_" + k] = f(v)
    return m


def kernel(**inputs):
    consts = _consts()
    shared = _shared_inputs(inputs, consts)
    nc = Builder([0, 1, 2, 3], True).build()
    in_maps = []
    for core in range(8):
        m = dict(shared)
        m.update(_core_inputs(inputs, core, consts))
        in_maps.append(m)
    res = run_bass_kernel_spmd(nc, in_maps, core_ids=list(range(8)))
    R = res.results
    yp = np.concatenate([R[c]["yp"].reshape(4, 256, D) for c in range(8)], axis=0)
    ys = np.stack([R[0]["ys"], R[4]["ys"]], axis=0)
    nwk = np.concatenate([R[c]["nwk"].reshape(2, 4, 256, 4, 128).transpose(1, 0, 2, 3, 4) for c in range(8)], axis=0)
    nwv = np.concatenate([R[c]["nwv"].reshape(2, 4, 256, 4, 128).transpose(1, 0, 2, 3, 4) for c in range(8)], axis=0)
    nmc = np.concatenate([R[c]["nmc"].reshape(4, 1, 256, 256) for c in range(8)], axis=0)
    nmr = np.concatenate([R[c]["nmr"].reshape(4, 1, 256, 64) for c in range(8)], axis=0)
    ngf = np.concatenate([R[c]["ngf"].reshape(4, 1, 4, 256, 512) for c in range(8)], axis=0)
    ngb = np.concatenate([R[c]["ngb"].reshape(4, 1, 4, 256, 512) for c in range(8)], axis=0)
    return tuple(np.ascontiguousarray(a, dtype=np.float32) for a in (yp, ys, nwk, nwv, nmc, nmr, ngf, ngb))
```

```python
import numpy as np
import concourse.bass as bass
import concourse.mybir as mybir
from concourse.bass_utils import run_bass_kernel_spmd

F32 = mybir.dt.float32
BF16 = mybir.dt.bfloat16
AF = mybir.ActivationFunctionType
ALU = mybir.AluOpType

P = 128
D = 2048
KC = 16
TP = 1024
TS = 4096
T = TP + TS
NBLK = T // 1024
EPS = 1e-6
FFN_H = 8192
PAST = 512


class _Op:
    __slots__ = ("eng", "fn", "deps", "idx", "signal", "semval", "dma", "dslot", "dval")


class Prog:
    ENGS = ("pe", "act", "dve", "pool", "sp")
    NDS = 8

    def __init__(self, nc):
        self.nc = nc
        self.ops = {e: [] for e in self.ENGS}
        self.res = {}
        self.ndma = {e: 0 for e in self.ENGS}
        self.dma_last = {}

    maxops = None
    count = 0
    log = None
    pre_hook = None
    hook_off = False

    def op(self, eng, fn, reads=(), writes=(), dma=False):
        if self.pre_hook is not None and not self.hook_off:
            self.pre_hook()
        self.count += 1
        if self.log is not None and self.count % 50 == 0:
            import traceback
            fr = traceback.extract_stack(limit=4)
            self.log.append((self.count, eng, [(f.name, f.lineno) for f in fr[:-1]]))
        if self.maxops is not None and self.count > self.maxops:
            d = _Op()
            d.eng = eng
            d.dma = False
            d.signal = False
            d.deps = []
            return d
        o = _Op()
        o.eng = eng
        o.fn = fn
        o.dma = dma
        o.signal = False
        o.semval = 0
        lst = self.ops[eng]
        o.idx = len(lst)
        deps = []
        for k in reads:
            st = self.res.get(k)
            if st is not None and st[0] is not None:
                deps.append(st[0])
        for k in writes:
            st = self.res.get(k)
            if st is not None:
                if st[0] is not None:
                    deps.append(st[0])
                lastrd = {}
                for r in st[1]:
                    if r.dma:
                        deps.append(r)
                    else:
                        lastrd[r.eng] = r
                deps.extend(lastrd.values())
        if dma:
            n = self.ndma[eng]
            slot = n % self.NDS
            self.ndma[eng] = n + 1
            prev = self.dma_last.get((eng, slot))
            if prev is not None:
                deps.append(prev)
            o.dslot = slot
            o.dval = 16 * (n // self.NDS + 1)
            self.dma_last[(eng, slot)] = o
        seen = set()
        fdeps = []
        for d in deps:
            if id(d) in seen:
                continue
            seen.add(id(d))
            if (not d.dma) and d.eng == "pe" and eng == "pe" and not dma:
                continue
            if not d.dma:
                d.signal = True
            fdeps.append(d)
        o.deps = fdeps
        lst.append(o)
        for k in reads:
            st = self.res.get(k)
            if st is None:
                self.res[k] = [None, [o]]
            else:
                st[1].append(o)
        for k in writes:
            self.res[k] = [o, []]
        return o

    def emit(self):
        nc = self.nc
        for e in self.ENGS:
            cnt = 0
            for o in self.ops[e]:
                if o.dma:
                    continue
                if o.signal:
                    cnt += 1
                    o.semval = cnt
        finals = []
        for e in self.ENGS:
            last = None
            for o in reversed(self.ops[e]):
                if not o.dma:
                    last = o
                    break
            if last is not None and e != "sp":
                if not last.signal:
                    last.signal = True
                    cnt = 0
                    for o in self.ops[e]:
                        if o.dma:
                            continue
                        if o.signal:
                            cnt += 1
                            o.semval = cnt
                finals.append(last)
        finals.extend(self.dma_last.values())
        import contextlib
        with contextlib.ExitStack() as st:
            sems = {}
            for e in self.ENGS:
                sems[("eng", e)] = st.enter_context(nc.semaphore("s_" + e))
                for s in range(self.NDS):
                    sems[("dma", e, s)] = st.enter_context(nc.semaphore("d_%s_%d" % (e, s)))
            block = st.enter_context(nc.Block())
            engobj = {"pe": block.tensor, "act": block.scalar, "dve": block.vector,
                      "pool": block.gpsimd, "sp": block.sync}

            def mk(e):
                lst = self.ops[e]

                def body(eng):
                    seen = {}

                    def wait(d):
                        if d.dma:
                            key, val = ("dma", d.eng, d.dslot), d.dval
                        else:
                            key, val = ("eng", d.eng), d.semval
                        if seen.get(key, 0) >= val:
                            return
                        seen[key] = val
                        eng.wait_ge(sems[key], val)

                    for o in lst:
                        for d in o.deps:
                            wait(d)
                        ins = o.fn(eng)
                        if o.dma:
                            ins.then_inc(sems[("dma", o.eng, o.dslot)], 16)
                        elif o.signal:
                            ins.then_inc(sems[("eng", o.eng)], 1)
                    if e == "sp":
                        for d in finals:
                            wait(d)
                return body

            for e in self.ENGS:
                engobj[e](mk(e))


def _rope_tables(dim):
    r = dim // 2
    half = r // 2
    inv = (10000.0 ** (-np.arange(half, dtype=np.float32) / half)).astype(np.float32)
    tok = np.arange(TS)
    row = (tok // 64).astype(np.float32)
    col = (tok % 64).astype(np.float32)
    cos = np.zeros((dim, TS), np.float32)
    sin = np.zeros((dim, TS), np.float32)
    for f in range(dim):
        pos = row if f < r else col
        i = (f % r) % half
        ang = pos * inv[i]
        cos[f] = np.cos(ang)
        sin[f] = np.sin(ang)
    R = np.zeros((dim, dim), np.float32)
    for f in range(dim):
        base = 0 if f < r else r
        j = f - base
        if j < half:
            R[f, base + j + half] = -1.0
        else:
            R[f, base + j - half] = 1.0
    return cos, sin, np.ascontiguousarray(R.T)


def _consts():
    c = {}
    cos, sin, rt = _rope_tables(128)
    c["cos128"], c["sin128"], c["rot128"] = cos, sin, rt
    cos, sin, rt = _rope_tables(64)
    c["cos64"], c["sin64"], c["rot64"] = cos, sin, rt
    c["ident"] = np.eye(128, dtype=np.float32)
    k = np.arange(128)[:, None]
    q = np.arange(128)[None, :]
    c["mprev"] = np.tile((k >= q).astype(np.float32), (1, 4))
    c["mnext"] = np.tile((k <= q).astype(np.float32), (1, 4))
    sel = np.zeros((2, 2, 128), np.float32)
    sel[0, 0, :] = 1.0
    sel[1, 1, :] = 1.0
    c["sel"] = sel
    j = np.arange(128)[:, None]
    i = np.arange(128)[None, :]
    same = (j // 64) == (i // 64)
    c["g_incl_f"] = (same & (j <= i)).astype(np.float32)
    c["g_suf_f"] = (same & (j > i)).astype(np.float32)
    c["g_incl_b"] = (same & (j >= i)).astype(np.float32)
    c["g_suf_b"] = (same & (j < i)).astype(np.float32)
    c["g_tri_f"] = (j[:64, :] <= i[:, :64]).astype(np.float32)
    c["g_tri_b"] = (j[:64, :] >= i[:, :64]).astype(np.float32)
    return c


CONST_SHAPES = {
    "cos128": (128, TS), "sin128": (128, TS), "rot128": (128, 128),
    "cos64": (64, TS), "sin64": (64, TS), "rot64": (64, 64),
    "ident": (128, 128), "mprev": (128, 512), "mnext": (128, 512), "sel": (2, 2, 128),
    "g_incl_f": (128, 128), "g_suf_f": (128, 128), "g_incl_b": (128, 128), "g_suf_b": (128, 128),
    "g_tri_f": (64, 64), "g_tri_b": (64, 64),
}

LAYER_KIND = [0, 1, 2, 0]
LAYER_J = [0, 0, 0, 1]


class Builder:
    def __init__(self, layers, final_norm=True, stop=None):
        self.stop = stop
        self.layers = layers
        self.final_norm = final_norm
        self.nc = bass.Bass("TRN2", target_bir_lowering=False)
        self.pg = Prog(self.nc)
        self.pg.pre_hook = self.flush
        self.uid = 0

    def din(self, name, shape, dt=F32):
        return self.nc.dram_tensor(name, list(shape), dt, kind="ExternalInput").ap()

    def dout(self, name, shape, dt=F32):
        return self.nc.dram_tensor(name, list(shape), dt, kind="ExternalOutput").ap()

    def dscr(self, name, shape, dt):
        return self.nc.dram_tensor(name, list(shape), dt).ap()

    def declare(self):
        I = {}
        I["xp"] = self.din("xp", (TP, D))
        I["xs"] = self.din("xs", (TS, D))
        I["cvT"] = self.din("cvT", (P, KC, 2))
        I["cwk"] = self.din("cwk", (2, PAST, 512))
        I["cwv"] = self.din("cwv", (2, PAST, 512))
        I["cmc"] = self.din("cmc", (PAST, 256))
        I["cmr"] = self.din("cmr", (PAST, 64))
        I["gsf"] = self.din("gsf", (4, 256, 512))
        I["gsb"] = self.din("gsb", (4, 256, 512))
        I["ada_w"] = [self.din("ada_w%d" % i, (D, 6 * D) if i in self.layers else (1, 1)) for i in range(4)]
        I["ada_bT"] = self.din("ada_bT", (4, P, 96))
        I["ngT"] = self.din("ngT", (4, 2, P, KC))
        need = lambda k: any(LAYER_KIND[l] == k for l in self.layers)
        needj = lambda j: any(LAYER_KIND[l] == 0 and LAYER_J[l] == j for l in self.layers)
        I["win_wqkv"] = [self.din("win_wqkv%d" % j, (D, 3072) if needj(j) else (1, 1)) for j in range(2)]
        I["win_sink"] = self.din("win_sink", (2, 1, 16))
        I["win_wo"] = [self.din("win_wo%d" % j, (D, D) if needj(j) else (1, 1)) for j in range(2)]
        I["mla_wdown"] = self.din("mla_wdown", (D, 832) if need(1) else (1, 1))
        I["mla_qnT"] = self.din("mla_qnT", (P, 4))
        I["mla_wuq"] = self.din("mla_wuq", (512, 3072) if need(1) else (1, 1))
        I["mla_kvn"] = self.din("mla_kvn", (1, 256))
        I["mla_wukv"] = self.din("mla_wukv", (256, 4096) if need(1) else (1, 1))
        I["mla_wo"] = self.din("mla_wo", (D, D) if need(1) else (1, 1))
        I["gla_win"] = self.din("gla_win", (D, 6144) if need(2) else (1, 1))
        I["gla_wa1"] = self.din("gla_wa1", (2, D, 16))
        I["gla_wa2"] = self.din("gla_wa2", (2, 16, 1024))
        I["gla_ba"] = self.din("gla_ba", (2, 1, 1024))
        I["gla_norm"] = self.din("gla_norm", (1, 512))
        I["gla_wo"] = self.din("gla_wo", (D, D) if need(2) else (1, 1))
        I["ffn_w1"] = [self.din("ffn_w1_%d" % i, (D, FFN_H) if i in self.layers else (1, 1)) for i in range(4)]
        I["ffn_w2"] = [self.din("ffn_w2_%d" % i, (FFN_H, D) if i in self.layers else (1, 1)) for i in range(4)]
        I["fnorm"] = self.din("fnorm", (1, D))
        for k, s in CONST_SHAPES.items():
            I["c_" + k] = self.din("c_" + k, s)
        self.I = I
        O = {}
        O["yp"] = self.dout("yp", (TP, D))
        O["ys"] = self.dout("ys", (TS, D))
        O["nwk"] = self.dout("nwk", (2, TP, 512))
        O["nwv"] = self.dout("nwv", (2, TP, 512))
        O["nmc"] = self.dout("nmc", (TP, 256))
        O["nmr"] = self.dout("nmr", (TP, 64))
        O["ngf"] = self.dout("ngf", (4, 4, 256, 512))
        O["ngb"] = self.dout("ngb", (4, 4, 256, 512))
        if self.stop is not None:
            O["dbg"] = self.dout("dbg", (D, T), BF16)
        self.O = O
        S = {}
        S["xres"] = self.dscr("xres", (T, D), F32)
        S["attT"] = self.dscr("attT", (D, T), BF16)
        S["qT"] = self.dscr("qT", (4, T // P, P, 512), BF16)
        S["kT"] = self.dscr("kT", (4, P, T), BF16)
        S["vv"] = self.dscr("vv", (T, 512), BF16)
        NK = T + PAST
        S["ckvT"] = self.dscr("ckvT", (2, P, NK), BF16)
        S["krT"] = self.dscr("krT", (64, NK), BF16)
        S["gqT"] = self.dscr("gqT", (1024, T), BF16)
        S["gkT"] = self.dscr("gkT", (1024, T), BF16)
        S["gk"] = self.dscr("gk", (T, 1024), BF16)
        S["gv"] = self.dscr("gv", (T, 2048), BF16)
        S["gr"] = self.dscr("gr", (T, 2048), BF16)
        S["gsc"] = self.dscr("gsc", (2, T, 1024), F32)
        S["gof"] = self.dscr("gof", (T, 2048), F32)
        S["qnT"] = self.dscr("qnT", (16, P, T), BF16)
        S["qrT"] = self.dscr("qrT", (16, 64, T), BF16)
        self.S = S
        self.dram_names = set()
        for dct in (self.I, self.O, self.S):
            for v in dct.values():
                for a in (v if isinstance(v, list) else [v]):
                    self.dram_names.add(a.name)

    def sb(self, name, shape, dt):
        return self._stack.enter_context(self.nc.sbuf_tensor(name, list(shape), dt))

    def psum(self, name, shape, dt):
        return self._stack.enter_context(self.nc.psum_tensor(name, list(shape), dt))

    def dma(self, q, out, in_, reads=(), writes=()):
        if self.STORE_Q_POOL and q == "sp" and in_.name not in self.dram_names:
            q = "pool"
        return self.pg.op(q, lambda e, o=out, i=in_: e.dma_start(out=o, in_=i), reads, writes, dma=True)

    def nps(self):
        i = self._psi
        self._psi = (i + 1) % len(self.PS)
        return i

    def mm(self, out, lhsT, rhs, start, stop, reads, writes=()):
        return self.pg.op("pe", lambda e: e.matmul(out, lhsT=lhsT, rhs=rhs, start=start, stop=stop), reads, writes)

    def tr(self, out, in_, reads, writes):
        return self.pg.op("pe", lambda e: e.transpose(out=out, in_=in_, identity=self.ident[0:in_.shape[0], 0:in_.shape[0]]),
                          list(reads) + ["ident"], writes)

    def act(self, out, in_, func, reads, writes, **kw):
        return self.pg.op("act", lambda e: e.activation(out=out, in_=in_, func=func, **kw), reads, writes)

    def cp(self, eng, out, in_, reads, writes):
        if eng == "act":
            return self.pg.op("act", lambda e: e.copy(out=out, in_=in_), reads, writes)
        return self.pg.op(eng, lambda e: e.tensor_copy(out=out, in_=in_), reads, writes)

    def tt(self, eng, out, a, b, op, reads, writes):
        return self.pg.op(eng, lambda e: e.tensor_tensor(out=out, in0=a, in1=b, op=op), reads, writes)

    def ts(self, eng, out, a, s1, s2, op0, op1, reads, writes):
        if s2 is None:
            return self.pg.op(eng, lambda e: e.tensor_scalar(out=out, in0=a, scalar1=s1, scalar2=None, op0=op0), reads, writes)
        return self.pg.op(eng, lambda e: e.tensor_scalar(out=out, in0=a, scalar1=s1, scalar2=s2, op0=op0, op1=op1), reads, writes)

    def stt(self, eng, out, a, sc, b, op0, op1, reads, writes):
        return self.pg.op(eng, lambda e: e.scalar_tensor_tensor(out=out, in0=a, scalar=sc, in1=b, op0=op0, op1=op1), reads, writes)

    def load_w(self, Wap, r0, nk, c0, ncols):
        pg = self.pg
        i = self._wti
        self._wti = (i + 1) % len(self.WT)
        wt = self.WT[i]
        gsz = max(1, min(nk, 2048 // ncols))
        ng = (nk + gsz - 1) // gsz
        self.wt_gsz[i] = gsz
        for g in range(ng):
            k0 = g * gsz
            kn = min(gsz, nk - k0)
            b = self._stgi
            self._stgi = (b + 1) % 2
            stg = self.STGF[b][:, 0:kn * ncols].rearrange("p (k n) -> p k n", n=ncols)
            src = Wap[r0 + k0 * P:r0 + (k0 + kn) * P, c0:c0 + ncols].rearrange("(kc p) n -> p kc n", p=P)
            self.dma("sp", stg, src, writes=[("stg", b)])
            ce = ("dve", "act")[self._casti % 2]
            self._casti += 1
            wkeys = [("wt", i, g)] + ([("wt", i, x) for x in range(1, 16)] if g == 0 else [])
            self.cp(ce, wt[:, k0:k0 + kn, 0:ncols], stg, [("stg", b)], wkeys)
        return i

    def build(self):
        import contextlib
        self.declare()
        nc, pg, I, O, S = self.nc, self.pg, self.I, self.O, self.S
        with contextlib.ExitStack() as stack:
            self._stack = stack
            self.hT = self.sb("hT", (P, KC, 1024), BF16)
            self.aT = self.sb("aT", (P, KC, 1024), BF16)
            self.big = self.sb("big", (P, KC, 1024), BF16)
            self.WT = [self.sb("wt%d" % i, (P, KC, 512), BF16) for i in range(2)]
            self._wti = 0
            self.XT = [self.sb("xt%d" % i, (P, D), F32) for i in range(2)]
            self.XN = [self.sb("xn%d" % i, (P, D), BF16) for i in range(2)]
            self.gbc = self.sb("gbc", (P, D), F32)
            bigf = self.big[:, :, :].rearrange("p a b -> p (a b)").bitcast(F32)
            self.STG = [bigf[:, 4096 + i * 2048:4096 + (i + 1) * 2048].rearrange("p (k n) -> p k n", n=P) for i in range(2)]
            self.STGF = [bigf[:, 4096 + i * 2048:4096 + (i + 1) * 2048] for i in range(2)]
            self.wt_gsz = [1, 1]
            self._stgi = 0
            self._casti = 0
            self.modT = self.sb("modT", (P, 96, 2), F32)
            self.adaB = self.sb("adaB", (P, 96), F32)
            self.AB = self.sb("AB", (P, 4, KC, 2), F32)
            self.ngt = self.sb("ngt", (P, 2, KC), F32)
            self.sc = self.sb("sc", (P, KC, 2), F32)
            self.grow = self.sb("grow", (2, 512), F32)
            self.sel = self.sb("sel", (2, 2, P), F32)
            self.ident_f = self.sb("ident_f", (P, P), F32)
            self.ident = self.sb("ident", (P, P), BF16)
            self.ones = self.sb("ones", (P, P), BF16)
            self.stat = self.sb("stat", (P, 16), F32)
            self.small = self.sb("small", (P, 64), F32)
            self.cosT = self.sb("cosT", (P, 1024), F32)
            self.sinT = self.sb("sinT", (P, 1024), F32)
            self.rot = self.sb("rot", (P, P), BF16)
            self.rotf = self.sb("rotf", (P, P), F32)
            self.mprev = self.sb("mprev", (P, 512), BF16)
            self.mnext = self.sb("mnext", (P, 512), BF16)
            self.mtmp = self.sb("mtmp", (P, 512), F32)
            self.kvn_bc = self.sb("kvn_bc", (P, 256), F32)
            self.rot64 = self.sb("rot64", (64, 64), BF16)
            self.w2b = self.sb("w2b", (16, 512), BF16)
            self.bab = self.sb("bab", (1, 512), BF16)
            self.ET = [self.sb("et%d" % i, (P, 512), BF16) for i in range(3)]
            self.tmpf = [self.sb("tmpf%d" % i, (P, 512), F32) for i in range(2)]
            self.tmpb = [self.sb("tmpb%d" % i, (P, 512), BF16) for i in range(2)]
            self.kvf = [self.sb("kvf%d" % i, (P, 512), F32) for i in range(2)]
            self.PS = [self.psum("ps%d" % i, (P, 512), F32) for i in range(6)]
            self.PT = [self.psum("pt%d" % i, (P, 1024), BF16) for i in range(2)]
            self._psi = 0
            self._pti = 0
            self._eti = 0

            self.prologue()
            for li in self.layers:
                self.layer(li)
            self.epilogue()
            self.flush()
            pg.emit()
        return nc

    def prologue(self):
        pg, I, S = self.pg, self.I, self.S
        self.dma("sp", S["xres"][0:TP, :], I["xp"][:, :], writes=[("xres", b) for b in range(0, 8)])
        for b in range(4):
            self.dma("sp", S["xres"][TP + b * 1024:TP + (b + 1) * 1024, :], I["xs"][b * 1024:(b + 1) * 1024, :],
                     writes=[("xres", t) for t in range(8 + b * 8, 16 + b * 8)])
        self.dma("sp", self.ident_f[:, :], I["c_ident"][:, :], writes=["ident_f"])
        self.dma("sp", self.rotf[:, :], I["c_rot128"][:, :], writes=["rotf"])
        self.dma("sp", self.sel[:, :, :], I["c_sel"][:, :, :], writes=["sel"])
        self.dma("sp", self.sc[:, :, :], I["cvT"][:, :, :], writes=["sc_raw"])
        self.dma("sp", self.tmpf[0][:, :], I["c_mprev"][:, :], writes=[("tmpf", 0)])
        self.dma("sp", self.tmpf[1][:, :], I["c_mnext"][:, :], writes=[("tmpf", 1)])
        pg.op("dve", lambda e: e.tensor_copy(out=self.ident[:, :], in_=self.ident_f[:, :]), ["ident_f"], ["ident"])
        pg.op("dve", lambda e: e.tensor_copy(out=self.rot[:, :], in_=self.rotf[:, :]), ["rotf"], ["rot"])
        pg.op("dve", lambda e: e.memset(self.ones[:, :], 1.0), [], ["ones"])
        pg.op("dve", lambda e: e.tensor_copy(out=self.mprev[:, :], in_=self.tmpf[0][:, :]), [("tmpf", 0)], ["mprev"])
        pg.op("dve", lambda e: e.tensor_copy(out=self.mnext[:, :], in_=self.tmpf[1][:, :]), [("tmpf", 1)], ["mnext"])
        pg.op("act", lambda e: e.activation(out=self.sc[:, :, :], in_=self.sc[:, :, :], func=AF.Silu),
              ["sc_raw"], ["sc_raw", "sc"])

    def modulation(self, li):
        pg, I = self.pg, self.I
        self.dma("sp", self.adaB[:, :], I["ada_bT"][li], writes=["adaB"])
        self.dma("sp", self.ngt[:, :, :], I["ngT"][li].rearrange("n p k -> p n k"), writes=["ngt"])
        W = I["ada_w"][li]
        psi = self.nps()
        ps = self.PS[psi]
        for j in range(96):
            b = self._stgi
            self._stgi = (b + 1) % 2
            wf = self.STG[b]
            self.dma("sp", wf[:, :, :], W[:, j * P:(j + 1) * P].rearrange("(kc p) n -> p kc n", p=P),
                     writes=[("stg", b)])
            for kc in range(KC):
                pg.op("pe", lambda e, wf=wf, kc=kc, j=j: e.matmul(
                    ps[:, 2 * j:2 * j + 2], lhsT=wf[:, kc, :], rhs=self.sc[:, kc, :],
                    start=(kc == 0), stop=(kc == KC - 1)), [("stg", b), "sc"],
                    [("ps", psi)] if (j == 0 and kc == 0) else [])
        self._mark_ps(psi)
        for c in range(2):
            pg.op("dve", lambda e, c=c: e.tensor_tensor(
                out=self.modT[:, :, c], in0=ps[:, 0:192].rearrange("p (j c) -> p j c", c=2)[:, :, c],
                in1=self.adaB[:, :], op=ALU.add), [("ps", psi), "adaB"], ["modT"])
        for n in range(2):
            sh, scl = (0, 1) if n == 0 else (3, 4)
            for c in range(2):
                pg.op("dve", lambda e, n=n, c=c, scl=scl: e.scalar_tensor_tensor(
                    out=self.AB[:, 2 * n, :, c], in0=self.modT[:, scl * 16:(scl + 1) * 16, c], scalar=1.0,
                    in1=self.ngt[:, n, :], op0=ALU.add, op1=ALU.mult), ["modT", "ngt"], ["AB"])
                pg.op("dve", lambda e, n=n, c=c, sh=sh: e.tensor_copy(
                    out=self.AB[:, 2 * n + 1, :, c], in_=self.modT[:, sh * 16:(sh + 1) * 16, c]), ["modT"], ["AB"])

    def load_gate(self, gi, cond):
        pg = self.pg
        seg = (2, 5)[gi]
        for q4 in range(4):
            psj = self.nps()
            pj = self.PS[psj]
            for cc in range(4):
                ch = q4 * 4 + cc
                pg.op("pe", lambda e, pj=pj, cc=cc, ch=ch: e.matmul(
                    pj[0:2, cc * P:(cc + 1) * P], lhsT=self.modT[:, seg * 16 + ch, :], rhs=self.ident_f[:, :],
                    start=True, stop=True), ["modT", "ident_f"], [("ps", psj)] if cc == 0 else [])
            self._mark_ps(psj)
            pg.op("dve", lambda e, pj=pj: e.tensor_copy(out=self.grow[:, :], in_=pj[0:2, :]), [("ps", psj)], ["grow"])
            psk = self.nps()
            pk = self.PS[psk]
            pg.op("pe", lambda e, pk=pk: e.matmul(pk[:, :], lhsT=self.sel[:, cond, :], rhs=self.grow[:, :],
                                                  start=True, stop=True), ["grow", "sel"], [("ps", psk)])
            pg.op("act", lambda e, pk=pk, q4=q4: e.copy(out=self.gbc[:, q4 * 512:(q4 + 1) * 512], in_=pk[:, :]),
                  [("ps", psk)], ["gbc"])

    def norm_block(self, blk, n, cond, src=None):
        pg, S = self.pg, self.S
        for t in range(8):
            tt = blk * 8 + t
            b = tt % 2
            xt, xn = self.XT[b], self.XN[b]
            if src is None:
                self.dma("sp", xt[:, :], S["xres"][tt * P:(tt + 1) * P, :], reads=[("xres", tt)], writes=[("xt", b)])
            else:
                src(t, xt, b)
            st = self.stat
            pg.op("act", lambda e, xt=xt, xn=xn, b=b: e.activation(
                out=xn[:, :], in_=xt[:, :], func=AF.Square, accum_out=self.stat[:, b:b + 1]),
                [("xt", b)], [("xn", b), ("stat", b)])
            pg.op("dve", lambda e, b=b: e.tensor_scalar(
                out=self.stat[:, 2 + b:3 + b], in0=self.stat[:, b:b + 1], scalar1=1.0 / D, scalar2=EPS,
                op0=ALU.mult, op1=ALU.add), [("stat", b)], [("stat2", b)])
            pg.op("act", lambda e, b=b: e.sqrt(out=self.stat[:, 6 + b:7 + b], in_=self.stat[:, 2 + b:3 + b]),
                  [("stat2", b)], [("stat2s", b)])
            pg.op("dve", lambda e, b=b: e.reciprocal(out=self.stat[:, 4 + b:5 + b], in_=self.stat[:, 6 + b:7 + b]),
                  [("stat2s", b)], [("stat3", b)])
            pg.op("dve", lambda e, xt=xt, xn=xn, b=b: e.tensor_scalar(
                out=xn[:, :], in0=xt[:, :], scalar1=self.stat[:, 4 + b:5 + b], scalar2=None, op0=ALU.mult),
                [("xt", b), ("stat3", b)], [("xn", b)])
            for c4 in range(4):
                pi = self._pti
                self._pti = (pi + 1) % 2
                pt = self.PT[pi]
                for cc in range(4):
                    ch = c4 * 4 + cc
                    pg.op("pe", lambda e, pt=pt, xn=xn, cc=cc, ch=ch: e.transpose(
                        out=pt[:, cc * P:(cc + 1) * P], in_=xn[:, ch * P:(ch + 1) * P], identity=self.ident[:, :]),
                        [("xn", b), "ident"], [("pt", pi)])
                for cc in range(4):
                    ch = c4 * 4 + cc
                    pg.op("act", lambda e, pt=pt, cc=cc, ch=ch, t=t: e.activation(
                        out=self.hT[:, ch, t * P:(t + 1) * P], in_=pt[:, cc * P:(cc + 1) * P], func=AF.Identity,
                        scale=self.AB[:, 2 * n, ch, cond:cond + 1], bias=self.AB[:, 2 * n + 1, ch, cond:cond + 1]),
                        [("pt", pi), "AB"], [("hT", ch, t)])

    _pending = None
    STORE_Q_POOL = False

    def flush(self):
        p = self._pending
        if p is not None:
            self._pending = None
            p()

    def _defer(self, W, r0, nk, c0, ncols, body):
        self.pg.hook_off = True
        try:
            wi = self.load_w(W, r0, nk, c0, ncols)
        finally:
            self.pg.hook_off = False
        self.flush()
        self._pending = lambda: body(wi)

    def lin_fm(self, src, srckey, W, r0, nk, c0, ncols, evac):
        self._defer(W, r0, nk, c0, ncols, lambda wi: self._lin_fm_body(src, srckey, nk, ncols, evac, wi))

    def _lin_fm_body(self, src, srckey, nk, ncols, evac, wi):
        pg = self.pg
        wt = self.WT[wi]
        gsz = max(1, min(nk, 2048 // ncols))
        for oc in range(ncols // P):
            for tb in range(2):
                psi = self.nps()
                ps = self.PS[psi]
                for kc in range(nk):
                    rd = [("wt", wi, kc // gsz)] + [(srckey, kc, t) for t in range(tb * 4, tb * 4 + 4)]
                    pg.op("pe", lambda e, ps=ps, wt=wt, kc=kc, oc=oc, tb=tb: e.matmul(
                        ps[:, :], lhsT=wt[:, kc, oc * P:(oc + 1) * P], rhs=src[:, kc, tb * 512:(tb + 1) * 512],
                        start=(kc == 0), stop=(kc == nk - 1)), rd, [("ps", psi)] if kc == 0 else [])
                self._mark_ps(psi)
                evac(ps, psi, oc, tb)

    def _mark_ps(self, psi):
        lst = self.pg.ops["pe"]
        self.pg.res[("ps", psi)] = [lst[-1], []]

    def lin_tm(self, src, srckey, W, r0, nk, c0, ncols, evac, tiles=range(8)):
        self._defer(W, r0, nk, c0, ncols, lambda wi: self._lin_tm_body(src, srckey, nk, ncols, evac, tiles, wi))

    def _lin_tm_body(self, src, srckey, nk, ncols, evac, tiles, wi):
        pg = self.pg
        wt = self.WT[wi]
        gsz = max(1, min(nk, 2048 // ncols))
        for t in tiles:
            psi = self.nps()
            ps = self.PS[psi]
            for kc in range(nk):
                rd = [("wt", wi, kc // gsz), (srckey, kc, t)]
                pg.op("pe", lambda e, ps=ps, wt=wt, kc=kc, t=t: e.matmul(
                    ps[:, 0:ncols], lhsT=src[:, kc, t * P:(t + 1) * P], rhs=wt[:, kc, 0:ncols],
                    start=(kc == 0), stop=(kc == nk - 1)), rd, [("ps", psi)] if kc == 0 else [])
            self._mark_ps(psi)
            evac(ps, psi, t)

    def resid_cols(self, xt, b, ps, psi, cb):
        pg = self.pg
        i = cb % 2
        tf = self.tmpf[i]
        pg.op("dve", lambda e, tf=tf, ps=ps, cb=cb: e.tensor_tensor(
            out=tf[:, :], in0=ps[:, :], in1=self.gbc[:, cb * 512:(cb + 1) * 512], op=ALU.mult),
            [("ps", psi), "gbc"], [("tmpf", i)])
        pg.op("pool", lambda e, tf=tf, xt=xt, cb=cb: e.tensor_tensor(
            out=xt[:, cb * 512:(cb + 1) * 512], in0=xt[:, cb * 512:(cb + 1) * 512], in1=tf[:, :], op=ALU.add),
            [("tmpf", i), ("xt", b)], [("xt", b)])

    def out_proj_and_ffn(self, li, Wo, blk, cond):
        pg, I, S = self.pg, self.I, self.S
        for ch in range(KC):
            self.dma("sp", self.aT[:, ch, :], S["attT"][ch * P:(ch + 1) * P, blk * 1024:(blk + 1) * 1024],
                     reads=[("attT", ch, blk)], writes=[("aT", ch, t) for t in range(8)] + ["knT", "kr_sb", "Sf", "Sb", "AT", "kd", "obf", "ofc", "on", ("qt", 0), ("qt", 1), ("kt", 0), ("kt", 1)])
        self.load_gate(0, cond)
        for cb in range(4):
            def evac(ps, psi, t, cb=cb):
                tt = blk * 8 + t
                b = t % 2
                kf = self.kvf[b]
                self.dma("sp", kf[:, :], S["xres"][tt * P:(tt + 1) * P, cb * 512:(cb + 1) * 512],
                         reads=[("xres", tt)], writes=[("kvf", b)])
                tf = self.tmpf[b]
                pg.op("dve", lambda e: e.tensor_tensor(out=tf[:, :], in0=ps[:, :], in1=self.gbc[:, cb * 512:(cb + 1) * 512],
                                                       op=ALU.mult), [("ps", psi), "gbc"], [("tmpf", b)])
                pg.op("pool", lambda e: e.tensor_tensor(out=kf[:, :], in0=kf[:, :], in1=tf[:, :], op=ALU.add),
                      [("tmpf", b), ("kvf", b)], [("kvf", b)])
                self.dma("sp", S["xres"][tt * P:(tt + 1) * P, cb * 512:(cb + 1) * 512], kf[:, :],
                         reads=[("kvf", b)], writes=[("xres", tt)])
            self.lin_tm(self.aT, "aT", Wo, 0, KC, cb * 512, 512, evac)
        self.norm_block(blk, 1, cond)
        self.load_gate(1, cond)
        W1, W2 = I["ffn_w1"][li], I["ffn_w2"][li]
        for hq in range(4):
            for w4 in range(4):
                def evac1(ps, psi, oc, tb, w4=w4):
                    hc = w4 * 4 + oc
                    i = self._eti
                    self._eti = (i + 1) % 2
                    tb_ = self.tmpb[i]
                    pg.op("act", lambda e: e.activation(out=tb_[:, :], in_=ps[:, :], func=AF.Relu),
                          [("ps", psi)], [("tmpb", i)])
                    pg.op("pool", lambda e: e.tensor_tensor(out=self.aT[:, hc, tb * 512:(tb + 1) * 512], in0=tb_[:, :],
                                                            in1=tb_[:, :], op=ALU.mult),
                          [("tmpb", i)], [("aT", hc, t) for t in range(tb * 4, tb * 4 + 4)])
                self.lin_fm(self.hT, "hT", W1, 0, KC, hq * 2048 + w4 * 512, 512, evac1)
            for cb in range(4):
                def evac2(ps, psi, t, cb=cb):
                    tt = blk * 8 + t
                    b = t % 2
                    kf = self.kvf[b]
                    self.dma("sp", kf[:, :], S["xres"][tt * P:(tt + 1) * P, cb * 512:(cb + 1) * 512],
                             reads=[("xres", tt)], writes=[("kvf", b)])
                    tf = self.tmpf[b]
                    pg.op("dve", lambda e: e.tensor_tensor(out=tf[:, :], in0=ps[:, :],
                                                           in1=self.gbc[:, cb * 512:(cb + 1) * 512], op=ALU.mult),
                          [("ps", psi), "gbc"], [("tmpf", b)])
                    pg.op("pool", lambda e: e.tensor_tensor(out=kf[:, :], in0=kf[:, :], in1=tf[:, :], op=ALU.add),
                          [("tmpf", b), ("kvf", b)], [("kvf", b)])
                    self.dma("sp", S["xres"][tt * P:(tt + 1) * P, cb * 512:(cb + 1) * 512], kf[:, :],
                             reads=[("kvf", b)], writes=[("xres", tt)])
                self.lin_tm(self.aT, "aT", W2, hq * 2048, KC, cb * 512, 512, evac2)

    def layer(self, li):
        kind, j = LAYER_KIND[li], LAYER_J[li]
        if li != self.layers[-1]:
            saved, self.stop = self.stop, None
            try:
                self._layer(li, kind, j)
            finally:
                self.stop = saved
        else:
            self._layer(li, kind, j)

    def _layer(self, li, kind, j):
        self.modulation(li)
        if self.stop == "mod":
            return
        if kind == 0:
            self.win_layer(li, j)
        elif kind == 1:
            self.mla_layer(li)
        else:
            self.gla_layer(li)

    def win_layer(self, li, j):
        pg, I, O, S = self.pg, self.I, self.O, self.S
        Wq = I["win_wqkv"][j]
        scale = 128.0 ** -0.5
        self.dma("sp", self.small[0:1, 16:32], I["win_sink"][j], writes=["sinkrow"])
        psj = self.nps()
        pg.op("pe", lambda e: e.matmul(self.PS[psj][:, 0:16], lhsT=self.sel[0:1, 0, :],
                                      rhs=self.small[0:1, 16:32], start=True, stop=True), ["sinkrow", "sel"], [("ps", psj)])
        pg.op("act", lambda e: e.activation(out=self.small[:, 0:16], in_=self.PS[psj][:, 0:16], func=AF.Exp),
              [("ps", psj)], ["sinkexp"])
        for blk in range(NBLK):
            def _blk(blk=blk):
                cond = 0 if blk == 0 else 1
                sample = blk > 0
                self.norm_block(blk, 0, cond)
                if self.stop == "norm":
                    return True
                tok0 = blk * 1024
                if sample:
                    s0 = (blk - 1) * 1024
                    self.dma("sp", self.cosT[:, :], I["c_cos128"][:, s0:s0 + 1024], writes=["cosT"])
                    self.dma("sp", self.sinT[:, :], I["c_sin128"][:, s0:s0 + 1024], writes=["sinT"])
                for w in range(5):
                    def evac(ps, psi, oc, tb, w=w):
                        i = self._eti
                        self._eti = (i + 1) % 2
                        tb_ = self.tmpb[i]
                        t0 = tok0 + tb * 512
                        if not sample:
                            pg.op("act", lambda e: e.copy(out=tb_[:, :], in_=ps[:, :]), [("ps", psi)], [("tmpb", i)])
                            res = tb_
                            rkey = ("tmpb", i)
                        else:
                            pg.op("act", lambda e: e.copy(out=tb_[:, :], in_=ps[:, :]), [("ps", psi)], [("tmpb", i)])
                            ps2i = self.nps()
                            ps2 = self.PS[ps2i]
                            pg.op("pe", lambda e: e.matmul(ps2[:, :], lhsT=self.rot[:, :], rhs=tb_[:, :], start=True, stop=True),
                                  [("tmpb", i), "rot"], [("ps", ps2i)])
                            tf = self.tmpf[i]
                            pg.op("dve", lambda e: e.tensor_tensor(out=tf[:, :], in0=ps2[:, :], in1=self.sinT[:, tb * 512:(tb + 1) * 512], op=ALU.mult),
                                  [("ps", ps2i), "sinT"], [("tmpf", i)])
                            pg.op("pool", lambda e: e.tensor_tensor(out=self.mtmp[:, :], in0=tb_[:, :], in1=self.cosT[:, tb * 512:(tb + 1) * 512], op=ALU.mult),
                                  [("tmpb", i), "cosT"], ["mtmp"])
                            pg.op("dve", lambda e: e.tensor_tensor(out=tb_[:, :], in0=tf[:, :], in1=self.mtmp[:, :], op=ALU.add),
                                  [("tmpf", i), "mtmp"], [("tmpb", i)])
                            res = tb_
                            rkey = ("tmpb", i)
                        if w < 4:
                            h = w * 4 + oc
                            kvh, g = h // 4, h % 4
                            qb0 = t0 // P
                            dst = S["qT"][kvh, qb0:qb0 + 4, :, g * P:(g + 1) * P].rearrange("q d i -> d q i")
                            self.dma("sp", dst, res[:, :].rearrange("d (q i) -> d q i", i=P), reads=[rkey],
                                     writes=[("qT", kvh, qb0 + x) for x in range(4)])
                        else:
                            kvh = oc
                            self.dma("sp", S["kT"][kvh, :, t0:t0 + 512], res[:, :], reads=[rkey],
                                     writes=[("kT", kvh, t0 // 512)])
                    c0 = w * 512 if w < 4 else 2048
                    self.lin_fm(self.hT, "hT", Wq, 0, KC, c0, 512, evac)
                def evac_v(ps, psi, t):
                    tt = blk * 8 + t
                    i = self._eti
                    self._eti = (i + 1) % 2
                    tb_ = self.tmpb[i]
                    if sample:
                        pg.op("act", lambda e: e.copy(out=tb_[:, :], in_=ps[:, :]), [("ps", psi)], [("tmpb", i)])
                    else:
                        b = t % 2
                        kf = self.kvf[b]
                        pg.op("dve", lambda e: e.tensor_copy(out=kf[:, :], in_=ps[:, :]), [("ps", psi)], [("kvf", b)])
                        pg.op("act", lambda e: e.copy(out=tb_[:, :], in_=kf[:, :]), [("kvf", b)], [("tmpb", i)])
                        self.dma("sp", O["nwv"][j, tt * P:(tt + 1) * P, :], kf[:, :], reads=[("kvf", b)], writes=[("nwv", j, tt)])
                    self.dma("sp", S["vv"][tt * P:(tt + 1) * P, :], tb_[:, :], reads=[("tmpb", i)], writes=[("vv", tt)])
                self.lin_tm(self.hT, "hT", Wq, 0, KC, 2560, 512, evac_v)
                if not sample:
                    def evac_k(ps, psi, t):
                        tt = blk * 8 + t
                        b = t % 2
                        kf = self.kvf[b]
                        pg.op("dve", lambda e: e.tensor_copy(out=kf[:, :], in_=ps[:, :]), [("ps", psi)], [("kvf", b)])
                        self.dma("sp", O["nwk"][j, tt * P:(tt + 1) * P, :], kf[:, :], reads=[("kvf", b)], writes=[("nwk", j, tt)])
                    self.lin_tm(self.hT, "hT", Wq, 0, KC, 2048, 512, evac_k)
                if self.stop == "qkv":
                    return True
            if _blk():
                return
        if self.stop == "p1":
            return
        kT_sb = self.big
        kflat = self.big[:, :, :].rearrange("p a b -> p (a b)")
        vflat = self.hT[:, :, :].rearrange("p a b -> p (a b)")
        for kvh in range(4):
            self.dma("sp", kflat[:, 0:TP], S["kT"][kvh, :, 0:TP], reads=[("kT", kvh, 0), ("kT", kvh, 1)], writes=["kflat"])
            self.dma("sp", vflat[:, 0:8 * P].rearrange("p (t c) -> p t c", c=P),
                     S["vv"][0:TP, kvh * P:(kvh + 1) * P].rearrange("(t p) c -> p t c", p=P),
                     reads=[("vv", t) for t in range(8)], writes=["vflat"])
            for s in range(4):
                for qb in range(2):
                    qblk = s * 2 + qb
                    tiles = [(s * 2 + kt, None) for kt in range(2)]
                    self.attn_block(kvh, qblk, tiles, kflat, vflat, scale, True)
            if self.stop == "attp":
                return
            for ct in range(4):
                b = ct % 2
                kf = self.kvf[b]
                self.dma("sp", kf[:, 0:P], I["cwk"][j, ct * P:(ct + 1) * P, kvh * P:(kvh + 1) * P], writes=[("kvf", b)])
                self.dma("sp", kf[:, P:2 * P], I["cwv"][j, ct * P:(ct + 1) * P, kvh * P:(kvh + 1) * P], writes=[("kvf", b)])
                xn = self.XN[b]
                pg.op("dve", lambda e, xn=xn, kf=kf: e.tensor_copy(out=xn[:, 0:P], in_=kf[:, 0:P]), [("kvf", b)], [("xn", b)])
                pg.op("dve", lambda e, kf=kf, ct=ct: e.tensor_copy(out=vflat[:, ct * P:(ct + 1) * P], in_=kf[:, P:2 * P]),
                      [("kvf", b)], ["vflat"])
                pi = self._pti
                self._pti = (pi + 1) % 2
                pt = self.PT[pi]
                pg.op("pe", lambda e, pt=pt, xn=xn: e.transpose(out=pt[:, 0:P], in_=xn[:, 0:P], identity=self.ident[:, :]),
                      [("xn", b), "ident"], [("pt", pi)])
                pg.op("act", lambda e, pt=pt, ct=ct: e.copy(out=kflat[:, ct * P:(ct + 1) * P], in_=pt[:, 0:P]),
                      [("pt", pi)], ["kflat"])
            self.dma("sp", kflat[:, 512:512 + TS], S["kT"][kvh, :, TP:T], reads=[("kT", kvh, x) for x in range(2, 10)],
                     writes=["kflat"])
            for v4 in range(4):
                self.dma("sp", vflat[:, 512 + v4 * 1024:512 + (v4 + 1) * 1024].rearrange("p (t c) -> p t c", c=P),
                         S["vv"][TP + v4 * 1024:TP + (v4 + 1) * 1024, kvh * P:(kvh + 1) * P].rearrange("(t p) c -> p t c", p=P),
                         reads=[("vv", t) for t in range(8 + v4 * 8, 16 + v4 * 8)], writes=[("vflat", v4)])
            for n in range(32):
                tiles = [(ct, None) for ct in range(4)]
                if n > 0:
                    tiles.append((4 + n - 1, self.mprev))
                tiles.append((4 + n, None))
                if n < 31:
                    tiles.append((4 + n + 1, self.mnext))
                self.attn_block(kvh, 8 + n, tiles, kflat, vflat, scale, True)
        if self.stop == "att":
            return
        for blk in range(NBLK):
            self.out_proj_and_ffn(li, I["win_wo"][j], blk, 0 if blk == 0 else 1)

    def attn_block(self, kvh, qblk, tiles, kflat, vflat, scale, use_sink):
        pg, S = self.pg, self.S
        qi = qblk % 2
        qsb = self.XN[qi][:, 0:512]
        self.dma("sp", qsb, S["qT"][kvh, qblk], reads=[("qT", kvh, qblk)], writes=[("xn", qi)])
        par = self._attpar
        self._attpar = 1 - par
        oi, di = 2 * par, 2 * par + 1
        ops_, dps = self.PS[oi], self.PS[di]
        pend = None
        n = len(tiles)
        first = True

        def od(kt, ei, last):
            nonlocal first
            et = self.ET[ei]
            pg.op("pe", lambda e, f=first: e.matmul(ops_[:, :], lhsT=vflat[:, kt * P:(kt + 1) * P], rhs=et[:, :],
                                                   start=f, stop=last), [("et", ei), "vflat"] + [("vflat", q) for q in range(4)], [("ps", oi)] if first else [])
            pg.op("pe", lambda e, f=first: e.matmul(dps[:, :], lhsT=self.ones[:, :], rhs=et[:, :],
                                                   start=f, stop=last), [("et", ei), "ones"], [("ps", di)] if first else [])
            first = False

        for idx, (kt, mask) in enumerate(tiles):
            si = 4 + self._atts
            self._atts = 1 - self._atts
            sps = self.PS[si]
            pg.op("pe", lambda e, sps=sps, kt=kt: e.matmul(sps[:, :], lhsT=kflat[:, kt * P:(kt + 1) * P], rhs=qsb,
                                                          start=True, stop=True), ["kflat", ("xn", qi)], [("ps", si)])
            if pend is not None:
                od(pend[0], pend[1], False)
            ei = self._et3
            self._et3 = (ei + 1) % 3
            et = self.ET[ei]
            pg.op("act", lambda e, sps=sps, et=et: e.activation(out=et[:, :], in_=sps[:, :], func=AF.Exp, scale=scale),
                  [("ps", si)], [("et", ei)])
            if mask is not None:
                pg.op("pool", lambda e, et=et, mask=mask: e.tensor_tensor(out=et[:, :], in0=et[:, :], in1=mask[:, :], op=ALU.mult),
                      [("et", ei), "mprev", "mnext"], [("et", ei)])
            pend = (kt, ei)
        od(pend[0], pend[1], True)
        self.pg.res[("ps", oi)] = [self.pg.ops["pe"][-2], []]
        self.pg.res[("ps", di)] = [self.pg.ops["pe"][-1], []]
        b = qblk % 2
        tf = self.tmpf[b]
        for g in range(4):
            h = kvh * 4 + g
            if use_sink:
                pg.op("dve", lambda e, g=g, h=h: e.tensor_scalar(out=tf[:, g * P:(g + 1) * P], in0=dps[:, g * P:(g + 1) * P],
                                                                 scalar1=self.small[:, h:h + 1], scalar2=None, op0=ALU.add),
                      [("ps", di), "sinkexp"], [("tmpf", b)])
        pg.op("dve", lambda e: e.reciprocal(out=tf[:, :], in_=tf[:, :]), [("tmpf", b)], [("tmpf", b)])
        ob = self.tmpb[b]
        pg.op("dve", lambda e: e.tensor_tensor(out=ob[:, :], in0=ops_[:, :], in1=tf[:, :], op=ALU.mult),
              [("ps", oi), ("tmpf", b)], [("tmpb", b)])
        dst = S["attT"][kvh * 512:(kvh + 1) * 512, qblk * P:(qblk + 1) * P].rearrange("(g d) i -> d g i", d=P)
        self.dma("sp", dst, ob[:, :].rearrange("d (g i) -> d g i", i=P), reads=[("tmpb", b)],
                 writes=[("attT", kvh * 4 + g, qblk // 8) for g in range(4)])

    _et3 = 0
    _attpar = 0
    _atts = 0
    _cs_loaded = -1

    def rstd_from_ss(self, b, n):
        pg = self.pg
        pg.op("dve", lambda e: e.tensor_scalar(out=self.stat[:, 2 + b:3 + b], in0=self.stat[:, b:b + 1], scalar1=1.0 / n,
                                               scalar2=EPS, op0=ALU.mult, op1=ALU.add), [("stat", b)], [("stat2", b)])
        pg.op("act", lambda e: e.sqrt(out=self.stat[:, 6 + b:7 + b], in_=self.stat[:, 2 + b:3 + b]),
              [("stat2", b)], [("stat2s", b)])
        pg.op("dve", lambda e: e.reciprocal(out=self.stat[:, 4 + b:5 + b], in_=self.stat[:, 6 + b:7 + b]),
              [("stat2s", b)], [("stat3", b)])

    def rope_small(self, src_ps, psi, nrow, ncol, coff, dst_tb, dkey, i):
        pg = self.pg
        pg.op("act", lambda e: e.copy(out=dst_tb[0:nrow, 0:ncol], in_=src_ps[0:nrow, 0:ncol]), [("ps", psi)], [dkey])
        p2i = self.nps()
        p2 = self.PS[p2i]
        pg.op("pe", lambda e: e.matmul(p2[0:nrow, 0:ncol], lhsT=self.rot64[0:nrow, 0:nrow], rhs=dst_tb[0:nrow, 0:ncol],
                                      start=True, stop=True), [dkey, "rot64"], [("ps", p2i)])
        tf = self.tmpf[i]
        pg.op("dve", lambda e: e.tensor_tensor(out=tf[0:nrow, 0:ncol], in0=p2[0:nrow, 0:ncol],
                                               in1=self.sinT[0:nrow, coff:coff + ncol], op=ALU.mult),
              [("ps", p2i), "sinT"], [("tmpf", i)])
        pg.op("pool", lambda e: e.tensor_tensor(out=self.mtmp[0:nrow, 0:ncol], in0=dst_tb[0:nrow, 0:ncol],
                                                in1=self.cosT[0:nrow, coff:coff + ncol], op=ALU.mult),
              [dkey, "cosT"], ["mtmp"])
        pg.op("dve", lambda e: e.tensor_tensor(out=dst_tb[0:nrow, 0:ncol], in0=tf[0:nrow, 0:ncol],
                                               in1=self.mtmp[0:nrow, 0:ncol], op=ALU.add),
              [("tmpf", i), "mtmp"], [dkey])

    def mla_layer(self, li):
        pg, I, O, S = self.pg, self.I, self.O, self.S
        NK = T + PAST
        scale = 192.0 ** -0.5
        Wd, Wuq, Wukv = I["mla_wdown"], I["mla_wuq"], I["mla_wukv"]
        self.dma("sp", self.small[:, 32:36], I["mla_qnT"][:, :], writes=["qnT_c"])
        self.dma("sp", self.grow[0:1, 0:256], I["mla_kvn"][:, :], writes=["grow"])
        psj = self.nps()
        pg.op("pe", lambda e: e.matmul(self.PS[psj][:, 0:256], lhsT=self.sel[0:1, 0, :], rhs=self.grow[0:1, 0:256],
                                      start=True, stop=True), ["grow", "sel"], [("ps", psj)])
        pg.op("act", lambda e: e.copy(out=self.kvn_bc[:, :], in_=self.PS[psj][:, 0:256]), [("ps", psj)], ["kvn_bc"])
        self.dma("sp", self.tmpf[0][0:64, 0:64], I["c_rot64"][:, :], writes=[("tmpf", 0)])
        pg.op("dve", lambda e: e.tensor_copy(out=self.rot64[:, :], in_=self.tmpf[0][0:64, 0:64]), [("tmpf", 0)], ["rot64"])
        cqT = self.aT
        for ct in range(4):
            b = ct % 2
            kf = self.kvf[b]
            self.dma("sp", kf[:, 0:256], I["cmc"][ct * P:(ct + 1) * P, :], writes=[("kvf", b)])
            self.dma("sp", kf[:, 256:320], I["cmr"][ct * P:(ct + 1) * P, :], writes=[("kvf", b)])
            self.mla_kv_tile(kf, b, TP + ct * P, False)
        for blk in range(NBLK):
            def _blk(blk=blk):
                cond = 0 if blk == 0 else 1
                sample = blk > 0
                self.norm_block(blk, 0, cond)
                tok0 = blk * 1024
                if sample:
                    s0 = (blk - 1) * 1024
                    self.dma("sp", self.cosT[0:64, :], I["c_cos64"][:, s0:s0 + 1024], writes=["cosT"])
                    self.dma("sp", self.sinT[0:64, :], I["c_sin64"][:, s0:s0 + 1024], writes=["sinT"])
                def evac_cq(ps, psi, t):
                    b = t % 2
                    i = b
                    tb_ = self.tmpb[i]
                    pg.op("act", lambda e: e.activation(out=tb_[:, :], in_=ps[:, :], func=AF.Square,
                                                        accum_out=self.stat[:, b:b + 1]), [("ps", psi)], [("tmpb", i), ("stat", b)])
                    self.rstd_from_ss(b, 512)
                    pg.op("dve", lambda e: e.tensor_scalar(out=tb_[:, :], in0=ps[:, :], scalar1=self.stat[:, 4 + b:5 + b],
                                                           scalar2=None, op0=ALU.mult), [("ps", psi), ("stat3", b)], [("tmpb", i)])
                    pi = self._pti
                    self._pti = (pi + 1) % 2
                    pt = self.PT[pi]
                    for c in range(4):
                        pg.op("pe", lambda e, c=c: e.transpose(out=pt[:, c * P:(c + 1) * P], in_=tb_[:, c * P:(c + 1) * P],
                                                               identity=self.ident[:, :]), [("tmpb", i), "ident"], [("pt", pi)])
                    for c in range(4):
                        pg.op("act", lambda e, c=c: e.activation(out=cqT[:, c, t * P:(t + 1) * P], in_=pt[:, c * P:(c + 1) * P],
                                                                 func=AF.Identity, scale=self.small[:, 32 + c:33 + c]),
                              [("pt", pi), "qnT_c"], [("aT", c, t)])
                self.lin_tm(self.hT, "hT", Wd, 0, KC, 0, 512, evac_cq)
                def evac_kv(ps, psi, t):
                    tt = blk * 8 + t
                    b = t % 2
                    kf = self.kvf[b]
                    i = b
                    tb_ = self.tmpb[i]
                    pg.op("act", lambda e: e.activation(out=tb_[:, 0:256], in_=ps[:, 0:256], func=AF.Square,
                                                        accum_out=self.stat[:, 8 + b:9 + b]), [("ps", psi)], [("tmpb", i), ("stat", 8 + b)])
                    pg.op("dve", lambda e: e.tensor_scalar(out=self.stat[:, 10 + b:11 + b], in0=self.stat[:, 8 + b:9 + b],
                                                           scalar1=1.0 / 256, scalar2=EPS, op0=ALU.mult, op1=ALU.add),
                          [("stat", 8 + b)], [("stat2", 8 + b)])
                    pg.op("act", lambda e: e.sqrt(out=self.stat[:, 12 + b:13 + b], in_=self.stat[:, 10 + b:11 + b]),
                          [("stat2", 8 + b)], [("stat2s", 8 + b)])
                    pg.op("dve", lambda e: e.reciprocal(out=self.stat[:, 14 + b:15 + b], in_=self.stat[:, 12 + b:13 + b]),
                          [("stat2s", 8 + b)], [("stat3", 8 + b)])
                    pg.op("dve", lambda e: e.scalar_tensor_tensor(out=kf[:, 0:256], in0=ps[:, 0:256], scalar=self.stat[:, 14 + b:15 + b],
                                                                  in1=self.kvn_bc[:, :], op0=ALU.mult, op1=ALU.mult),
                          [("ps", psi), ("stat3", 8 + b), "kvn_bc"], [("kvf", b)])
                    pg.op("dve", lambda e: e.tensor_copy(out=kf[:, 256:320], in_=ps[:, 256:320]), [("ps", psi), ("stat3", 8 + b)], [("kvf", b)])
                    if not sample:
                        self.dma("sp", O["nmc"][tt * P:(tt + 1) * P, :], kf[:, 0:256], reads=[("kvf", b)], writes=[("nmc", tt)])
                        self.dma("sp", O["nmr"][tt * P:(tt + 1) * P, :], kf[:, 256:320], reads=[("kvf", b)], writes=[("nmr", tt)])
                    col0 = tt * P if not sample else TP + PAST + (tt - 8) * P
                    self.mla_kv_tile(kf, b, col0, sample, (t * P) if sample else 0)
                self.lin_tm(self.hT, "hT", Wd, 0, KC, 512, 320, evac_kv)
                for h in range(16):
                    wi = self.load_w(Wuq, 0, 4, h * 192, 192)
                    wt = self.WT[wi]
                    for tb in range(2):
                        t0 = tok0 + tb * 512
                        psi = self.nps()
                        ps = self.PS[psi]
                        for kc in range(4):
                            pg.op("pe", lambda e, ps=ps, kc=kc, tb=tb, wt=wt: e.matmul(ps[:, :], lhsT=wt[:, kc, 0:P], rhs=cqT[:, kc, tb * 512:(tb + 1) * 512],
                                                                              start=(kc == 0), stop=(kc == 3)),
                                  [("wt", wi, 0)] + [("aT", kc, t) for t in range(tb * 4, tb * 4 + 4)], [("ps", psi)] if kc == 0 else [])
                        self._mark_ps(psi)
                        i = self._eti
                        self._eti = (i + 1) % 2
                        tb_ = self.tmpb[i]
                        pg.op("act", lambda e, tb_=tb_, ps=ps: e.copy(out=tb_[:, :], in_=ps[:, :]), [("ps", psi)], [("tmpb", i)])
                        self.dma("sp", S["qnT"][h, :, t0:t0 + 512], tb_[:, :], reads=[("tmpb", i)], writes=[("qnT", h, t0 // 512)])
                        psi2 = self.nps()
                        ps2 = self.PS[psi2]
                        for kc in range(4):
                            pg.op("pe", lambda e, ps2=ps2, kc=kc, tb=tb, wt=wt: e.matmul(ps2[0:64, :], lhsT=wt[:, kc, P:192], rhs=cqT[:, kc, tb * 512:(tb + 1) * 512],
                                                                                start=(kc == 0), stop=(kc == 3)),
                                  [("wt", wi, 1)] + [("aT", kc, t) for t in range(tb * 4, tb * 4 + 4)], [("ps", psi2)] if kc == 0 else [])
                        self._mark_ps(psi2)
                        i2 = self._eti
                        self._eti = (i2 + 1) % 2
                        tb2 = self.tmpb[i2]
                        if sample:
                            self.rope_small(ps2, psi2, 64, 512, tb * 512, tb2, ("tmpb", i2), i2)
                        else:
                            pg.op("act", lambda e, tb2=tb2, ps2=ps2: e.copy(out=tb2[0:64, :], in_=ps2[0:64, :]), [("ps", psi2)], [("tmpb", i2)])
                        self.dma("sp", S["qrT"][h, :, t0:t0 + 512], tb2[0:64, :], reads=[("tmpb", i2)], writes=[("qrT", h, t0 // 512)])
            if _blk():
                return
        if self.stop == "p1":
            return
        hflat = self.hT[:, :, :].rearrange("p a b -> p (a b)")
        aflat = self.aT[:, :, :].rearrange("p a b -> p (a b)")
        bflat = self.big[:, :, :].rearrange("p a b -> p (a b)")
        ckv_sb = [hflat[:, 0:NK], hflat[:, NK:2 * NK]]
        kr_sb = aflat[0:64, 0:NK]
        knT = aflat[:, NK:2 * NK]
        vsb = bflat[:, 0:NK]
        for c in range(2):
            for q4 in range(4):
                w = NK // 4
                self.dma("sp", ckv_sb[c][:, q4 * w:(q4 + 1) * w], S["ckvT"][c, :, q4 * w:(q4 + 1) * w],
                         reads=[("ckvT", x) for x in range(NK // P)], writes=[("hflat", c)] + [("hT", ch, t) for ch in range(KC) for t in range(8)])
        self.dma("sp", kr_sb, S["krT"][:, :], reads=[("krT", x) for x in range(NK // P)], writes=["kr_sb", "knT"] + [("aT", ch, t) for ch in range(KC) for t in range(8)])
        for h in range(16):
            wi = self.load_w(Wukv, 0, 2, h * 256, 256)
            wt = self.WT[wi]
            for kb in range(NK // 512):
                psi = self.nps()
                ps = self.PS[psi]
                for kc in range(2):
                    pg.op("pe", lambda e, ps=ps, kc=kc, kb=kb, wt=wt: e.matmul(ps[:, :], lhsT=wt[:, kc, 0:P], rhs=ckv_sb[kc][:, kb * 512:(kb + 1) * 512],
                                                                      start=(kc == 0), stop=(kc == 1)),
                          [("wt", wi, 0), ("hflat", kc)], [("ps", psi)] if kc == 0 else [])
                self._mark_ps(psi)
                pg.op("act", lambda e, ps=ps, kb=kb: e.copy(out=knT[:, kb * 512:(kb + 1) * 512], in_=ps[:, :]), [("ps", psi)], ["knT"])
                psi = self.nps()
                ps = self.PS[psi]
                for k4 in range(4):
                    kt = kb * 4 + k4
                    for kc in range(2):
                        pg.op("pe", lambda e, ps=ps, kc=kc, kt=kt, k4=k4, wt=wt: e.matmul(
                            ps[:, k4 * P:(k4 + 1) * P], lhsT=ckv_sb[kc][:, kt * P:(kt + 1) * P], rhs=wt[:, kc, P:256],
                            start=(kc == 0), stop=(kc == 1)), [("wt", wi, 1), ("hflat", kc)],
                            [("ps", psi)] if (kc == 0 and k4 == 0) else [])
                self._mark_ps(psi)
                pg.op("dve", lambda e, ps=ps, kb=kb: e.tensor_copy(out=vsb[:, kb * 512:(kb + 1) * 512], in_=ps[:, :]), [("ps", psi)], ["vsb"])
            for s_ in range(4):
                self.mla_attn(h, s_ * 256, 256, [s_ * 2, s_ * 2 + 1], knT, kr_sb, vsb, scale)
            for qb in range(8):
                self.mla_attn(h, TP + qb * 512, 512, list(range(8, 8 + 36)), knT, kr_sb, vsb, scale)
        if self.stop == "att":
            return
        for blk in range(NBLK):
            self.out_proj_and_ffn(li, I["mla_wo"], blk, 0 if blk == 0 else 1)

    def mla_kv_tile(self, kf, b, col0, rope, coff=0):
        pg, S = self.pg, self.S
        xn = self.XN[b]
        pg.op("act", lambda e: e.copy(out=xn[:, 0:320], in_=kf[:, 0:320]), [("kvf", b)], [("xn", b)])
        pi = self._pti
        self._pti = (pi + 1) % 2
        pt = self.PT[pi]
        for c in range(2):
            pg.op("pe", lambda e, c=c: e.transpose(out=pt[:, c * P:(c + 1) * P], in_=xn[:, c * P:(c + 1) * P],
                                                   identity=self.ident[:, :]), [("xn", b), "ident"], [("pt", pi)])
        pg.op("pe", lambda e: e.transpose(out=pt[0:64, 256:384], in_=xn[:, 256:320], identity=self.ident[:, :]),
              [("xn", b), "ident"], [("pt", pi)])
        i = b
        tb_ = self.tmpb[i]
        pg.op("act", lambda e: e.copy(out=tb_[:, 0:256], in_=pt[:, 0:256]), [("pt", pi)], [("tmpb", i)])
        for c in range(2):
            self.dma("sp", S["ckvT"][c, :, col0:col0 + P], tb_[:, c * P:(c + 1) * P], reads=[("tmpb", i)],
                     writes=[("ckvT", col0 // P)])
        e2 = self.ET[b]
        if rope:
            pg.op("act", lambda e: e.copy(out=e2[0:64, 0:P], in_=pt[0:64, 256:384]), [("pt", pi)], [("et", b)])
            p2i = self.nps()
            p2 = self.PS[p2i]
            pg.op("pe", lambda e: e.matmul(p2[0:64, 0:P], lhsT=self.rot64[:, :], rhs=e2[0:64, 0:P], start=True, stop=True),
                  [("et", b), "rot64"], [("ps", p2i)])
            tf = self.tmpf[b]
            pg.op("dve", lambda e: e.tensor_tensor(out=tf[0:64, 0:P], in0=p2[0:64, 0:P], in1=self.sinT[0:64, coff:coff + P],
                                                   op=ALU.mult), [("ps", p2i), "sinT"], [("tmpf", b)])
            pg.op("pool", lambda e: e.tensor_tensor(out=self.mtmp[0:64, 0:P], in0=e2[0:64, 0:P], in1=self.cosT[0:64, coff:coff + P],
                                                    op=ALU.mult), [("et", b), "cosT"], ["mtmp"])
            pg.op("dve", lambda e: e.tensor_tensor(out=e2[0:64, 0:P], in0=tf[0:64, 0:P], in1=self.mtmp[0:64, 0:P], op=ALU.add),
                  [("tmpf", b), "mtmp"], [("et", b)])
        else:
            pg.op("act", lambda e: e.copy(out=e2[0:64, 0:P], in_=pt[0:64, 256:384]), [("pt", pi)], [("et", b)])
        self.dma("sp", S["krT"][:, col0:col0 + P], e2[0:64, 0:P], reads=[("et", b)], writes=[("krT", col0 // P)])

    def mla_attn(self, h, tok0, N, tiles, knT, kr_sb, vsb, scale):
        pg, S = self.pg, self.S
        qi = self._attpar
        xq = self.XN[qi]
        self.dma("sp", xq[:, 0:N], S["qnT"][h, :, tok0:tok0 + N], reads=[("qnT", h, tok0 // 512)], writes=[("xn", qi)])
        self.dma("sp", xq[0:64, 512:512 + N], S["qrT"][h, :, tok0:tok0 + N], reads=[("qrT", h, tok0 // 512)], writes=[("xn", qi)])
        par = self._attpar
        self._attpar = 1 - par
        oi, di = 2 * par, 2 * par + 1
        ops_, dps = self.PS[oi], self.PS[di]
        pend = None
        first = True

        def od(kt, ei, last):
            nonlocal first
            et = self.ET[ei]
            pg.op("pe", lambda e, f=first: e.matmul(ops_[:, 0:N], lhsT=vsb[:, kt * P:(kt + 1) * P], rhs=et[:, 0:N],
                                                   start=f, stop=last), [("et", ei), "vsb"], [("ps", oi)] if first else [])
            pg.op("pe", lambda e, f=first: e.matmul(dps[:, 0:N], lhsT=self.ones[:, :], rhs=et[:, 0:N],
                                                   start=f, stop=last), [("et", ei), "ones"], [("ps", di)] if first else [])
            first = False

        for kt in tiles:
            si = 4 + self._atts
            self._atts = 1 - self._atts
            sps = self.PS[si]
            pg.op("pe", lambda e, sps=sps, kt=kt: e.matmul(sps[:, 0:N], lhsT=knT[:, kt * P:(kt + 1) * P], rhs=xq[:, 0:N],
                                                          start=True, stop=False), ["knT", ("xn", qi)], [("ps", si)])
            pg.op("pe", lambda e, sps=sps, kt=kt: e.matmul(sps[:, 0:N], lhsT=kr_sb[:, kt * P:(kt + 1) * P], rhs=xq[0:64, 512:512 + N],
                                                          start=False, stop=True), ["kr_sb", ("xn", qi)], [])
            self._mark_ps(si)
            if pend is not None:
                od(pend[0], pend[1], False)
            ei = self._et3
            self._et3 = (ei + 1) % 3
            et = self.ET[ei]
            pg.op("act", lambda e, sps=sps, et=et: e.activation(out=et[:, 0:N], in_=sps[:, 0:N], func=AF.Exp, scale=scale),
                  [("ps", si)], [("et", ei)])
            pend = (kt, ei)
        od(pend[0], pend[1], True)
        self.pg.res[("ps", oi)] = [self.pg.ops["pe"][-2], []]
        self.pg.res[("ps", di)] = [self.pg.ops["pe"][-1], []]
        b = par
        tf = self.tmpf[b]
        pg.op("dve", lambda e: e.reciprocal(out=tf[:, 0:N], in_=dps[:, 0:N]), [("ps", di)], [("tmpf", b)])
        ob = self.tmpb[b]
        pg.op("dve", lambda e: e.tensor_tensor(out=ob[:, 0:N], in0=ops_[:, 0:N], in1=tf[:, 0:N], op=ALU.mult),
              [("ps", oi), ("tmpf", b)], [("tmpb", b)])
        self.dma("sp", S["attT"][h * P:(h + 1) * P, tok0:tok0 + N], ob[:, 0:N], reads=[("tmpb", b)],
                 writes=[("attT", h, tok0 // 1024)])

    def gla_layer(self, li):
        pg, I, O, S = self.pg, self.I, self.O, self.S
        Wg = I["gla_win"]
        for blk in range(NBLK):
            def _blk(blk=blk):
                cond = 0 if blk == 0 else 1
                self.norm_block(blk, 0, cond)
                tok0 = blk * 1024
                for w in range(4):
                    dstT = S["gqT"] if w < 2 else S["gkT"]
                    nm = "gqT" if w < 2 else "gkT"

                    def evacT(ps, psi, oc, tb, w=w, dstT=dstT, nm=nm):
                        i = self._eti
                        self._eti = (i + 1) % 2
                        tb_ = self.tmpb[i]
                        self.cp("act", tb_[:, :], ps[:, :], [("ps", psi)], [("tmpb", i)])
                        row = ((w % 2) * 4 + oc) * P
                        t0 = tok0 + tb * 512
                        self.dma("sp", dstT[row:row + P, t0:t0 + 512], tb_[:, :], reads=[("tmpb", i)],
                                 writes=[(nm, row // P, t0 // 512)])
                    self.lin_fm(self.hT, "hT", Wg, 0, KC, w * 512, 512, evacT)
                for w in range(10):
                    if w < 2:
                        dst, nm, c0, cc = S["gk"], "gk", 1024 + w * 512, w * 512
                    elif w < 6:
                        dst, nm, c0, cc = S["gv"], "gv", 2048 + (w - 2) * 512, (w - 2) * 512
                    else:
                        dst, nm, c0, cc = S["gr"], "gr", 4096 + (w - 6) * 512, (w - 6) * 512

                    def evacM(ps, psi, t, w=w, dst=dst, nm=nm, cc=cc):
                        tt = blk * 8 + t
                        i = self._eti
                        self._eti = (i + 1) % 2
                        tb_ = self.tmpb[i]
                        if w >= 6:
                            self.act(tb_[:, :], ps[:, :], AF.Silu, [("ps", psi)], [("tmpb", i)])
                        else:
                            self.cp("act", tb_[:, :], ps[:, :], [("ps", psi)], [("tmpb", i)])
                        self.dma("sp", dst[tt * P:(tt + 1) * P, cc:cc + 512], tb_[:, :], reads=[("tmpb", i)],
                                 writes=[(nm, tt, cc // 512)])
                    self.lin_tm(self.hT, "hT", Wg, 0, KC, c0, 512, evacM)
                for dr in range(2):
                    wi = self.load_w(I["gla_wa1"][dr], 0, KC, 0, 16)
                    wt = self.WT[wi]
                    uT = self.XN[dr]
                    for tb in range(2):
                        psi = self.nps()
                        ps = self.PS[psi]
                        for kc in range(KC):
                            self.mm(ps[0:16, :], wt[:, kc, 0:16], self.hT[:, kc, tb * 512:(tb + 1) * 512], kc == 0, kc == KC - 1,
                                    [("wt", wi, 0)] + [("hT", kc, t) for t in range(tb * 4, tb * 4 + 4)],
                                    [("ps", psi)] if kc == 0 else [])
                        self._mark_ps(psi)
                        self.cp("act", uT[0:16, tb * 512:(tb + 1) * 512], ps[0:16, :], [("ps", psi)], [("xn", dr)])
                    for cb in range(2):
                        self.dma("sp", self.tmpf[0][0:16, :], I["gla_wa2"][dr, :, cb * 512:(cb + 1) * 512], writes=[("tmpf", 0)])
                        self.dma("sp", self.tmpf[1][0:1, :], I["gla_ba"][dr, :, cb * 512:(cb + 1) * 512], writes=[("tmpf", 1)])
                        self.cp("dve", self.w2b[0:16, :], self.tmpf[0][0:16, :], [("tmpf", 0)], ["w2b"])
                        self.cp("dve", self.bab[0:1, :], self.tmpf[1][0:1, :], [("tmpf", 1)], ["bab"])
                        for t in range(8):
                            tt = blk * 8 + t
                            psi = self.nps()
                            ps = self.PS[psi]
                            self.mm(ps[:, :], uT[0:16, t * P:(t + 1) * P], self.w2b[0:16, :], True, False,
                                    [("xn", dr), "w2b"], [("ps", psi)])
                            self.mm(ps[:, :], self.ones[0:1, :], self.bab[0:1, :], False, True, ["ones", "bab"], [])
                            self._mark_ps(psi)
                            b = t % 2
                            kf = self.kvf[b]
                            self.act(kf[:, :], ps[:, :], AF.Exp, [("ps", psi)], [("kvf", b)], scale=-1.0)
                            self.ts("dve", kf[:, :], kf[:, :], 1.0, None, ALU.add, None, [("kvf", b)], [("kvf", b)])
                            self.act(kf[:, :], kf[:, :], AF.Ln, [("kvf", b)], [("kvf", b)])
                            self.ts("pool", kf[:, :], kf[:, :], 1.0 / 16.0, None, ALU.mult, None, [("kvf", b)], [("kvf", b)])
                            self.dma("sp", S["gsc"][dr, tt * P:(tt + 1) * P, cb * 512:(cb + 1) * 512], kf[:, :],
                                     reads=[("kvf", b)], writes=[("gsc", dr, tt, cb)])
            if _blk():
                return
        if self.stop == "p1":
            return
        ab = self.aT[:, :, :].rearrange("p a b -> p (a b)")
        off = [0]

        def ab16(n, rows=P):
            o = off[0]
            off[0] += n
            return ab[0:rows, o:o + n]

        def af32(n, rows=P):
            o = off[0]
            off[0] += 2 * n
            return ab[:, o:o + 2 * n].bitcast(F32)[0:rows, :]

        Sf = af32(1024).rearrange("p (c n) -> p c n", n=512)
        Sb = ab16(1024).rearrange("p (c n) -> p c n", n=512)
        gch = [af32(256, 64) for _ in range(2)]
        vch = [ab16(512, 64) for _ in range(2)]
        kch = [ab16(256, 64) for _ in range(2)]
        E1 = af32(128).rearrange("p (c n) -> p c n", n=64)
        E2 = af32(128).rearrange("p (c n) -> p c n", n=64)
        Es = af32(256, 64)
        qt = ab16(128).rearrange("p (c n) -> p c n", n=64)
        kt_ = ab16(128).rearrange("p (c n) -> p c n", n=64)
        kd = ab16(256, 64)
        AT = ab16(64, 64)
        ofc = af32(512, 64)
        rch = ab16(512, 64)
        on_ = af32(512, 64)
        obf = ab16(512, 64)
        Minc = [af32(64, 64) for _ in range(2)]
        Msuf = [af32(64, 64) for _ in range(2)]
        tri = [ab16(64, 64) for _ in range(2)]
        gn_bc = af32(512, 64)
        tri32 = af32(64, 64)
        assert off[0] <= 16384
        allkeys = [("aT", ch, t) for ch in range(KC) for t in range(8)] + ["knT", "kr_sb"]
        for d_ in range(2):
            sfx = "f" if d_ == 0 else "b"
            self.dma("sp", Minc[d_][:, :], I["c_g_incl_" + sfx][0:64, 0:64], writes=[("Minc", d_)] + (allkeys if d_ == 0 else []))
            self.dma("sp", Msuf[d_][:, :], I["c_g_suf_" + sfx][0:64, 0:64], writes=[("Msuf", d_)])
            self.dma("sp", tri32[:, :], I["c_g_tri_" + sfx][:, :], writes=["tri32"])
            self.cp("dve", tri[d_][:, :], tri32[:, :], ["tri32"], [("tri", d_)])
        self.dma("sp", self.grow[0:1, :], I["gla_norm"][:, :], writes=["grow"])
        psj = self.nps()
        self.mm(self.PS[psj][0:64, :], self.sel[0:1, 0, 0:64], self.grow[0:1, :], True, True, ["grow", "sel"], [("ps", psj)])
        self.cp("act", gn_bc[:, :], self.PS[psj][0:64, :], [("ps", psj)], ["gn_bc"])
        hfl = self.hT[:, :, :].rearrange("p a b -> p (a b)")
        groups = [(s_ * 256, 256, s_) for s_ in range(4)] + [(TP, TS, None)]
        for (g0, L, pseq) in groups:
            nch = L // 64
            for h in range(4):
                qT = hfl[:, 0:2 * L].rearrange("p (c n) -> p c n", n=L)
                kT = hfl[:, 8192:8192 + 2 * L].rearrange("p (c n) -> p c n", n=L)
                for c in range(2):
                    for q4 in range(max(1, L // 1024)):
                        w_ = min(L, 1024)
                        self.dma("sp", qT[:, c, q4 * w_:(q4 + 1) * w_], S["gqT"][(h * 2 + c) * P:(h * 2 + c + 1) * P, g0 + q4 * w_:g0 + (q4 + 1) * w_],
                                 reads=[("gqT", h * 2 + c, x) for x in range(T // 512)],
                                 writes=["gq_sb"] + [("hT", ch, t) for ch in range(KC) for t in range(8)] + [("hflat", 0), ("hflat", 1)])
                        self.dma("sp", kT[:, c, q4 * w_:(q4 + 1) * w_], S["gkT"][(h * 2 + c) * P:(h * 2 + c + 1) * P, g0 + q4 * w_:g0 + (q4 + 1) * w_],
                                 reads=[("gkT", h * 2 + c, x) for x in range(T // 512)], writes=["gk_sb"])
                for d_ in range(2):
                    if pseq is None:
                        src = I["gsf"] if d_ == 0 else I["gsb"]
                        for c in range(2):
                            self.dma("sp", Sf[:, c, :], src[h, c * P:(c + 1) * P, :], writes=["Sf"])
                    else:
                        pg.op("pool", lambda e: e.memset(Sf[:, :, :], 0.0), [], ["Sf"])
                    self.cp("act", Sb[:, :, :], Sf[:, :, :], ["Sf"], ["Sb"])
                    order = range(nch) if d_ == 0 else range(nch - 1, -1, -1)
                    for ci in order:
                        tk = g0 + ci * 64
                        tile_, half = tk // P, (tk % P) // 64
                        lc = ci * 64
                        bq = ci % 2
                        g_, v_, k_ = gch[bq], vch[bq], kch[bq]
                        self.dma("sp", g_[:, :], S["gsc"][d_, tk:tk + 64, h * 256:(h + 1) * 256],
                                 reads=[("gsc", d_, tile_, (h * 256) // 512)], writes=[("gch", bq)])
                        self.dma("sp", v_[:, :], S["gv"][tk:tk + 64, h * 512:(h + 1) * 512], reads=[("gv", tile_, h)], writes=[("vch", bq)])
                        self.dma("sp", k_[:, :], S["gk"][tk:tk + 64, h * 256:(h + 1) * 256], reads=[("gk", tile_, (h * 256) // 512)],
                                 writes=[("kch", bq)])
                        for c in range(2):
                            psi = self.nps()
                            ps = self.PS[psi]
                            self.mm(ps[:, 0:64], g_[:, c * P:(c + 1) * P], Minc[d_][:, :], True, True, [("gch", bq), ("Minc", d_)], [("ps", psi)])
                            self.act(E1[:, c, :], ps[:, 0:64], AF.Exp, [("ps", psi)], [("E1", c)], scale=-1.0)
                            self.act(E2[:, c, :], ps[:, 0:64], AF.Exp, [("ps", psi)], [("E2", c)])
                            self.stt("dve", qt[:, c, :], qT[:, c, lc:lc + 64], 1.0 / 16.0, E1[:, c, :], ALU.mult, ALU.mult,
                                     ["gq_sb", ("E1", c)], [("qt", c)])
                            self.tt("pool", kt_[:, c, :], kT[:, c, lc:lc + 64], E2[:, c, :], ALU.mult, ["gk_sb", ("E2", c)], [("kt", c)])
                        psi = self.nps()
                        ps = self.PS[psi]
                        self.mm(ps[0:64, 0:256], Msuf[d_][:, :], g_[:, :], True, True, [("gch", bq), ("Msuf", d_)], [("ps", psi)])
                        self.act(Es[:, :], ps[0:64, 0:256], AF.Exp, [("ps", psi)], ["Es"], scale=-1.0)
                        self.tt("dve", kd[:, :], k_[:, :], Es[:, :], ALU.mult, [("kch", bq), "Es"], ["kd"])
                        psi = self.nps()
                        ps = self.PS[psi]
                        for c in range(2):
                            self.mm(ps[0:64, 0:64], kt_[:, c, :], qt[:, c, :], c == 0, c == 1, [("kt", c), ("qt", c)],
                                    [("ps", psi)] if c == 0 else [])
                        self._mark_ps(psi)
                        self.tt("dve", AT[:, :], ps[0:64, 0:64], tri[d_][:, :], ALU.mult, [("ps", psi), ("tri", d_)], ["AT"])
                        pso = self.nps()
                        po = self.PS[pso]
                        self.mm(po[0:64, :], AT[:, :], v_[:, :], True, False, ["AT", ("vch", bq)], [("ps", pso)])
                        for c in range(2):
                            self.mm(po[0:64, :], qt[:, c, :], Sb[:, c, :], False, c == 1, [("qt", c), "Sb"], [])
                        self._mark_ps(pso)
                        if d_ == 0:
                            self.cp("act", ofc[:, :], po[0:64, :], [("ps", pso)], ["ofc"])
                            self.dma("sp", S["gof"][tk:tk + 64, h * 512:(h + 1) * 512], ofc[:, :], reads=["ofc"],
                                     writes=[("gof", tk // 64, h)])
                        else:
                            self.dma("sp", ofc[:, :], S["gof"][tk:tk + 64, h * 512:(h + 1) * 512], reads=[("gof", tk // 64, h)], writes=["ofc"])
                            self.dma("sp", rch[:, :], S["gr"][tk:tk + 64, h * 512:(h + 1) * 512], reads=[("gr", tile_, h)], writes=["rch"])
                            self.tt("dve", ofc[:, :], ofc[:, :], po[0:64, :], ALU.add, ["ofc", ("ps", pso)], ["ofc"])
                            self.act(on_[:, :], ofc[:, :], AF.Square, ["ofc"], ["on", ("stat", 0)], accum_out=self.stat[0:64, 0:1])
                            self.rstd_from_ss(0, 512)
                            self.stt("dve", on_[:, :], ofc[:, :], self.stat[0:64, 4:5], gn_bc[:, :], ALU.mult, ALU.mult,
                                     ["ofc", ("stat3", 0), "gn_bc"], ["on"])
                            self.tt("pool", obf[:, :], on_[:, :], rch[:, :], ALU.mult, ["on", "rch"], ["obf"])
                            pi = self._pti
                            self._pti = (pi + 1) % 2
                            pt = self.PT[pi]
                            for c4 in range(4):
                                self.tr(pt[:, c4 * 64:(c4 + 1) * 64], obf[:, c4 * P:(c4 + 1) * P], ["obf"], [("pt", pi)])
                            i = self._eti
                            self._eti = (i + 1) % 2
                            tb_ = self.tmpb[i]
                            self.cp("act", tb_[:, 0:256], pt[:, 0:256], [("pt", pi)], [("tmpb", i)])
                            dst = S["attT"][h * 512:(h + 1) * 512, tk:tk + 64].rearrange("(c d) i -> d c i", d=P)
                            self.dma("sp", dst, tb_[:, 0:256].rearrange("d (c i) -> d c i", i=64), reads=[("tmpb", i)],
                                     writes=[("attT", h * 4 + c4, tk // 1024) for c4 in range(4)])
                        last = 63 if d_ == 0 else 0
                        for c in range(2):
                            psi = self.nps()
                            ps = self.PS[psi]
                            self.mm(ps[:, :], kd[:, c * P:(c + 1) * P], v_[:, :], True, True, ["kd", ("vch", bq)], [("ps", psi)])
                            self.stt("dve", Sf[:, c, :], Sf[:, c, :], E1[:, c, last:last + 1], ps[:, :], ALU.mult, ALU.add,
                                     ["Sf", ("E1", c), ("ps", psi)], ["Sf"])
                        self.cp("act", Sb[:, :, :], Sf[:, :, :], ["Sf"], ["Sb"])
                    if pseq is not None:
                        dsto = O["ngf"] if d_ == 0 else O["ngb"]
                        for c in range(2):
                            self.dma("sp", dsto[pseq, h, c * P:(c + 1) * P, :], Sf[:, c, :], reads=["Sf"], writes=[("ngo", d_, pseq, h, c)])
        if self.stop == "att":
            return
        for blk in range(NBLK):
            self.out_proj_and_ffn(li, I["gla_wo"], blk, 0 if blk == 0 else 1)

    def epilogue(self):
        pg, I, O, S = self.pg, self.I, self.O, self.S
        if self.stop is not None:
            for ch in range(KC):
                self.dma("sp", O["dbg"][ch * P:(ch + 1) * P, :], S["attT"][ch * P:(ch + 1) * P, :],
                         reads=[("attT", ch, b) for b in range(NBLK)], writes=[("dbg", ch)])
        if self.final_norm:
            for q4 in range(4):
                self.dma("sp", self.grow[0:1, :], I["fnorm"][:, q4 * 512:(q4 + 1) * 512], writes=["grow"])
                psj = self.nps()
                pj = self.PS[psj]
                pg.op("pe", lambda e, pj=pj: e.matmul(pj[:, :], lhsT=self.sel[0:1, 0, :], rhs=self.grow[0:1, :],
                                                      start=True, stop=True), ["grow", "sel"], [("ps", psj)])
                pg.op("act", lambda e, pj=pj, q4=q4: e.copy(out=self.gbc[:, q4 * 512:(q4 + 1) * 512], in_=pj[:, :]),
                      [("ps", psj)], ["gbc"])
        for tt in range(T // P):
            b = tt % 2
            xt = self.XT[b]
            self.dma("sp", xt[:, :], S["xres"][tt * P:(tt + 1) * P, :], reads=[("xres", tt)], writes=[("xt", b)])
            if self.final_norm:
                xn = self.XN[b]
                pg.op("act", lambda e, xt=xt, xn=xn, b=b: e.activation(out=xn[:, :], in_=xt[:, :], func=AF.Square,
                                                                        accum_out=self.stat[:, b:b + 1]),
                      [("xt", b)], [("xn", b), ("stat", b)])
                pg.op("dve", lambda e, b=b: e.tensor_scalar(out=self.stat[:, 2 + b:3 + b], in0=self.stat[:, b:b + 1],
                                                            scalar1=1.0 / D, scalar2=EPS, op0=ALU.mult, op1=ALU.add),
                      [("stat", b)], [("stat2", b)])
                pg.op("act", lambda e, b=b: e.sqrt(out=self.stat[:, 6 + b:7 + b], in_=self.stat[:, 2 + b:3 + b]),
                      [("stat2", b)], [("stat2s", b)])
                pg.op("dve", lambda e, b=b: e.reciprocal(out=self.stat[:, 4 + b:5 + b], in_=self.stat[:, 6 + b:7 + b]),
                      [("stat2s", b)], [("stat3", b)])
                pg.op("dve", lambda e, xt=xt, b=b: e.scalar_tensor_tensor(out=xt[:, :], in0=xt[:, :], scalar=self.stat[:, 4 + b:5 + b],
                                                                          in1=self.gbc[:, :], op0=ALU.mult, op1=ALU.mult),
                      [("xt", b), ("stat3", b), "gbc"], [("xt", b)])
            if tt < 8:
                dst = O["yp"][tt * P:(tt + 1) * P, :]
            else:
                dst = O["ys"][(tt - 8) * P:(tt - 7) * P, :]
            self.dma("sp", dst, xt[:, :], reads=[("xt", b)], writes=[("yout", tt)])


def _core_inputs(inp, core, consts):
    f = lambda a: np.ascontiguousarray(np.asarray(a, dtype=np.float32))
    b = core // 4
    m = {}
    m["xp"] = f(inp["x_prompt"][core * 4:(core + 1) * 4].reshape(TP, D))
    m["xs"] = f(inp["x_sample"][b])
    cv = np.stack([np.asarray(inp["c_ctx"]), np.asarray(inp["c"][b])], axis=0)
    m["cvT"] = f(cv.reshape(2, KC, P).transpose(2, 1, 0))
    m["cwk"] = f(np.asarray(inp["cache_win_k"][b]).reshape(2, PAST, 512))
    m["cwv"] = f(np.asarray(inp["cache_win_v"][b]).reshape(2, PAST, 512))
    m["cmc"] = f(inp["cache_mla_ckv"][b, 0])
    m["cmr"] = f(inp["cache_mla_krope"][b, 0])
    m["gsf"] = f(inp["state_gla_fwd"][b, 0])
    m["gsb"] = f(inp["state_gla_bwd"][b, 0])
    return m


def _shared_inputs(inp, consts, layers=(0, 1, 2, 3)):
    f = lambda a: np.ascontiguousarray(np.asarray(a, dtype=np.float32))
    z = np.zeros((1, 1), np.float32)
    need = lambda k: any(LAYER_KIND[l] == k for l in layers)
    needj = lambda j: any(LAYER_KIND[l] == 0 and LAYER_J[l] == j for l in layers)
    m = {}
    for i in range(4):
        m["ada_w%d" % i] = f(inp["ada_w"][i]) if i in layers else z
        m["ffn_w1_%d" % i] = f(inp["ffn_w1"][i]) if i in layers else z
        m["ffn_w2_%d" % i] = f(inp["ffn_w2"][i]) if i in layers else z
    for j in range(2):
        m["win_wqkv%d" % j] = f(inp["win_wqkv"][j]) if needj(j) else z
        m["win_wo%d" % j] = f(inp["win_wo"][j]) if needj(j) else z
    m["ada_bT"] = f(np.asarray(inp["ada_b"]).reshape(4, 96, P).transpose(0, 2, 1))
    m["ngT"] = f(np.asarray(inp["norm_g"]).reshape(4, 2, KC, P).transpose(0, 1, 3, 2))
    m["win_sink"] = f(np.asarray(inp["win_sink"]).reshape(2, 1, 16))
    m["mla_wdown"] = f(inp["mla_wdown"][0]) if need(1) else z
    m["mla_qnT"] = f(np.asarray(inp["mla_q_norm"][0]).reshape(4, P).T)
    m["mla_wuq"] = f(inp["mla_wuq"][0]) if need(1) else z
    m["mla_kvn"] = f(np.asarray(inp["mla_kv_norm"][0]).reshape(1, 256))
    m["mla_wukv"] = f(inp["mla_wukv"][0]) if need(1) else z
    m["mla_wo"] = f(inp["mla_wo"][0]) if need(1) else z
    m["gla_win"] = f(inp["gla_win"][0]) if need(2) else z
    m["gla_wa1"] = f(inp["gla_wa1"][0])
    m["gla_wa2"] = f(inp["gla_wa2"][0])
    m["gla_ba"] = f(np.asarray(inp["gla_ba"][0]).reshape(2, 1, 1024))
    m["gla_norm"] = f(np.asarray(inp["gla_norm"][0]).reshape(1, 512))
    m["gla_wo"] = f(inp["gla_wo"][0]) if need(2) else z
    m["fnorm"] = f(np.asarray(inp["final_norm"]).reshape(1, D))
    for k, v in consts.items():
        m["c_" + k] = f(v)
    return m


def kernel(**inputs):
    consts = _consts()
    shared = _shared_inputs(inputs, consts)
    nc = Builder([0, 1, 2, 3], True).build()
    in_maps = []
    for core in range(8):
        m = dict(shared)
        m.update(_core_inputs(inputs, core, consts))
        in_maps.append(m)
    res = run_bass_kernel_spmd(nc, in_maps, core_ids=list(range(8)))
    R = res.results
    yp = np.concatenate([R[c]["yp"].reshape(4, 256, D) for c in range(8)], axis=0)
    ys = np.stack([R[0]["ys"], R[4]["ys"]], axis=0)
    nwk = np.concatenate([R[c]["nwk"].reshape(2, 4, 256, 4, 128).transpose(1, 0, 2, 3, 4) for c in range(8)], axis=0)
    nwv = np.concatenate([R[c]["nwv"].reshape(2, 4, 256, 4, 128).transpose(1, 0, 2, 3, 4) for c in range(8)], axis=0)
    nmc = np.concatenate([R[c]["nmc"].reshape(4, 1, 256, 256) for c in range(8)], axis=0)
    nmr = np.concatenate([R[c]["nmr"].reshape(4, 1, 256, 64) for c in range(8)], axis=0)
    ngf = np.concatenate([R[c]["ngf"].reshape(4, 1, 4, 256, 512) for c in range(8)], axis=0)
    ngb = np.concatenate([R[c]["ngb"].reshape(4, 1, 4, 256, 512) for c in range(8)], axis=0)
    return tuple(np.ascontiguousarray(a, dtype=np.float32) for a in (yp, ys, nwk, nwv, nmc, nmr, ngf, ngb))
```
